# Optimizing a Trainium2 kernel written in Bass

```python
import math
import jax, jax.numpy as jnp
from jax import lax
import numpy as np

D_MODEL = 1024
BATCH = 8
SEQ = 2048
DEPTH = 4

N_MIXERS = 4
HEAD_DIM = 64
ROPE_THETA = 500000.0
ROT_DIM = HEAD_DIM // 4
NEG_INF = -1e30
RMS_EPS = 1e-6
Q_BLOCK = 128

DIL_PAIRS = ((128, 1), (512, 4), (2048, 16))
A_HEADS_PER_GROUP = 5
A_HEADS = A_HEADS_PER_GROUP * len(DIL_PAIRS)
BAND_BLOCK = 64

B_HEADS = 16
B_Q_RANK = 256
B_KV_RANK = 128
B_NOPE = 64
B_ROPE = 32
B_V = 64

C_HEADS = 16
GRID_W = 64
NA_ROWS = 8
NA_COLS = 16

D_HEADS = 8
D_HEAD = 64

MLP_HIDDEN = 4 * D_MODEL
PLE_DIM = 256

kernel_name = "hybrid_interleaved_bidir_encoder"


def rmsnorm(x, g):
    xf = x.astype(jnp.float32)
    y = xf * lax.rsqrt(jnp.mean(xf * xf, axis=-1, keepdims=True) + RMS_EPS)
    return (y * g.astype(jnp.float32)).astype(x.dtype)


def rope_tables(seq_len, rot_dim):
    inv = ROPE_THETA ** (-jnp.arange(0, rot_dim, 2, dtype=jnp.float32) / rot_dim)
    ang = jnp.arange(seq_len, dtype=jnp.float32)[:, None] * inv[None, :]
    return jnp.cos(ang), jnp.sin(ang)


def apply_rope(x, cos, sin):
    r = cos.shape[-1]
    c = cos.astype(x.dtype)
    s = sin.astype(x.dtype)
    x1, x2, rest = x[..., :r], x[..., r:2 * r], x[..., 2 * r:]
    return jnp.concatenate([x1 * c - x2 * s, x2 * c + x1 * s, rest], axis=-1)


def banded_attention(q, k, v, half):
    Z, L, dh = q.shape
    qb = math.gcd(L, BAND_BLOCK)
    nb = L // qb
    kw = qb + 2 * half
    kp = jnp.pad(k, ((0, 0), (half, half), (0, 0)))
    vp = jnp.pad(v, ((0, 0), (half, half), (0, 0)))
    idx = np.arange(nb)[:, None] * qb + np.arange(kw)[None, :]
    kb = kp[:, idx]
    vb = vp[:, idx]
    rel = np.arange(kw)[None, :] - half - np.arange(qb)[:, None]
    kpos = idx - half
    valid = (np.abs(rel) <= half)[None] & ((kpos >= 0) & (kpos < L))[:, None, :]
    s = jnp.einsum('znqd,znkd->znqk', q.reshape(Z, nb, qb, dh), kb).astype(jnp.float32) * (dh ** -0.5)
    s = jnp.where(valid, s, NEG_INF)
    m = jnp.max(s, axis=-1, keepdims=True)
    e = jnp.exp(s - m)
    l = jnp.sum(e, axis=-1, keepdims=True)
    o = jnp.einsum('znqk,znkd->znqd', (e / l).astype(v.dtype), vb)
    lse = (m + jnp.log(l))[..., 0]
    return o.reshape(Z, L, dh), lse.reshape(Z, L)


def dense_block_attention(q, k, v, scale):
    B, H, S, dq = q.shape
    nb = S // Q_BLOCK
    qb = q.reshape(B, H, nb, Q_BLOCK, dq).transpose(2, 0, 1, 3, 4)

    def one(qi):
        s = jnp.einsum('bhqd,bhkd->bhqk', qi, k).astype(jnp.float32) * scale
        pr = jax.nn.softmax(s, axis=-1)
        return jnp.einsum('bhqk,bhkd->bhqd', pr.astype(v.dtype), v)

    o = lax.map(one, qb)
    return o.transpose(1, 2, 0, 3, 4).reshape(B, H, S, v.shape[-1])


def diff_block_attention(q1, q2, k1, k2, v, lam, scale):
    B, H, S, d = q1.shape
    nb = S // Q_BLOCK
    blk = lambda t: t.reshape(B, H, nb, Q_BLOCK, d).transpose(2, 0, 1, 3, 4)

    def one(qs):
        a, b = qs
        p1 = jax.nn.softmax(jnp.einsum('bhqd,bhkd->bhqk', a, k1).astype(jnp.float32) * scale, axis=-1)
        p2 = jax.nn.softmax(jnp.einsum('bhqd,bhkd->bhqk', b, k2).astype(jnp.float32) * scale, axis=-1)
        return jnp.einsum('bhqk,bhkd->bhqd', (p1 - lam * p2).astype(v.dtype), v)

    o = lax.map(one, (blk(q1), blk(q2)))
    return o.transpose(1, 2, 0, 3, 4).reshape(B, H, S, v.shape[-1])


def neighborhood_attention(q, k, v, rpb):
    B, H, S, dh = q.shape
    rows = S // GRID_W
    kh = min(NA_ROWS, rows)
    cb = NA_COLS
    kcw = 2 * NA_COLS
    ncb = GRID_W // cb
    starts = np.clip(np.arange(ncb) * cb - NA_COLS // 2, 0, GRID_W - kcw)
    key_cols = starts[:, None] + np.arange(kcw)[None, :]
    q_cols = np.arange(GRID_W).reshape(ncb, cb)
    win0 = np.clip(q_cols - NA_COLS // 2, 0, GRID_W - NA_COLS)[..., None]
    kc = key_cols[:, None, :]
    col_mask = (kc >= win0) & (kc < win0 + NA_COLS)
    col_off = np.clip(kc - q_cols[..., None] + NA_COLS - 1, 0, 2 * NA_COLS - 2)
    rpb_cols = rpb[:, :, col_off]
    grid = lambda t: t.reshape(B, H, rows, GRID_W, dh).transpose(2, 0, 1, 3, 4)
    qg, kg, vg = grid(q), grid(k), grid(v)
    scale = dh ** -0.5

    def one_row(r):
        rs = jnp.clip(r - kh // 2, 0, rows - kh)
        q_r = lax.dynamic_index_in_dim(qg, r, 0, keepdims=False).reshape(B, H, ncb, cb, dh)
        k_r = lax.dynamic_slice_in_dim(kg, rs, kh, 0)[:, :, :, key_cols]
        v_r = lax.dynamic_slice_in_dim(vg, rs, kh, 0)[:, :, :, key_cols]
        s = jnp.einsum('bhnqd,rbhnkd->bhnqrk', q_r, k_r).astype(jnp.float32) * scale
        row_off = rs + jnp.arange(kh) - r + NA_ROWS - 1
        bias = rpb_cols[:, row_off].astype(jnp.float32).transpose(0, 2, 3, 1, 4)
        s = jnp.where(col_mask[:, :, None, :], s + bias[None], NEG_INF)
        pr = jax.nn.softmax(s.reshape(B, H, ncb, cb, kh * kcw), axis=-1).reshape(s.shape)
        o = jnp.einsum('bhnqrk,rbhnkd->bhnqd', pr.astype(v.dtype), v_r)
        return o.reshape(B, H, GRID_W, dh)

    o = lax.map(one_row, jnp.arange(rows))
    return o.transpose(1, 2, 0, 3, 4).reshape(B, H, S, dh)


def _by_stride(t, dil):
    B, G, S, dh = t.shape
    return t.reshape(B, G, S // dil, dil, dh).transpose(0, 1, 3, 2, 4).reshape(B * G * dil, S // dil, dh)


def dilated_window_mixer(h, w_qkv, w_o, cos, sin):
    B, S, _ = h.shape
    G = A_HEADS_PER_GROUP
    qkv = (h @ w_qkv).reshape(B, S, 3, A_HEADS, HEAD_DIM).transpose(2, 0, 3, 1, 4)
    q = apply_rope(qkv[0], cos, sin)
    k = apply_rope(qkv[1], cos, sin)
    v = qkv[2]
    outs, lses = [], []
    for g, (window, dil) in enumerate(DIL_PAIRS):
        sl = slice(g * G, (g + 1) * G)
        half = window // (2 * dil)
        o, lse = banded_attention(_by_stride(q[:, sl], dil), _by_stride(k[:, sl], dil),
                                  _by_stride(v[:, sl], dil), half)
        outs.append(o.reshape(B, G, dil, S // dil, HEAD_DIM).transpose(0, 1, 3, 2, 4).reshape(B, G, S, HEAD_DIM))
        lses.append(lse.reshape(B, G, dil, S // dil).transpose(0, 1, 3, 2).reshape(B, G, S))
    alpha = jax.nn.softmax(jnp.stack(lses, axis=0), axis=0)
    o = jnp.concatenate([outs[g] * alpha[g][..., None].astype(h.dtype) for g in range(len(DIL_PAIRS))], axis=1)
    return o.transpose(0, 2, 1, 3).reshape(B, S, A_HEADS * HEAD_DIM) @ w_o


def latent_attention_mixer(h, w_in, q_norm, w_uq, kv_norm, w_ukv, w_o, cos, sin):
    B, S, _ = h.shape
    z = h @ w_in
    c_q = z[..., :B_Q_RANK]
    c_kv = z[..., B_Q_RANK:B_Q_RANK + B_KV_RANK]
    k_rope = z[..., B_Q_RANK + B_KV_RANK:]
    q = (rmsnorm(c_q, q_norm) @ w_uq).reshape(B, S, B_HEADS, B_NOPE + B_ROPE).transpose(0, 2, 1, 3)
    kv = (rmsnorm(c_kv, kv_norm) @ w_ukv).reshape(B, S, B_HEADS, B_NOPE + B_V).transpose(0, 2, 1, 3)
    q = jnp.concatenate([q[..., :B_NOPE], apply_rope(q[..., B_NOPE:], cos, sin)], axis=-1)
    k_rope = jnp.broadcast_to(apply_rope(k_rope, cos, sin)[:, None], (B, B_HEADS, S, B_ROPE))
    k = jnp.concatenate([kv[..., :B_NOPE], k_rope], axis=-1)
    v = kv[..., B_NOPE:]
    o = dense_block_attention(q, k, v, (B_NOPE + B_ROPE) ** -0.5)
    return o.transpose(0, 2, 1, 3).reshape(B, S, B_HEADS * B_V) @ w_o


def neighborhood_mixer(h, w_qkv, rpb, w_o):
    B, S, _ = h.shape
    qkv = (h @ w_qkv).reshape(B, S, 3, C_HEADS, HEAD_DIM).transpose(2, 0, 3, 1, 4)
    o = neighborhood_attention(qkv[0], qkv[1], qkv[2], rpb)
    return o.transpose(0, 2, 1, 3).reshape(B, S, C_HEADS * HEAD_DIM) @ w_o


def differential_mixer(h, w_qkv, lq1, lk1, lq2, lk2, subln, w_o, cos, sin, lambda_init):
    B, S, _ = h.shape
    q, k, v = jnp.split(h @ w_qkv, 3, axis=-1)
    q = apply_rope(q.reshape(B, S, 2 * D_HEADS, D_HEAD).transpose(0, 2, 1, 3), cos, sin).reshape(B, D_HEADS, 2, S, D_HEAD)
    k = apply_rope(k.reshape(B, S, 2 * D_HEADS, D_HEAD).transpose(0, 2, 1, 3), cos, sin).reshape(B, D_HEADS, 2, S, D_HEAD)
    v = v.reshape(B, S, D_HEADS, 2 * D_HEAD).transpose(0, 2, 1, 3)
    f32 = jnp.float32
    lam = (jnp.exp(jnp.sum(lq1.astype(f32) * lk1.astype(f32)))
           - jnp.exp(jnp.sum(lq2.astype(f32) * lk2.astype(f32))) + lambda_init)
    o = diff_block_attention(q[:, :, 0], q[:, :, 1], k[:, :, 0], k[:, :, 1], v, lam, D_HEAD ** -0.5)
    o = rmsnorm(o, subln) * (1.0 - lambda_init)
    return o.transpose(0, 2, 1, 3).reshape(B, S, 2 * D_HEADS * D_HEAD) @ w_o


def sq_relu_mlp(h, w_up, w_down):
    return jnp.square(jax.nn.relu(h @ w_up)) @ w_down


def setup_inputs(seed: int = 0) -> dict:
    key = jax.random.key(seed)
    keys = iter(jax.random.split(key, 40))

    def nrm(shape, scale):
        return jax.random.normal(next(keys), shape, jnp.float32) * scale

    def gain(shape):
        return 1.0 + nrm(shape, 0.02)

    nA, nB, nC, nD = (len(range(m, DEPTH, N_MIXERS)) for m in range(N_MIXERS))
    D = D_MODEL
    return {
        "x": nrm((BATCH, SEQ, D), 1.0),
        "p": nrm((DEPTH, BATCH, SEQ, PLE_DIM), 1.0),
        "a_norm": gain((nA, D)),
        "a_w_qkv": nrm((nA, D, 3 * A_HEADS * HEAD_DIM), D ** -0.5),
        "a_w_o": nrm((nA, A_HEADS * HEAD_DIM, D), (A_HEADS * HEAD_DIM) ** -0.5),
        "b_norm": gain((nB, D)),
        "b_w_in": nrm((nB, D, B_Q_RANK + B_KV_RANK + B_ROPE), D ** -0.5),
        "b_q_norm": gain((nB, B_Q_RANK)),
        "b_w_uq": nrm((nB, B_Q_RANK, B_HEADS * (B_NOPE + B_ROPE)), B_Q_RANK ** -0.5),
        "b_kv_norm": gain((nB, B_KV_RANK)),
        "b_w_ukv": nrm((nB, B_KV_RANK, B_HEADS * (B_NOPE + B_V)), B_KV_RANK ** -0.5),
        "b_w_o": nrm((nB, B_HEADS * B_V, D), (B_HEADS * B_V) ** -0.5),
        "c_norm": gain((nC, D)),
        "c_w_qkv": nrm((nC, D, 3 * C_HEADS * HEAD_DIM), D ** -0.5),
        "c_rpb": nrm((nC, C_HEADS, 2 * NA_ROWS - 1, 2 * NA_COLS - 1), 0.02),
        "c_w_o": nrm((nC, C_HEADS * HEAD_DIM, D), (C_HEADS * HEAD_DIM) ** -0.5),
        "d_norm": gain((nD, D)),
        "d_w_qkv": nrm((nD, D, 3 * 2 * D_HEADS * D_HEAD), D ** -0.5),
        "d_lambda_q1": nrm((nD, D_HEAD), 0.1),
        "d_lambda_k1": nrm((nD, D_HEAD), 0.1),
        "d_lambda_q2": nrm((nD, D_HEAD), 0.1),
        "d_lambda_k2": nrm((nD, D_HEAD), 0.1),
        "d_subln": gain((nD, 2 * D_HEAD)),
        "d_w_o": nrm((nD, 2 * D_HEADS * D_HEAD, D), (2 * D_HEADS * D_HEAD) ** -0.5),
        "mlp_norm": gain((DEPTH, D)),
        "w_up": nrm((DEPTH, D, MLP_HIDDEN), D ** -0.5),
        "w_down": nrm((DEPTH, MLP_HIDDEN, D), MLP_HIDDEN ** -0.5),
        "ple_norm": gain((DEPTH, D)),
        "w_ple_gate": nrm((DEPTH, D, D), D ** -0.5),
        "w_ple_proj": nrm((DEPTH, PLE_DIM, D), PLE_DIM ** -0.5),
        "final_norm": gain((D,)),
    }


def reference(x, p, a_norm, a_w_qkv, a_w_o,
              b_norm, b_w_in, b_q_norm, b_w_uq, b_kv_norm, b_w_ukv, b_w_o,
              c_norm, c_w_qkv, c_rpb, c_w_o,
              d_norm, d_w_qkv, d_lambda_q1, d_lambda_k1, d_lambda_q2, d_lambda_k2, d_subln, d_w_o,
              mlp_norm, w_up, w_down, ple_norm, w_ple_gate, w_ple_proj, final_norm):
    S = x.shape[1]
    cos_p, sin_p = rope_tables(S, ROT_DIM)
    cos_l, sin_l = rope_tables(S, B_ROPE)
    for i in range(DEPTH):
        mix, j = i % N_MIXERS, i // N_MIXERS
        if mix == 0:
            x = x + dilated_window_mixer(rmsnorm(x, a_norm[j]), a_w_qkv[j], a_w_o[j], cos_p, sin_p)
        elif mix == 1:
            x = x + latent_attention_mixer(rmsnorm(x, b_norm[j]), b_w_in[j], b_q_norm[j], b_w_uq[j],
                                           b_kv_norm[j], b_w_ukv[j], b_w_o[j], cos_l, sin_l)
        elif mix == 2:
            x = x + neighborhood_mixer(rmsnorm(x, c_norm[j]), c_w_qkv[j], c_rpb[j], c_w_o[j])
        else:
            lambda_init = 0.8 - 0.6 * math.exp(-0.3 * i)
            x = x + differential_mixer(rmsnorm(x, d_norm[j]), d_w_qkv[j], d_lambda_q1[j], d_lambda_k1[j],
                                       d_lambda_q2[j], d_lambda_k2[j], d_subln[j], d_w_o[j],
                                       cos_p, sin_p, lambda_init)
        x = x + sq_relu_mlp(rmsnorm(x, mlp_norm[i]), w_up[i], w_down[i])
        gate = jax.nn.sigmoid(rmsnorm(x, ple_norm[i]) @ w_ple_gate[i])
        x = x + gate * (p[i] @ w_ple_proj[i])
    return rmsnorm(x, final_norm)
```

```python
import math
import numpy as np
import ml_dtypes
import concourse.bass as bass
import concourse.mybir as mybir
from concourse.bass_utils import run_bass_kernel_spmd

F32 = mybir.dt.float32
BF16 = mybir.dt.bfloat16
AF = mybir.ActivationFunctionType
ALU = mybir.AluOpType

S = 2048
D = 1024
NCORES = 8
TC = 512
NTC = S // TC
EPS = 1e-6
NEG = -30000.0


class Sched:
    ENGS = ("pe", "act", "dve", "pool", "sp")

    def __init__(self):
        self.ops = {e: [] for e in self.ENGS}
        self.count = {e: 0 for e in self.ENGS}
        self.known = {e: {} for e in self.ENGS}
        self.lastw = {}
        self.readers = {}
        self.dma_count = {}
        self.phase = "init"

    def _deps(self, eng, reads, writes, is_dma):
        deps = set()
        for k in reads:
            t = self.lastw.get(k)
            if t is not None:
                deps.add(t)
        for k in writes:
            t = self.lastw.get(k)
            if t is not None:
                deps.add(t)
            for r in self.readers.get(k, ()):
                deps.add(r)
        need = {}
        for (sk, val, e) in deps:
            if e == eng and not is_dma and eng == "pe":
                continue
            if self.known[eng].get(sk, 0) >= val:
                continue
            if need.get(sk, 0) < val:
                need[sk] = val
        for sk, val in need.items():
            self.known[eng][sk] = val
        return list(need.items())

    def _commit(self, tok, reads, writes):
        for k in writes:
            self.lastw[k] = tok
            self.readers[k] = []
        for k in reads:
            if k in writes:
                continue
            self.readers.setdefault(k, []).append(tok)

    def op(self, eng, fn, reads=(), writes=(), strict=False):
        waits = self._deps(eng, reads, writes, strict)
        self.count[eng] += 1
        tok = (eng, self.count[eng], eng)
        self._commit(tok, reads, writes)
        self.ops[eng].append((waits, fn, (eng, 1), self.phase))

    def dma(self, eng, sem, fn, reads=(), writes=()):
        waits = self._deps(eng, reads, writes, True)
        self.dma_count[sem] = self.dma_count.get(sem, 0) + 16
        tok = (sem, self.dma_count[sem], None)
        self._commit(tok, reads, writes)
        self.ops[eng].append((waits, fn, (sem, 16), self.phase))
        return tok

    def retag(self, keys, sem):
        tok = (sem, self.dma_count[sem], None)
        for k in keys:
            self.lastw[k] = tok

    def barrier(self):
        cur = dict(self.count)
        for e in self.ENGS:
            waits = []
            for f in self.ENGS:
                if f == e or cur[f] == 0:
                    continue
                if self.known[e].get(f, 0) < cur[f]:
                    self.known[e][f] = cur[f]
                    waits.append((f, cur[f]))
            if waits:
                self.ops[e].append((waits, None, None, self.phase))

    def final_wait(self, eng, sems):
        waits = [(s, self.dma_count[s]) for s in sems]
        self.ops[eng].append((waits, None, None, self.phase))


def keys(name, *idx):
    out = [(name,)]
    for ix in idx:
        if isinstance(ix, int):
            ix = (ix,)
        out = [o + (i,) for o in out for i in ix]
    return out


LAMBDA_INIT = [0.8 - 0.6 * math.exp(-0.3 * i) for i in range(4)]

G_MIX, G_MLP, G_PLE, G_FINAL = 0, 4, 8, 12
NG = 13
NGC = NG * 8 + 8


class Builder:
    def __init__(self, layers, do_final, parts=("mix", "mlp", "ple")):
        self.layers = list(layers)
        self.do_final = do_final
        self.parts = parts
        self.nc = bass.Bass("TRN2", target_bir_lowering=False)
        self.s = Sched()
        self.dram = {}
        self.sb = {}
        self.ps = []
        self.psi = 0
        self.slot_i = 0
        self.fs_i = 0
        self.bank_rot = {}
        self.ring_i = 0
        self.debug = False
        self.scopes = False
        self.dbg_names = []

    def din(self, name, shape, dt=F32):
        t = self.nc.dram_tensor(name, list(shape), dt, kind="ExternalInput").ap()
        self.dram[name] = t
        return t

    def declare_io(self):
        self.din("xT", [D, S])
        self.din("pT", [4, 256, S])
        self.din("gains", [128, NGC])
        self.din("ident", [128, 128])
        self.din("w_up", [4, D, 4 * D])
        self.din("w_down", [4, 4 * D, D])
        self.din("w_ple_gate", [4, D, D])
        self.din("w_ple_proj", [4, 256, D])
        self.din("ropeL", [2, 128, S])
        self.din("ropeP", [2, 128, S])
        if 1 in self.layers and "mix" in self.parts:
            self.din("b_w_in", [D, 416])
            self.din("b_w_kr", [D, 192])
            self.din("b_w_uq", [256, 1536])
            self.din("b_w_uq_p", [256, 1536])
            self.din("b_w_ukv", [128, 2048])
            self.din("b_w_o", [D, D])
        if 3 in self.layers and "mix" in self.parts:
            self.din("d_w_a", [8, D, 384])
            self.din("d_w_b", [8, D, 256])
            self.din("d_w_o", [D, D])
            self.din("d_lamv", [128, 256])
        if 2 in self.layers and "mix" in self.parts:
            self.din("c_w_a", [8, D, 384])
            self.din("c_w_o", [D, D])
            self.din("na_bias", [16, 128, 21, 128])
        if 0 in self.layers and "mix" in self.parts:
            self.din("a_w_h", [15, D, 320])
            self.din("a_w_o", [960, D])
            self.din("dil_masks", [128, 3, 128])
        self.out = self.nc.dram_tensor("yT", [D, S], F32, kind="ExternalOutput").ap()

    def bank(self, group=None):
        if group is None:
            b = self.psi
            self.psi = (self.psi + 1) % 8
            return b
        i = self.bank_rot.get(group, 0)
        self.bank_rot[group] = (i + 1) % len(group)
        return group[i]

    def fscr(self):
        i = self.fs_i
        self.fs_i = (self.fs_i + 1) % self.NFS
        return i

    def load_slot(self, src_fn, kc, ncols, parts=128, si=None):
        if si is None:
            si = self.slot_i
            self.slot_i = (self.slot_i + 1) % self.NSLOT
        slot = self.sb["wslot"][si]
        view = slot[0:parts, 0:kc * ncols].rearrange("p (k n) -> p k n", k=kc)
        sem = f"w{si}"
        for k in range(kc):
            src = src_fn(k)
            self.s.dma("pool", sem,
                       (lambda e, o=view[:, k, :], i=src: e.dma_start(out=o, in_=i)),
                       writes=[("wslot", si)])
        self.s.retag([("wslot", si)], sem)
        return si, view


    def mm(self, out, lhsT, rhs, start, stop, r, w):
        self.s.op("pe", (lambda e: e.matmul(out, lhsT, rhs, start=start, stop=stop)), reads=r, writes=w)

    def act(self, out, in_, func, r, w, **kw):
        self.s.op("act", (lambda e: e.activation(out=out, in_=in_, func=func, **kw)), reads=r, writes=w)

    def tt(self, eng, out, in0, in1, op, r, w, strict=False):
        self.s.op(eng, (lambda e: e.tensor_tensor(out=out, in0=in0, in1=in1, op=op)), reads=r, writes=w, strict=strict)

    def stt(self, eng, out, in0, scalar, in1, op0, op1, r, w):
        self.s.op(eng, (lambda e: e.scalar_tensor_tensor(out=out, in0=in0, scalar=scalar, in1=in1, op0=op0, op1=op1)),
                  reads=r, writes=w)

    def tsc(self, eng, out, in0, s1, op0, r, w, s2=None, op1=None):
        if op1 is None:
            self.s.op(eng, (lambda e: e.tensor_scalar(out=out, in0=in0, scalar1=s1, scalar2=None, op0=op0)), reads=r, writes=w)
        else:
            self.s.op(eng, (lambda e: e.tensor_scalar(out=out, in0=in0, scalar1=s1, scalar2=s2, op0=op0, op1=op1)), reads=r, writes=w)

    def cp(self, eng, out, in_, r, w):
        self.s.op(eng, (lambda e: e.tensor_copy(out=out, in_=in_)), reads=r, writes=w)

    def recip(self, out, in_, r, w):
        self.s.op("dve", (lambda e: e.reciprocal(out=out, in_=in_)), reads=r, writes=w)

    def memset(self, eng, ap, val, w):
        self.s.op(eng, (lambda e: e.memset(ap, val)), writes=w)

    def rview(self, off, n, dt=BF16):
        v = self.sb["R"][:, off:off + n]
        return v.bitcast(F32) if dt == F32 else v

    def load_rope(self, name):
        sb, s = self.sb, self.s
        for i, key in enumerate(("ropeC", "ropeS")):
            s.dma("sp", "rope", (lambda e, i=i, key=key: e.dma_start(out=sb[key][:, :], in_=self.dram[name][i])),
                  writes=[(key,)])
        s.retag([("ropeC",), ("ropeS",)], "rope")

    def out_proj_pair(self, wo_view, wi, si, otp, otp_key, kparts=128):
        sb = self.sb
        xT = sb["xT"]
        for tc in range(NTC):
            for oc in range(8):
                tsl = slice(tc * TC, (tc + 1) * TC)
                b = self.bank()
                pb = self.ps[b]
                self.mm(pb[:, :], wo_view[0:kparts, wi, oc * 128:(oc + 1) * 128], otp[0:kparts, tsl], True, True,
                        r=[("wslot", si)] + keys(otp_key, tc), w=keys("ps", b))
                self.tt("dve", xT[:, oc, tsl], pb[:, :], xT[:, oc, tsl], ALU.add,
                        r=keys("ps", b) + keys("xT", oc, tc), w=keys("xT", oc, tc))

    LA = 3

    def flush_deferred(self):
        for (_, fn) in self.deferred:
            fn()
        self.deferred = []

    def run_pipeline(self, n, front, back, la=None, flush=True):
        la = self.LA if la is None else la
        self.deferred = []
        for i in range(n + la):
            if i < n:
                front(i)
            if i >= la:
                back(i - la)
                keep = []
                for (due, fn) in self.deferred:
                    if due <= i - la:
                        fn()
                    else:
                        keep.append((due, fn))
                self.deferred = keep
        if flush:
            self.flush_deferred()

    def attn_dense_head(self, KT, QT, kparts, VA, acc_parts_v, emit_norm, nkt=16):
        sb = self.sb
        PT2 = sb["PT2"]
        npair = nkt // 2
        n = NTC * npair
        st = {}
        accb = [self.bank(self.ACC) for _ in range(NTC)]
        RING = (1, 2, 3)

        def front(i):
            qc, kp = divmod(i, npair)
            qsl = slice(qc * TC, (qc + 1) * TC)
            pb = RING[self.ring_i % len(RING)]
            self.ring_i += 1
            for half in range(2):
                kt = 2 * kp + half
                self.mm(self.ps[2 * pb + half][:, :], KT[0:kparts, kt * 128:(kt + 1) * 128], QT[0:kparts, qsl], True, True,
                        r=keys("KT", kt // 4) + keys("QT", qc), w=keys("ps", 2 * pb + half))
            pi = self.pt_i
            self.pt_i = (self.pt_i + 1) % 4
            self.act(PT2[pi][:, :], self.ps2[pb][:, :], AF.Exp, r=keys("ps", 2 * pb) + keys("ps", 2 * pb + 1), w=keys("PT2", pi))
            st[i] = pi

        def back(i):
            qc, kp = divmod(i, npair)
            ba = accb[qc]
            pi = st.pop(i)
            for half in range(2):
                kt = 2 * kp + half
                self.mm(self.ps[ba][:, :], VA[:, kt, :], PT2[pi][:, half * TC:(half + 1) * TC], kt == 0, kt == nkt - 1,
                        r=keys("VA", kt // 8) + keys("PT2", pi), w=keys("ps", ba))
            if kp == npair - 1:
                emit_norm(qc, ba)
        self.run_pipeline(n, front, back, la=2)

    def mixer_mla(self, li):
        s, sb, dram = self.s, self.sb, self.dram
        xT, hT, sq, ones, G = sb["xT"], sb["hT"], sb["sq"], sb["ones"], sb["gains"]
        QT, KT, VA, OTp, cqn, ckvn, KR = sb["QT"], sb["KT"], sb["VA"], sb["OTp"], sb["cqn"], sb["ckvn"], sb["KR"]
        C, Sn = sb["ropeC"], sb["ropeS"]
        scale = float((64 + 32) ** -0.5)
        self.load_rope("ropeL")
        self.memset("pool", VA[:, :, 64:128], 1.0, keys("VA", range(2)))
        self.memset("pool", QT[96:128, :], 0.0, keys("QT", range(NTC)))
        self.memset("pool", KT[96:128, :], 0.0, keys("KT", range(NTC)))
        self.rmsnorm(G_MIX + li)
        w_in, w_kr = dram["b_w_in"], dram["b_w_kr"]
        ai, av = self.load_slot(lambda k: w_in[k * 128:(k + 1) * 128, 0:384], 8, 384, si=0)
        bi, bv = self.load_slot(lambda k: w_kr[k * 128:(k + 1) * 128, :], 8, 192, si=1)
        ui, uv = self.load_slot(lambda k: dram["b_w_uq"][k * 128:(k + 1) * 128, :], 2, 1536, si=2)
        upi, upv = self.load_slot(lambda k: dram["b_w_uq_p"][k * 128:(k + 1) * 128, :], 2, 1536, si=3)
        for tc in range(NTC):
            tsl = slice(tc * TC, (tc + 1) * TC)
            zb = []
            for (view, vi, c0, m) in ((av, ai, 0, 128), (av, ai, 128, 128), (av, ai, 256, 128), (bv, bi, 0, 96), (bv, bi, 96, 96)):
                b = self.bank()
                zb.append(b)
                for k in range(8):
                    self.mm(self.ps[b][0:m, :], view[:, k, c0:c0 + m], hT[:, k, tsl], k == 0, k == 7,
                            r=keys("hT", k, tc) + [("wslot", vi)], w=keys("ps", b))
            for (chunks, nfeat, gcol, dst, dkey) in (((0, 1), 256, NG * 8, cqn, "cqn"), ((2,), 128, NG * 8 + 2, ckvn, "ckvn")):
                for j, zi in enumerate(chunks):
                    self.act(sq[:, j, :], self.ps[zb[zi]][:, :], AF.Square, r=keys("ps", zb[zi]), w=keys("sq", j))
                bn = self.bank()
                for j in range(len(chunks)):
                    self.mm(self.ps[bn][:, :], ones[:, :], sq[:, j, :], j == 0, j == len(chunks) - 1,
                            r=keys("sq", j), w=keys("ps", bn))
                fi = self.fscr()
                fs = sb["fscr"][fi]
                self.act(fs[:, :], self.ps[bn][:, :], AF.Ln, r=keys("ps", bn), w=keys("fscr", fi),
                         scale=1.0 / nfeat, bias=sb["eps"][:, 0:1])
                self.act(fs[:, :], fs[:, :], AF.Exp, r=keys("fscr", fi), w=keys("fscr", fi), scale=-0.5)
                for j, zi in enumerate(chunks):
                    o = dst[:, j, tsl] if len(chunks) > 1 else dst[:, tsl]
                    self.stt("dve", o, self.ps[zb[zi]][:, :], G[:, gcol + j:gcol + j + 1], fs[:, :], ALU.mult, ALU.mult,
                             r=keys("ps", zb[zi]) + keys("fscr", fi), w=keys(dkey, tc))
            f1i, f2i = self.fscr(), self.fscr()
            f1, f2 = sb["fscr"][f1i], sb["fscr"][f2i]
            self.tt("dve", f1[64:96, :], self.ps[zb[3]][64:96, :], C[64:96, tsl], ALU.mult,
                    r=keys("ps", zb[3]) + [("ropeC",)], w=keys("fscr", f1i))
            self.tt("dve", f2[64:96, :], self.ps[zb[4]][64:96, :], Sn[64:96, tsl], ALU.mult,
                    r=keys("ps", zb[4]) + [("ropeS",)], w=keys("fscr", f2i))
            self.tt("pool", KT[64:96, tsl], f1[64:96, :], f2[64:96, :], ALU.add,
                    r=keys("fscr", f1i) + keys("fscr", f2i), w=keys("KT", tc))
        ki, kv = self.load_slot(lambda k: dram["b_w_ukv"][:, :], 1, 2048, si=0)
        woi = 1
        wov = None
        def proj(h):
            for tc in range(NTC):
                tsl = slice(tc * TC, (tc + 1) * TC)
                bq, bqp = self.bank((2, 3, 4, 5, 6, 7)), self.bank((2, 3, 4, 5, 6, 7))
                for (b, view, vi) in ((bq, uv, ui), (bqp, upv, upi)):
                    for k in range(2):
                        self.mm(self.ps[b][0:96, :], view[:, k, h * 96:(h + 1) * 96], cqn[:, k, tsl], k == 0, k == 1,
                                r=keys("cqn", tc) + [("wslot", vi)], w=keys("ps", b))
                self.tsc("dve", QT[0:64, tsl], self.ps[bq][0:64, :], scale, ALU.mult, r=keys("ps", bq), w=keys("QT", tc))
                f1i, f2i = self.fscr(), self.fscr()
                f1, f2 = sb["fscr"][f1i], sb["fscr"][f2i]
                self.stt("dve", f1[64:96, :], self.ps[bq][64:96, :], scale, C[64:96, tsl], ALU.mult, ALU.mult,
                         r=keys("ps", bq) + [("ropeC",)], w=keys("fscr", f1i))
                self.stt("dve", f2[64:96, :], self.ps[bqp][64:96, :], scale, Sn[64:96, tsl], ALU.mult, ALU.mult,
                         r=keys("ps", bqp) + [("ropeS",)], w=keys("fscr", f2i))
                self.tt("pool", QT[64:96, tsl], f1[64:96, :], f2[64:96, :], ALU.add,
                        r=keys("fscr", f1i) + keys("fscr", f2i), w=keys("QT", tc))
                bk = self.bank((2, 3, 4, 5, 6, 7))
                self.mm(self.ps[bk][0:64, :], kv[:, 0, h * 128:h * 128 + 64], ckvn[:, tsl], True, True,
                        r=keys("ckvn", tc) + [("wslot", ki)], w=keys("ps", bk))
                self.cp("dve", KT[0:64, tsl], self.ps[bk][0:64, :], r=keys("ps", bk), w=keys("KT", tc))
            for half in range(2):
                bvb = self.bank((2, 3, 4, 5, 6, 7))
                for j in range(8):
                    tt_ = half * 8 + j
                    self.mm(self.ps[bvb][:, j * 64:(j + 1) * 64], ckvn[:, tt_ * 128:(tt_ + 1) * 128], kv[:, 0, h * 128 + 64:h * 128 + 128],
                            True, True, r=keys("ckvn", tt_ // 4) + [("wslot", ki)], w=keys("ps", bvb))
                self.cp("dve", VA[:, half * 8:(half + 1) * 8, 0:64], self.ps[bvb][:, :].rearrange("p (t d) -> p t d", t=8),
                        r=keys("ps", bvb), w=keys("VA", half))
        proj(0)
        for h in range(16):
            if h % 8 == 0:
                half = h // 8
                _, wov = self.load_slot(lambda k, half=half: dram["b_w_o"][half * 512 + k * 128:half * 512 + (k + 1) * 128, :], 4, 1024, si=woi)
            par = 0
            ot = OTp[par]
            okey = f"OTp{par}"

            def norm(qc, ba, h=h, ot=ot, okey=okey):
                qsl = slice(qc * TC, (qc + 1) * TC)
                fi = self.fscr()
                fs = sb["fscr"][fi]
                self.recip(fs[64:128, :], self.ps[ba][64:128, :], r=keys("ps", ba), w=keys("fscr", fi))
                r0 = (h % 2) * 64
                self.tt("dve", ot[r0:r0 + 64, qsl], self.ps[ba][0:64, :], fs[64:128, :], ALU.mult,
                        r=keys("ps", ba) + keys("fscr", fi), w=keys(okey, qc))
            self.attn_dense_head(KT, QT, 128, VA, 64, norm)
            if h + 1 < 16:
                proj(h + 1)
                if h + 1 == 15:
                    self.mlp_prefetch(li, 2)
            if h % 2 == 1:
                self.out_proj_pair(wov, (h // 2) % 4, woi, ot, okey)


    def dbg(self, name, ap, rkeys):
        if not getattr(self, "debug", False):
            return
        parts, n = ap.shape
        t = self.nc.dram_tensor("dbg_" + name, [parts, n], F32, kind="ExternalOutput").ap()
        self.s.dma("pool", "dbg", (lambda e: e.dma_start(out=t, in_=ap)), reads=rkeys)
        self.dbg_names.append(name)

    def fill_slot(self, si, pieces):
        slot = self.sb["wslot"][si]
        sem = f"w{si}"
        for pc in pieces:
            off, src = pc[0], pc[1]
            p0 = pc[2] if len(pc) > 2 else 0
            parts, n = src.shape
            self.s.dma("pool", sem, (lambda e, o=slot[p0:p0 + parts, off:off + n], i=src: e.dma_start(out=o, in_=i)),
                       writes=[("wslot", si)])
        self.s.retag([("wslot", si)], sem)
        return slot

    def rope_evac(self, dst, psq, psqp, bq, bqp, tsl, scale, dkeys, rows=slice(0, 128)):
        sb = self.sb
        C, Sn = sb["ropeC"], sb["ropeS"]
        f1i, f2i = self.fscr(), self.fscr()
        f1, f2 = sb["fscr"][f1i], sb["fscr"][f2i]
        self.stt("dve", f1[rows, :], psq[rows, :], scale, C[rows, tsl], ALU.mult, ALU.mult,
                 r=keys("ps", bq) + [("ropeC",)], w=keys("fscr", f1i))
        self.stt("dve", f2[rows, :], psqp[rows, :], scale, Sn[rows, tsl], ALU.mult, ALU.mult,
                 r=keys("ps", bqp) + [("ropeS",)], w=keys("fscr", f2i))
        self.tt("pool", dst, f1[rows, :], f2[rows, :], ALU.add,
                r=keys("fscr", f1i) + keys("fscr", f2i), w=dkeys)

    def rope_evac2(self, dstA, dstB, psq, psqp, bq, bqp, tsl, scale, dkeys):
        sb = self.sb
        C, Sn = sb["ropeC"], sb["ropeS"]
        f1i, f2i = self.fscr(), self.fscr()
        f1, f2 = sb["fscr"][f1i], sb["fscr"][f2i]
        self.stt("dve", f1[:, :], psq[:, :], scale, C[:, tsl], ALU.mult, ALU.mult,
                 r=keys("ps", bq) + [("ropeC",)], w=keys("fscr", f1i))
        self.stt("dve", f2[:, :], psqp[:, :], scale, Sn[:, tsl], ALU.mult, ALU.mult,
                 r=keys("ps", bqp) + [("ropeS",)], w=keys("fscr", f2i))
        self.tt("pool", dstA, f1[0:64, :], f2[0:64, :], ALU.add, r=keys("fscr", f1i) + keys("fscr", f2i), w=dkeys)
        self.tt("pool", dstB, f1[64:128, :], f2[64:128, :], ALU.add, r=keys("fscr", f1i) + keys("fscr", f2i), w=dkeys)

    def mixer_diff(self, li):
        s, sb, dram = self.s, self.sb, self.dram
        xT, hT, sq, ones, G = sb["xT"], sb["hT"], sb["sq"], sb["ones"], sb["gains"]
        QT, KT, VA, OTp, PT = sb["QT"], sb["KT"], sb["VA"], sb["OTp"], sb["PT"]
        lam_init = LAMBDA_INIT[li]
        scale = 0.125
        self.load_rope("ropeP")
        QA, QB = QT, self.rview(12288, 2048)
        O1 = [self.rview(o_, 1024, F32) for o_ in (14336, 15360, 20480, 21504)]
        self.memset("pool", QA[64:128, :], 0.0, keys("QT", range(NTC)))
        self.memset("pool", QB[0:64, :], 0.0, keys("QT", range(NTC)))
        lamv, lams = sb["lamv"], sb["lams"]
        s.dma("sp", "lamv", (lambda e: e.dma_start(out=lamv[:, :], in_=dram["d_lamv"])), writes=[("lamv",)])
        fi = self.fscr()
        fs = sb["fscr"][fi]
        for j in range(2):
            self.tt("dve", fs[:, j * 64:(j + 1) * 64], lamv[:, (2 * j) * 64:(2 * j + 1) * 64], lamv[:, (2 * j + 1) * 64:(2 * j + 2) * 64],
                    ALU.mult, r=[("lamv",)], w=keys("fscr", fi), strict=True)
            s.op("dve", (lambda e, j=j: e.reduce_sum(out=lams[:, j:j + 1], in_=fs[:, j * 64:(j + 1) * 64], axis=mybir.AxisListType.X)),
                 reads=keys("fscr", fi), writes=[("lams",)], strict=True)
        self.act(lams[:, 2:4], lams[:, 0:2], AF.Exp, r=[("lams",)], w=[("lams",)])
        s.op("dve", (lambda e: e.memset(lams[:, 6:7], -lam_init)), writes=[("lams",)], strict=True)
        self.tt("dve", lams[:, 4:5], lams[:, 3:4], lams[:, 2:3], ALU.subtract, r=[("lams",)], w=[("lams",)], strict=True)
        self.tt("dve", lams[:, 4:5], lams[:, 4:5], lams[:, 6:7], ALU.add, r=[("lams",)], w=[("lams",)], strict=True)
        s.op("dve", (lambda e: e.tensor_scalar(out=lams[:, 5:6], in0=G[:, NG * 8 + 3:NG * 8 + 4], scalar1=1.0 - lam_init, scalar2=None, op0=ALU.mult)),
             reads=[("gains",), ("lams",)], writes=[("lams",)], strict=True)
        self.rmsnorm(G_MIX + li)
        wa, wb, wo = dram["d_w_a"], dram["d_w_b"], dram["d_w_o"]

        def fetch(h):
            sa, sbi = (0, 1) if h % 2 == 0 else (2, 3)
            A = self.fill_slot(sa, [(k * 384, wa[h, k * 128:(k + 1) * 128, :]) for k in range(8)])
            B = self.fill_slot(sbi, [(k * 256, wb[h, k * 128:(k + 1) * 128, :]) for k in range(8)]
                               + [(2048, wo[h * 128:(h + 1) * 128, :])])
            return (sa, A[:, 0:3072].rearrange("p (k n) -> p k n", k=8), sbi, B[:, 0:2048].rearrange("p (k n) -> p k n", k=8),
                    B[:, 2048:3072].rearrange("p (a n) -> p a n", a=1))
        def proj(sa, A, sbi, B):
            for (dst, dk, c0, c0p) in ((QT, "QT", 0, 0), (KT, "KT", 128, 128)):
                for tc in range(NTC):
                    tsl = slice(tc * TC, (tc + 1) * TC)
                    bq, bqp = self.bank((2, 3, 4, 5)), self.bank((2, 3, 4, 5))
                    for k in range(8):
                        self.mm(self.ps[bq][:, :], A[:, k, c0:c0 + 128], hT[:, k, tsl], k == 0, k == 7,
                                r=keys("hT", k, tc) + [("wslot", sa)], w=keys("ps", bq))
                    for k in range(8):
                        self.mm(self.ps[bqp][:, :], B[:, k, c0p:c0p + 128], hT[:, k, tsl], k == 0, k == 7,
                                r=keys("hT", k, tc) + [("wslot", sbi)], w=keys("ps", bqp))
                    if dk == "KT":
                        self.rope_evac(dst[:, tsl], self.ps[bq], self.ps[bqp], bq, bqp, tsl, 1.0, keys(dk, tc))
                    else:
                        self.rope_evac2(QA[0:64, tsl], QB[64:128, tsl], self.ps[bq], self.ps[bqp], bq, bqp, tsl, scale, keys(dk, tc))
            for t4 in range(4):
                bvb = self.bank((2, 3, 4, 5))
                for j in range(4):
                    tt_ = t4 * 4 + j
                    for k in range(8):
                        self.mm(self.ps[bvb][:, j * 128:(j + 1) * 128], hT[:, k, tt_ * 128:(tt_ + 1) * 128], A[:, k, 256:384], k == 0, k == 7,
                                r=keys("hT", k, tt_ // 4) + [("wslot", sa)], w=keys("ps", bvb))
                self.cp("dve", VA[:, t4 * 4:(t4 + 1) * 4, :], self.ps[bvb][:, :].rearrange("p (t d) -> p t d", t=4),
                        r=keys("ps", bvb), w=keys("VA", t4 // 2))
        cur = fetch(0)
        proj(cur[0], cur[1], cur[2], cur[3])
        for h in range(8):
            sa, A, sbi, B, WO = cur
            nxt = fetch(h + 1) if h + 1 < 8 else None
            if h == 7:
                self.mlp_prefetch(li, 0)
            par = 0
            ot, okey = OTp[par], f"OTp{par}"
            if h == 0:
                self.dbg("QT", QT[:, :], keys("QT", range(4)))
                self.dbg("KT", KT[:, :], keys("KT", range(4)))
                self.dbg("V0", VA[:, 0, :], keys("VA", range(2)))
                self.dbg("lams", lams[:, :], [("lams",)])
            n = NTC * 2 * 8
            st = {}
            o1s = {}
            SETS = ((0, 1), (6, 7))
            RING = (1, 2)
            PT2 = sb["PT2"]

            def front(i, h=h):
                g_, kp = divmod(i, 8)
                qc, mp = divmod(g_, 2)
                qsl = slice(qc * TC, (qc + 1) * TC)
                pb = RING[self.ring_i % 2]
                self.ring_i += 1
                for half in range(2):
                    kt = 2 * kp + half
                    self.mm(self.ps[2 * pb + half][:, :], KT[:, kt * 128:(kt + 1) * 128], (QA if mp == 0 else QB)[:, qsl], True, True,
                            r=keys("KT", kt // 4) + keys("QT", qc), w=keys("ps", 2 * pb + half))
                pi = self.pt_i
                self.pt_i = (self.pt_i + 1) % 4
                self.act(PT2[pi][:, :], self.ps2[pb][:, :], AF.Exp, r=keys("ps", 2 * pb) + keys("ps", 2 * pb + 1), w=keys("PT2", pi))
                st[i] = pi

            def back(i, h=h, ot=ot, okey=okey):
                g_, kp = divmod(i, 8)
                qc, mp = divmod(g_, 2)
                qsl = slice(qc * TC, (qc + 1) * TC)
                ba, bl = SETS[g_ % 2]
                pi = st.pop(i)
                for half in range(2):
                    kt = 2 * kp + half
                    pt = PT2[pi][:, half * TC:(half + 1) * TC]
                    self.mm(self.ps[ba][:, :], VA[:, kt, :], pt, kt == 0, kt == 15,
                            r=keys("VA", kt // 8) + keys("PT2", pi), w=keys("ps", ba))
                    self.mm(self.ps[bl][:, :], ones[:, :], pt, kt == 0, kt == 15,
                            r=keys("PT2", pi), w=keys("ps", bl))
                if kp != 7:
                    return
                ri = self.fscr()
                rr = sb["fscr"][ri]
                self.recip(rr[:, :], self.ps[bl][:, :], r=keys("ps", bl), w=keys("fscr", ri))
                o1 = O1[qc]
                if mp == 0:
                    self.tt("dve", o1[:, :], self.ps[ba][:, :], rr[:, :], ALU.mult,
                            r=keys("ps", ba) + keys("fscr", ri), w=keys("O1", qc))
                    return
                self.tt("dve", rr[:, :], self.ps[ba][:, :], rr[:, :], ALU.mult,
                        r=keys("ps", ba) + keys("fscr", ri), w=keys("fscr", ri))
                self.stt("dve", o1[:, :], rr[:, :], lams[:, 4:5], o1[:, :], ALU.mult, ALU.add,
                         r=keys("fscr", ri) + keys("O1", qc) + [("lams",)], w=keys("O1", qc))
                self.act(sq[:, qc, :], o1[:, :], AF.Square, r=keys("O1", qc), w=keys("sq", qc))

                def tail(qc=qc, o1=o1, qsl=qsl):
                    pbn = RING[self.ring_i % 2]
                    self.ring_i += 1
                    bn = 2 * pbn
                    self.mm(self.ps[bn][:, :], ones[:, :], sq[:, qc, :], True, True, r=keys("sq", qc), w=keys("ps", bn))
                    r2 = self.fscr()
                    r2t = sb["fscr"][r2]
                    self.act(r2t[:, :], self.ps[bn][:, :], AF.Ln, r=keys("ps", bn), w=keys("fscr", r2), scale=1.0 / 128, bias=sb["eps"][:, 0:1])
                    self.act(r2t[:, :], r2t[:, :], AF.Exp, r=keys("fscr", r2), w=keys("fscr", r2), scale=-0.5)
                    self.stt("dve", ot[:, qsl], o1[:, :], lams[:, 5:6], r2t[:, :], ALU.mult, ALU.mult,
                             r=keys("O1", qc) + keys("fscr", r2) + [("lams",)], w=keys(okey, qc))
                self.deferred.append((i + 9, tail))
            self.run_pipeline(n, front, back, la=1, flush=False)
            if h == 0:
                self.dbg("OT", ot[:, :], keys(okey, range(4)))
            if nxt is not None:
                proj(nxt[0], nxt[1], nxt[2], nxt[3])
            self.flush_deferred()
            self.out_proj_pair(WO, 0, sbi, ot, okey)
            cur = nxt


    @staticmethod
    def na_tiles(t):
        if 2 <= t <= 13:
            return [(t + d, d + 2) for d in range(-2, 3)]
        base = {0: 5, 1: 9, 14: 13, 15: 17}[t]
        k0 = 0 if t < 2 else 12
        return [(k0 + j, base + j) for j in range(4)]

    def mixer_na(self, li):
        s, sb, dram = self.s, self.sb, self.dram
        xT, hT, ident = sb["xT"], sb["hT"], sb["ident"]
        QT, KT, OTp, PT = sb["QT"], sb["KT"], sb["OTp"], sb["PT"]
        VA2 = self.rview(12288, 4096).rearrange("p (t h d) -> p t h d", t=16, h=2)
        BIs = [self.rview(o_, 5376).rearrange("p (h b q) -> p h b q", h=2, b=21) for o_ in (24576, 16384)]
        self.memset("pool", VA2[:, :, :, 64:128], 1.0, keys("VA", range(2)))
        QA, QB = QT, self.rview(29952, 2048)
        self.memset("pool", QA[64:128, :], 0.0, keys("QT", range(NTC)))
        self.memset("pool", QB[0:64, :], 0.0, keys("QT", range(NTC)))
        self.rmsnorm(G_MIX + li)
        s.barrier()
        wa, wo, nab = dram["c_w_a"], dram["c_w_o"], dram["na_bias"]

        def fetch(c):
            si = c % 4
            A = self.fill_slot(si, [(k * 384, wa[c, k * 128:(k + 1) * 128, :]) for k in range(8)]
                               + [(3072, wo[c * 128:(c + 1) * 128, :])])
            return si, A[:, 0:3072].rearrange("p (k n) -> p k n", k=8), A[:, 3072:4096].rearrange("p (a n) -> p a n", a=1)
        def load_bias(c):
            for hh in range(2):
                s.dma("pool", f"nab{c % 2}", (lambda e, hh=hh, c=c: e.dma_start(out=BIs[c % 2][:, hh, :, :], in_=nab[2 * c + hh])),
                      writes=[("BI", c % 2)])
            s.retag([("BI", c % 2)], f"nab{c % 2}")

        def proj(si, A):
            for (dst, dk, c0, sc) in ((QT, "QT", 0, 0.125), (KT, "KT", 128, 1.0)):
                for tc in range(NTC):
                    tsl = slice(tc * TC, (tc + 1) * TC)
                    bq = self.bank((2, 3, 4, 5))
                    for k in range(8):
                        self.mm(self.ps[bq][:, :], A[:, k, c0:c0 + 128], hT[:, k, tsl], k == 0, k == 7,
                                r=keys("hT", k, tc) + [("wslot", si)], w=keys("ps", bq))
                    if dk == "KT":
                        self.tsc("dve", dst[:, tsl], self.ps[bq][:, :], sc, ALU.mult, r=keys("ps", bq), w=keys(dk, tc))
                    else:
                        self.tsc("dve", QA[0:64, tsl], self.ps[bq][0:64, :], sc, ALU.mult, r=keys("ps", bq), w=keys(dk, tc))
                        self.tsc("dve", QB[64:128, tsl], self.ps[bq][64:128, :], sc, ALU.mult, r=keys("ps", bq), w=keys(dk, tc))
            for t4 in range(4):
                bvb = self.bank((2, 3, 4, 5))
                for j in range(4):
                    tt_ = t4 * 4 + j
                    for k in range(8):
                        self.mm(self.ps[bvb][:, j * 128:(j + 1) * 128], hT[:, k, tt_ * 128:(tt_ + 1) * 128], A[:, k, 256:384], k == 0, k == 7,
                                r=keys("hT", k, tt_ // 4) + [("wslot", si)], w=keys("ps", bvb))
                self.cp("dve", VA2[:, t4 * 4:(t4 + 1) * 4, :, 0:64], self.ps[bvb][:, :].rearrange("p (t h d) -> p t h d", t=4, h=2),
                        r=keys("ps", bvb), w=keys("VA", t4 // 2))
        wl = {0: fetch(0), 1: fetch(1)}
        load_bias(0)
        proj(wl[0][0], wl[0][1])
        for c in range(8):
            si, A, WO = wl[c]
            if c + 2 < 8:
                wl[c + 2] = fetch(c + 2)
            if c + 1 < 8:
                load_bias(c + 1)
            if c == 7:
                self.mlp_prefetch(li, 0)
            BI = BIs[c % 2]
            par = c % 2
            ot, okey = OTp[par], f"OTp{par}"
            units = [(t, j, kt, blk, len(self.na_tiles(t))) for t in range(16) for j, (kt, blk) in enumerate(self.na_tiles(t))]
            st = {}

            def front(i):
                t, j, kt, blk, nt = units[i]
                q128 = slice(t * 128, (t + 1) * 128)
                bs = self.bank(self.SB)
                for hh in range(2):
                    self.mm(self.ps[bs][:, hh * 128:(hh + 1) * 128], KT[:, kt * 128:(kt + 1) * 128], (QA, QB)[hh][:, q128], True, False,
                            r=keys("KT", kt // 4) + keys("QT", t // 4), w=keys("ps", bs))
                    self.mm(self.ps[bs][:, hh * 128:(hh + 1) * 128], ident[:, :], BI[:, hh, blk, :], False, True,
                            r=[("BI", c % 2), ("ident",)], w=keys("ps", bs))
                pi = self.pt_i
                self.pt_i = (self.pt_i + 1) % 4
                self.act(PT[pi][:, 0:256], self.ps[bs][:, 0:256], AF.Exp, r=keys("ps", bs), w=keys("PT", pi))
                st[i] = pi

            def back(i, ot=ot, okey=okey):
                t, j, kt, blk, nt = units[i]
                q128 = slice(t * 128, (t + 1) * 128)
                bas = (0, 1) if t % 2 == 0 else (6, 7)
                pi = st.pop(i)
                for hh in range(2):
                    self.mm(self.ps[bas[hh]][:, 0:128], VA2[:, kt, hh, :], PT[pi][:, hh * 128:(hh + 1) * 128],
                            j == 0, j == nt - 1, r=keys("VA", kt // 8) + keys("PT", pi), w=keys("ps", bas[hh]))
                if j != nt - 1:
                    return
                fi = self.fscr()
                fs = sb["fscr"][fi]
                for hh in range(2):
                    ba = bas[hh]
                    self.recip(fs[64:128, hh * 128:(hh + 1) * 128], self.ps[ba][64:128, 0:128], r=keys("ps", ba), w=keys("fscr", fi))
                    self.tt("dve", ot[hh * 64:(hh + 1) * 64, q128], self.ps[ba][0:64, 0:128],
                            fs[64:128, hh * 128:(hh + 1) * 128], ALU.mult,
                            r=keys("ps", ba) + keys("fscr", fi), w=keys(okey, t // 4))
            self.run_pipeline(len(units), front, back)
            if c + 1 < 8:
                proj(wl[c + 1][0], wl[c + 1][1])
            self.out_proj_pair(WO, 0, si, ot, okey)


    def mixer_dil(self, li):
        s, sb, dram = self.s, self.sb, self.dram
        xT, hT, ident, PT = sb["xT"], sb["hT"], sb["ident"], sb["PT"]
        C, Sn = sb["ropeC"], sb["ropeS"]
        QTd, KTd = self.rview(0, 2048), self.rview(2048, 2048)
        VAd = self.rview(4096, 4096).rearrange("p (t d) -> p t d", t=32)
        OTa = [self.rview(8192 + p_ * 512, 512) for p_ in range(2)]
        OTb = [self.rview(9216 + p_ * 512, 512) for p_ in range(2)]
        U = [self.rview(12288 + g * 4096, 4096, F32) for g in range(3)]
        MK, scl = sb["dmask"], sb["qkscl"]
        self.load_rope("ropeP")
        s.dma("pool", "dmask", (lambda e: e.dma_start(out=MK[:, :, :], in_=dram["dil_masks"])), writes=[("dmask",)])
        self.memset("dve", scl[0:64, :], 0.125, [("qkscl",)])
        self.memset("dve", scl[64:128, :], 1.0, [("qkscl",)])
        self.rmsnorm(G_MIX + li)
        s.barrier()
        self.memset("pool", VAd[:, :, 64:128], 1.0, [("VAd",)])
        for p_ in range(2):
            self.memset("pool", OTb[p_][64:128, :], 0.0, [("OTb", p_)])
        self.memset("pool", QTd[64:128, :], 0.0, [("QTd",)])
        self.memset("pool", KTd[64:128, :], 0.0, [("KTd",)])
        wh, wo = dram["a_w_h"], dram["a_w_o"]
        DIL = (1, 4, 16)

        def fetch(idx):
            j, g = idx // 3, idx % 3
            h = g * 5 + j
            si = idx % 3
            A = self.fill_slot(si, [(k * 320, wh[h, k * 128:(k + 1) * 128, :]) for k in range(8)])
            return si, A[:, 0:2560].rearrange("p (k n) -> p k n", k=8)
        def proj_v(si, A, dil, L):
            for tc in range(NTC):
                tsl = slice(tc * TC, (tc + 1) * TC)
                bq, bqp = self.bank(), self.bank()
                for (b, c0) in ((bq, 0), (bqp, 128)):
                    for k in range(8):
                        self.mm(self.ps[b][:, :], A[:, k, c0:c0 + 128], hT[:, k, tsl], k == 0, k == 7,
                                r=keys("hT", k, tc) + [("wslot", si)], w=keys("ps", b))
                f1i, f2i = self.fscr(), self.fscr()
                f1, f2 = sb["fscr"][f1i], sb["fscr"][f2i]
                self.stt("dve", f1[:, :], self.ps[bq][:, :], scl[:, 0:1], C[:, tsl], ALU.mult, ALU.mult,
                         r=keys("ps", bq) + [("ropeC",), ("qkscl",)], w=keys("fscr", f1i))
                self.stt("dve", f2[:, :], self.ps[bqp][:, :], scl[:, 0:1], Sn[:, tsl], ALU.mult, ALU.mult,
                         r=keys("ps", bqp) + [("ropeS",), ("qkscl",)], w=keys("fscr", f2i))
                n = TC // dil
                l0 = tc * n
                for (dst, dk, r0) in ((QTd, "QTd", 0), (KTd, "KTd", 64)):
                    d = dst[0:64, :].rearrange("p (r l) -> p r l", r=dil)[:, :, l0:l0 + n]
                    a0 = f1[r0:r0 + 64, :].rearrange("p (i r) -> p r i", r=dil)
                    a1 = f2[r0:r0 + 64, :].rearrange("p (i r) -> p r i", r=dil)
                    self.tt("pool" if r0 == 0 else "dve", d, a0, a1, ALU.add, r=keys("fscr", f1i) + keys("fscr", f2i), w=[(dk,)])
            nj = L // 128
            vt = {}
            tl = []
            for r in range(dil):
                for jt in range(nj + 1):
                    p0, m = (0, 64) if jt == 0 else ((jt * 128 - 64, 64) if jt == nj else (jt * 128 - 64, 128))
                    vt[(r, jt)] = (len(tl), m)
                    tl.append((r, p0, m))
            hsub = [hT[:, k, :].rearrange("p (l r) -> p r l", r=dil) for k in range(8)]
            for t0 in range(0, len(tl), 8):
                grp = tl[t0:t0 + 8]
                bvb = self.bank()
                for jj, (r, p0, m) in enumerate(grp):
                    for k in range(8):
                        self.mm(self.ps[bvb][0:m, jj * 64:(jj + 1) * 64], hsub[k][:, r, p0:p0 + m], A[:, k, 256:320], k == 0, k == 7,
                                r=keys("hT", k, range(NTC)) + [("wslot", si)], w=keys("ps", bvb))
                ng = len(grp)
                self.cp("dve", VAd[:, t0:t0 + ng, 0:64], self.ps[bvb][:, 0:ng * 64].rearrange("p (t d) -> p t d", t=ng),
                        r=keys("ps", bvb), w=[("VAd",)])
            return vt, nj

        def attn(g, dil, L, vt, nj):
            Ug = U[g][:, :].rearrange("p (l r) -> p r l", r=dil)
            units = [(r, jq) for r in range(dil) for jq in range(nj)]
            st = {}

            def front(i, L=L, nj=nj, vt=vt):
                r, jq = units[i]
                q0 = r * L + jq * 128
                bs = self.bank(self.SB)
                wins = []
                for w_, jt in enumerate((jq, jq + 1)):
                    ti_, m = vt[(r, jt)]
                    if jt == 0:
                        k0_, mk = r * L, MK[0:64, 2, :]
                    elif jt == nj:
                        k0_, mk = r * L + L - 64, MK[0:64, 1, :]
                    else:
                        k0_, mk = r * L + jt * 128 - 64, MK[:, w_, :]
                    cs = slice(w_ * 128, (w_ + 1) * 128)
                    self.mm(self.ps[bs][0:m, cs], KTd[:, k0_:k0_ + m], QTd[:, q0:q0 + 128], True, False,
                            r=[("KTd",), ("QTd",)], w=keys("ps", bs))
                    self.mm(self.ps[bs][0:m, cs], ident[0:m, 0:m], mk, False, True,
                            r=[("dmask",), ("ident",)], w=keys("ps", bs))
                    wins.append((ti_, m, cs))
                pi = self.pt_i
                self.pt_i = (self.pt_i + 1) % 4
                for (ti_, m, cs) in wins:
                    self.act(PT[pi][0:m, cs], self.ps[bs][0:m, cs], AF.Exp, r=keys("ps", bs), w=keys("PT", pi))
                st[i] = (pi, wins)

            def back(i, Ug=Ug, g=g):
                r, jq = units[i]
                pi, wins = st.pop(i)
                ba = self.bank(self.ACC)
                for w_, (ti_, m, cs) in enumerate(wins):
                    self.mm(self.ps[ba][:, 0:128], VAd[0:m, ti_, :], PT[pi][0:m, cs], w_ == 0, w_ == 1,
                            r=[("VAd",)] + keys("PT", pi), w=keys("ps", ba))
                self.act(Ug[:, r, jq * 128:(jq + 1) * 128], self.ps[ba][:, 0:128], AF.Copy, r=keys("ps", ba), w=[("U", g)])
            self.run_pipeline(len(units), front, back)

        def combine(j, WOv):
            fsum = U[0]
            self.tt("pool", U[0][64:128, :], U[0][64:128, :], U[1][64:128, :], ALU.add, r=[("U", 0), ("U", 1)], w=[("U", 0)])
            self.tt("pool", U[0][64:128, :], U[0][64:128, :], U[2][64:128, :], ALU.add, r=[("U", 0), ("U", 2)], w=[("U", 0)])
            for tc in range(NTC):
                tsl = slice(tc * TC, (tc + 1) * TC)
                par = tc % 2
                fi = self.fscr()
                fs = sb["fscr"][fi]
                self.act(fs[0:64, :], U[0][64:128, tsl], AF.Ln, r=[("U", 0)], w=keys("fscr", fi))
                self.act(fs[0:64, :], fs[0:64, :], AF.Exp, r=keys("fscr", fi), w=keys("fscr", fi), scale=-1.0)
                self.tt("dve", OTa[par][0:64, :], U[0][0:64, tsl], fs[0:64, :], ALU.mult,
                        r=[("U", 0)] + keys("fscr", fi), w=[("OTa", par)])
                self.tt("dve", OTa[par][64:128, :], U[1][0:64, tsl], fs[0:64, :], ALU.mult,
                        r=[("U", 1)] + keys("fscr", fi), w=[("OTa", par)])
                self.tt("dve", OTb[par][0:64, :], U[2][0:64, tsl], fs[0:64, :], ALU.mult,
                        r=[("U", 2)] + keys("fscr", fi), w=[("OTb", par)])
                for oc in range(8):
                    b = self.bank()
                    pb = self.ps[b]
                    self.mm(pb[:, :], WOv[:, 0, oc * 128:(oc + 1) * 128], OTa[par][:, :], True, False,
                            r=[("wslot", 3), ("OTa", par)], w=keys("ps", b))
                    self.mm(pb[:, :], WOv[:, 1, oc * 128:(oc + 1) * 128], OTb[par][:, :], False, True,
                            r=[("wslot", 3), ("OTb", par)], w=keys("ps", b))
                    self.tt("dve", xT[:, oc, tsl], pb[:, :], xT[:, oc, tsl], ALU.add,
                            r=keys("ps", b) + keys("xT", oc, tc), w=keys("xT", oc, tc))


        wl = {0: fetch(0), 1: fetch(1)}
        stt_ = {0: proj_v(wl[0][0], wl[0][1], DIL[0], S // DIL[0])}
        WOv = None
        for idx in range(15):
            j, g = divmod(idx, 3)
            dil = DIL[g]
            L = S // dil
            if idx + 2 < 15:
                wl[idx + 2] = fetch(idx + 2)
            if idx == 14:
                self.mlp_prefetch(li, 0)
            if g == 0:
                hrow = lambda g_, j=j: wo[(g_ * 5 + j) * 64:(g_ * 5 + j + 1) * 64, :]
                WOs = self.fill_slot(3, [(0, hrow(0), 0), (0, hrow(1), 64), (1024, hrow(2), 0), (1024, hrow(2), 64)])
                WOv = WOs[:, 0:2048].rearrange("p (a n) -> p a n", a=2)
            vt, nj = stt_.pop(idx)
            attn(g, dil, L, vt, nj)
            if idx + 1 < 15:
                g1 = (idx + 1) % 3
                stt_[idx + 1] = proj_v(wl[idx + 1][0], wl[idx + 1][1], DIL[g1], S // DIL[g1])
            if g == 2:
                combine(j, WOv)

    def mixer(self, li):
        self.s.barrier()
        getattr(self, ["mixer_dil", "mixer_mla", "mixer_na", "mixer_diff"][li])(li)
        self.s.barrier()

    def rmsnorm(self, gcol):
        s, sb = self.s, self.sb
        xT, hT, sq, ones, G = sb["xT"], sb["hT"], sb["sq"], sb["ones"], sb["gains"]
        for tc in range(NTC):
            tsl = slice(tc * TC, (tc + 1) * TC)
            for c in range(8):
                s.op("act", (lambda e, c=c, tsl=tsl: e.activation(out=sq[:, c, :], in_=xT[:, c, tsl], func=AF.Square)),
                     reads=keys("xT", c, tc), writes=keys("sq", c))
            b = self.bank()
            pb = self.ps[b]
            for c in range(8):
                s.op("pe", (lambda e, c=c, pb=pb: e.matmul(pb[:, :], ones[:, :], sq[:, c, :], start=(c == 0), stop=(c == 7))),
                     reads=keys("sq", c), writes=keys("ps", b))
            fi = self.fscr()
            fs = sb["fscr"][fi]
            s.op("act", (lambda e, pb=pb, fs=fs: e.activation(out=fs[:, :], in_=pb[:, :], func=AF.Ln, scale=1.0 / D, bias=sb["eps"][:, 0:1])),
                 reads=keys("ps", b), writes=keys("fscr", fi))
            s.op("act", (lambda e, fs=fs: e.activation(out=fs[:, :], in_=fs[:, :], func=AF.Exp, scale=-0.5)),
                 reads=keys("fscr", fi), writes=keys("fscr", fi))
            for c in range(8):
                s.op("dve", (lambda e, c=c, tsl=tsl, fs=fs: e.scalar_tensor_tensor(
                    out=hT[:, c, tsl], in0=xT[:, c, tsl], scalar=G[:, gcol * 8 + c:gcol * 8 + c + 1], in1=fs[:, :],
                    op0=ALU.mult, op1=ALU.mult)),
                     reads=keys("xT", c, tc) + keys("fscr", fi), writes=keys("hT", c, tc))

    def mlp_prefetch(self, li, first_slot):
        if "mlp" not in self.parts:
            return
        self.slot_i = first_slot
        w_up = self.dram["w_up"][li]
        w_down = self.dram["w_down"][li]
        u = self.load_slot(lambda k: w_up[k * 128:(k + 1) * 128, 0:512], 8, 512)
        d = self.load_slot(lambda k: w_down[k * 128:(k + 1) * 128, :], 4, 1024)
        self.mlp_pre = (u, d)

    def mlp(self, li):
        s, sb = self.s, self.sb
        xT, hT, hid = sb["xT"], sb["hT"], sb["hid"]
        w_up = self.dram["w_up"][li]
        w_down = self.dram["w_down"][li]
        self.rmsnorm(G_MLP + li)
        NG_ = 8
        fills = {}

        def fetch(g):
            u = self.load_slot(lambda k, g=g: w_up[k * 128:(k + 1) * 128, g * 512:(g + 1) * 512], 8, 512)
            d = self.load_slot(lambda k, g=g: w_down[g * 512 + k * 128:g * 512 + (k + 1) * 128, :], 4, 1024)
            fills[g] = (u, d)

        if getattr(self, "mlp_pre", None) is not None:
            fills[0] = self.mlp_pre
            self.mlp_pre = None
        else:
            fetch(0)
        self.ple_pre = None
        for g in range(NG_):
            if g + 1 < NG_:
                fetch(g + 1)
            elif "ple" in self.parts:
                self.ple_pre = self.ple_prefetch(li)
            (ui, uv), (di, dv) = fills[g]
            hb = g % 2
            for tc in range(NTC):
                for mi in range(4):
                    tsl = slice(tc * TC, (tc + 1) * TC)
                    b = self.bank()
                    pb = self.ps[b]
                    for k in range(8):
                        s.op("pe", (lambda e, k=k, pb=pb, uv=uv, mi=mi, tsl=tsl: e.matmul(
                            pb[:, :], uv[:, k, mi * 128:(mi + 1) * 128], hT[:, k, tsl], start=(k == 0), stop=(k == 7))),
                             reads=keys("hT", k, tc) + [("wslot", ui)], writes=keys("ps", b))
                    fi = self.fscr()
                    fs = sb["fscr"][fi]
                    s.op("act", (lambda e, pb=pb, fs=fs: e.activation(out=fs[:, :], in_=pb[:, :], func=AF.Relu)),
                         reads=keys("ps", b), writes=keys("fscr", fi))
                    s.op("pool", (lambda e, fs=fs, hb=hb, mi=mi, tsl=tsl: e.tensor_tensor(
                        out=hid[hb][:, mi, tsl], in0=fs[:, :], in1=fs[:, :], op=ALU.mult)),
                         reads=keys("fscr", fi), writes=keys("hid", hb, mi, tc))
            if g == NG_ - 1 and self.ple_pre is not None:
                wg_ = self.dram["w_ple_gate"][li]
                g1 = self.load_slot(lambda k: wg_[k * 128:(k + 1) * 128, 512:1024], 8, 512)
                self.ple_pre = self.ple_pre + (g1,)
            for tc in range(NTC):
                for oc in range(8):
                    tsl = slice(tc * TC, (tc + 1) * TC)
                    b = self.bank()
                    pb = self.ps[b]
                    for k in range(4):
                        s.op("pe", (lambda e, k=k, pb=pb, dv=dv, oc=oc, hb=hb, tsl=tsl: e.matmul(
                            pb[:, :], dv[:, k, oc * 128:(oc + 1) * 128], hid[hb][:, k, tsl], start=(k == 0), stop=(k == 3))),
                             reads=keys("hid", hb, k, tc) + [("wslot", di)], writes=keys("ps", b))
                    s.op("dve", (lambda e, pb=pb, oc=oc, tsl=tsl: e.tensor_tensor(
                        out=xT[:, oc, tsl], in0=pb[:, :], in1=xT[:, oc, tsl], op=ALU.add)),
                         reads=keys("ps", b) + keys("xT", oc, tc), writes=keys("xT", oc, tc))

    def ple_prefetch(self, li):
        s, sb = self.s, self.sb
        pT = sb["pT"]
        wg = self.dram["w_ple_gate"][li]
        wp = self.dram["w_ple_proj"][li]
        for k in range(2):
            s.dma("pool", "pT", (lambda e, k=k: e.dma_start(out=pT[:, k, :], in_=self.dram["pT"][li, k * 128:(k + 1) * 128, :])),
                  writes=[("pT",)])
        s.retag([("pT",)], "pT")
        p_ = self.load_slot(lambda k: wp[k * 128:(k + 1) * 128, :], 2, 1024)
        g0 = self.load_slot(lambda k: wg[k * 128:(k + 1) * 128, 0:512], 8, 512)
        return p_, g0

    def ple(self, li):
        s, sb = self.s, self.sb
        xT, hT, pT = sb["xT"], sb["hT"], sb["pT"]
        wg = self.dram["w_ple_gate"][li]
        pre = getattr(self, "ple_pre", None)
        if pre is None:
            pre = self.ple_prefetch(li)
        self.ple_pre = None
        (pi, pv), g0 = pre[0], pre[1]
        self.rmsnorm(G_PLE + li)
        g1 = pre[2] if len(pre) > 2 else self.load_slot(lambda k: wg[k * 128:(k + 1) * 128, 512:1024], 8, 512)
        gates = (g0, g1)
        for tc in range(NTC):
            tsl = slice(tc * TC, (tc + 1) * TC)
            for oc in range(8):
                gi, gv = gates[oc // 4]
                m = oc % 4
                b = self.bank()
                pb = self.ps[b]
                for k in range(8):
                    self.mm(pb[:, :], gv[:, k, m * 128:(m + 1) * 128], hT[:, k, tsl], k == 0, k == 7,
                            r=keys("hT", k, tc) + [("wslot", gi)], w=keys("ps", b))
                b2 = self.bank()
                pb2 = self.ps[b2]
                for k in range(2):
                    self.mm(pb2[:, :], pv[:, k, oc * 128:(oc + 1) * 128], pT[:, k, tsl], k == 0, k == 1,
                            r=[("pT",), ("wslot", pi)], w=keys("ps", b2))
                fi = self.fscr()
                fs = sb["fscr"][fi]
                self.act(fs[:, :], pb[:, :], AF.Sigmoid, r=keys("ps", b), w=keys("fscr", fi))
                self.tt("dve", fs[:, :], pb2[:, :], fs[:, :], ALU.mult, r=keys("ps", b2) + keys("fscr", fi), w=keys("fscr", fi))
                self.tt("pool", xT[:, oc, tsl], fs[:, :], xT[:, oc, tsl], ALU.add,
                        r=keys("fscr", fi) + keys("xT", oc, tc), w=keys("xT", oc, tc))

    def final_norm(self):
        s, sb = self.s, self.sb
        xT, sq, ones, G = sb["xT"], sb["sq"], sb["ones"], sb["gains"]
        for tc in range(NTC):
            tsl = slice(tc * TC, (tc + 1) * TC)
            for c in range(8):
                s.op("act", (lambda e, c=c, tsl=tsl: e.activation(out=sq[:, c, :], in_=xT[:, c, tsl], func=AF.Square)),
                     reads=keys("xT", c, tc), writes=keys("sq", c))
            b = self.bank()
            pb = self.ps[b]
            for c in range(8):
                s.op("pe", (lambda e, c=c, pb=pb: e.matmul(pb[:, :], ones[:, :], sq[:, c, :], start=(c == 0), stop=(c == 7))),
                     reads=keys("sq", c), writes=keys("ps", b))
            fi = self.fscr()
            fs = sb["fscr"][fi]
            s.op("act", (lambda e, pb=pb, fs=fs: e.activation(out=fs[:, :], in_=pb[:, :], func=AF.Ln, scale=1.0 / D, bias=sb["eps"][:, 0:1])),
                 reads=keys("ps", b), writes=keys("fscr", fi))
            s.op("act", (lambda e, fs=fs: e.activation(out=fs[:, :], in_=fs[:, :], func=AF.Exp, scale=-0.5)),
                 reads=keys("fscr", fi), writes=keys("fscr", fi))
            for c in range(8):
                s.op("dve", (lambda e, c=c, tsl=tsl, fs=fs: e.scalar_tensor_tensor(
                    out=xT[:, c, tsl], in0=xT[:, c, tsl], scalar=G[:, G_FINAL * 8 + c:G_FINAL * 8 + c + 1], in1=fs[:, :],
                    op0=ALU.mult, op1=ALU.mult)),
                     reads=keys("xT", c, tc) + keys("fscr", fi), writes=keys("xT", c, tc))

    def build(self):
        nc, s = self.nc, self.s
        self.declare_io()
        self.NSLOT = 4
        self.NFS = 6
        import contextlib
        with contextlib.ExitStack() as st:
            def sbt(name, shape, dt):
                return st.enter_context(nc.sbuf_tensor(name, shape, dt))
            sb = self.sb
            sb["xT"] = sbt("xT_sb", [128, 8, S], F32)
            sb["hT"] = sbt("hT_sb", [128, 8, S], BF16)
            sb["wslot"] = [sbt(f"wslot{i}", [128, 4096], BF16) for i in range(self.NSLOT)]
            sb["fscr"] = [sbt(f"fscr{i}", [128, TC], F32) for i in range(self.NFS)]
            sb["gains"] = sbt("gains_sb", [128, NGC], F32)
            sb["ones"] = sbt("ones_sb", [128, 128], BF16)
            sb["ident"] = sbt("ident_sb", [128, 128], BF16)
            sb["eps"] = sbt("eps_sb", [128, 1], F32)
            sb["lamv"] = sbt("lamv_sb", [128, 256], F32)
            sb["dmask"] = sbt("dmask_sb", [128, 3, 128], BF16)
            sb["qkscl"] = sbt("qkscl_sb", [128, 1], F32)
            sb["lams"] = sbt("lams_sb", [128, 8], F32)
            sb["R"] = sbt("R_sb", [128, 32768], BF16)
            rv = self.rview
            sb["hid"] = [rv(i * 8192, 8192).rearrange("p (m t) -> p m t", m=4) for i in range(2)]
            sb["sq"] = rv(16384, 4096).rearrange("p (c t) -> p c t", c=8)
            sb["pT"] = rv(20480, 4096).rearrange("p (k t) -> p k t", k=2)
            sb["QT"] = rv(0, 2048)
            sb["KT"] = rv(2048, 2048)
            sb["VA"] = rv(4096, 2048).rearrange("p (t d) -> p t d", t=16)
            sb["OTp"] = [rv(6144 + i * 2048, 2048) for i in range(2)]
            sb["PT"] = [rv(10240 + i * 512, 512) for i in range(4)]
            sb["PT2"] = [rv(8192 + i * 1024, 1024) for i in range(4)]
            sb["cqn"] = rv(12288, 4096).rearrange("p (k t) -> p k t", k=2)
            sb["ckvn"] = rv(20480, 2048)
            sb["KR"] = rv(22528, 2048)
            sb["ropeC"] = rv(24576, 4096, F32)
            sb["ropeS"] = rv(28672, 4096, F32)
            self.ACC, self.SB, self.MISC = (0, 1), (2, 3, 4, 5), (6, 7)
            self.pt_i = 0
            self.ps2 = [st.enter_context(nc.psum_tensor(f"ps{i}", [128, 2 * TC], F32)) for i in range(4)]
            self.ps = [self.ps2[i // 2][:, (i % 2) * TC:(i % 2 + 1) * TC] for i in range(8)]

            xT = sb["xT"]
            xsrc = self.dram["xT"].rearrange("(c p) t -> p c t", p=128)
            for c in range(8):
                s.dma("sp", "xin", (lambda e, c=c: e.dma_start(out=xT[:, c, :], in_=xsrc[:, c, :])),
                      writes=keys("xT", c, range(NTC)))
            s.retag(keys("xT", range(8), range(NTC)), "xin")
            s.dma("sp", "cst", (lambda e: e.dma_start(out=sb["gains"][:, :], in_=self.dram["gains"])), writes=[("gains",)])
            s.dma("pool", "cst2", (lambda e: e.dma_start(out=sb["ident"][:, :], in_=self.dram["ident"])), writes=[("ident",)])
            s.op("dve", (lambda e: e.memset(sb["ones"][:, :], 1.0)), writes=[("ones",)])
            s.op("dve", (lambda e: e.memset(sb["eps"][:, :], EPS)), writes=[("eps",)])
            s.barrier()
            for e in ("pe", "act", "dve", "pool"):
                for cs_ in ("cst", "cst2"):
                    s.ops[e].append(([(cs_, s.dma_count[cs_])], None, None, "init"))
                    s.known[e][cs_] = s.dma_count[cs_]

            for li in self.layers:
                if "mix" in self.parts:
                    s.phase = f"mix{li}"
                    self.mixer(li)
                if "mlp" in self.parts:
                    s.phase = f"mlp{li}"
                    self.mlp(li)
                if "ple" in self.parts:
                    s.phase = f"ple{li}"
                    self.ple(li)
            s.phase = "final"
            if self.do_final:
                self.final_norm()
            ydst = self.out.rearrange("(c p) t -> p c t", p=128)
            for c in range(8):
                s.dma("sp", "yout", (lambda e, c=c: e.dma_start(out=ydst[:, c, :], in_=xT[:, c, :])),
                      reads=keys("xT", c, range(NTC)))
            s.final_wait("sp", ["yout"] + (["dbg"] if self.dbg_names else []))

            self.emit(st)
        return nc

    def emit(self, st):
        nc, s = self.nc, self.s
        semnames = list(Sched.ENGS) + sorted(s.dma_count.keys())
        sems = {n: st.enter_context(nc.semaphore(f"sem_{n}")) for n in semnames}
        block = st.enter_context(nc.Block())

        def run(eng_name):
            def body(e):
                for waits, fn, inc, phase in s.ops[eng_name]:
                    for sk, val in waits:
                        e.wait_ge(sems[sk], val)
                    if fn is None:
                        continue
                    if self.scopes:
                        with nc.named_scope(phase):
                            ins = fn(e)
                    else:
                        ins = fn(e)
                    ins.then_inc(sems[inc[0]], inc[1])
            return body

        block.tensor(run("pe"))
        block.scalar(run("act"))
        block.vector(run("dve"))
        block.gpsimd(run("pool"))
        block.sync(run("sp"))


def rope_np(rot_dim):
    inv = np.power(np.float32(500000.0), -(np.arange(0, rot_dim, 2, dtype=np.float32) / np.float32(rot_dim))).astype(np.float32)
    ang = (np.arange(S, dtype=np.float32)[:, None] * inv[None, :]).astype(np.float32)
    return np.cos(ang).astype(np.float32), np.sin(ang).astype(np.float32)


def make_consts():
    d = {"ident": np.eye(128, dtype=np.float32)}
    c, sn = rope_np(32)
    C = np.ones((128, S), np.float32)
    Sg = np.zeros((128, S), np.float32)
    C[64:80] = c.T; C[80:96] = c.T
    Sg[64:80] = -sn.T; Sg[80:96] = sn.T
    d["ropeL"] = np.stack([C, Sg])
    c, sn = rope_np(16)
    C = np.ones((128, S), np.float32)
    Sg = np.zeros((128, S), np.float32)
    for o in (0, 64):
        C[o:o + 8] = c.T; C[o + 8:o + 16] = c.T
        Sg[o:o + 8] = -sn.T; Sg[o + 8:o + 16] = sn.T
    d["ropeP"] = np.stack([C, Sg])
    return d


def make_na_bias(rpb):
    out = np.full((16, 128, 21, 128), NEG, np.float32)
    pairs = [(5, 5 + d, d + 2) for d in range(-2, 3)]
    for t in (0, 1, 14, 15):
        pairs += [(t, kt, blk) for (kt, blk) in Builder.na_tiles(t)]
    qq = np.arange(128)
    kk = np.arange(128)
    for (t, kt, blk) in pairs:
        r = 2 * t + qq // 64
        c = qq % 64
        kr = 2 * kt + kk // 64
        kc = kk % 64
        rs = np.clip(r - 4, 0, 24)
        w0 = np.clip(c - 8, 0, 48)
        valid = ((kr[:, None] >= rs[None, :]) & (kr[:, None] < rs[None, :] + 8)
                 & (kc[:, None] >= w0[None, :]) & (kc[:, None] < w0[None, :] + 16))
        ro = np.clip(kr[:, None] - r[None, :] + 7, 0, 14)
        co = np.clip(kc[:, None] - c[None, :] + 15, 0, 30)
        g = rpb[:, ro, co]
        out[:, :, blk, :] = np.where(valid[None], g, np.float32(NEG))
    return out


def chunked(v):
    return np.ascontiguousarray(v.reshape(8, 128).T)


def make_gains(inp):
    g = np.zeros((128, NGC), np.float32)
    mixn = [inp["a_norm"][0], inp["b_norm"][0], inp["c_norm"][0], inp["d_norm"][0]]
    for i in range(4):
        g[:, (G_MIX + i) * 8:(G_MIX + i + 1) * 8] = chunked(mixn[i])
        g[:, (G_MLP + i) * 8:(G_MLP + i + 1) * 8] = chunked(inp["mlp_norm"][i])
        g[:, (G_PLE + i) * 8:(G_PLE + i + 1) * 8] = chunked(inp["ple_norm"][i])
    g[:, G_FINAL * 8:(G_FINAL + 1) * 8] = chunked(inp["final_norm"])
    g[:, NG * 8:NG * 8 + 2] = inp["b_q_norm"][0].reshape(2, 128).T
    g[:, NG * 8 + 2] = inp["b_kv_norm"][0]
    g[:, NG * 8 + 3] = inp["d_subln"][0]
    return g


def shared_inputs(inp, layers=(0, 1, 2, 3), parts=("mix", "mlp", "ple")):
    d = make_consts()
    d["gains"] = make_gains(inp)
    for k in ("w_up", "w_down", "w_ple_gate", "w_ple_proj"):
        d[k] = np.ascontiguousarray(inp[k], dtype=np.float32)
    if 1 in layers and "mix" in parts:
        w_in = inp["b_w_in"][0]
        perm32 = np.concatenate([np.arange(16, 32), np.arange(0, 16)])
        kr = w_in[:, 384:416]
        d["b_w_in"] = np.ascontiguousarray(w_in)
        d["b_w_kr"] = np.ascontiguousarray(np.concatenate([w_in[:, 0:64], kr, w_in[:, 0:64], kr[:, perm32]], axis=1))
        wq = inp["b_w_uq"][0]
        idx = np.arange(1536).reshape(16, 96).copy()
        idx[:, 64:96] = idx[:, 64:96][:, perm32]
        d["b_w_uq"] = np.ascontiguousarray(wq)
        d["b_w_uq_p"] = np.ascontiguousarray(wq[:, idx.reshape(-1)])
        d["b_w_ukv"] = np.ascontiguousarray(inp["b_w_ukv"][0])
        d["b_w_o"] = np.ascontiguousarray(inp["b_w_o"][0])
    if 0 in layers and "mix" in parts:
        w = inp["a_w_qkv"][0]
        perm16 = np.concatenate([np.arange(8, 16), np.arange(0, 8), np.arange(16, 64)])
        wh = np.zeros((15, D, 320), np.float32)
        for h in range(15):
            q = w[:, h * 64:(h + 1) * 64]
            k = w[:, 960 + h * 64:960 + (h + 1) * 64]
            v = w[:, 1920 + h * 64:1920 + (h + 1) * 64]
            wh[h] = np.concatenate([q, k, q[:, perm16], k[:, perm16], v], axis=1)
        d["a_w_h"] = wh
        d["a_w_o"] = np.ascontiguousarray(inp["a_w_o"][0])
        kk = np.arange(128)[:, None]
        qq = np.arange(128)[None, :]
        mA = np.where(kk >= qq, 0.0, NEG)
        mB = np.where(kk <= qq, 0.0, NEG)
        mAe = np.full((128, 128), NEG)
        mAe[0:64] = mA[64:128]
        d["dil_masks"] = np.ascontiguousarray(np.stack([mA, mB, mAe], axis=1).astype(np.float32))
    if 2 in layers and "mix" in parts:
        w = inp["c_w_qkv"][0]
        wa = np.zeros((8, D, 384), np.float32)
        for c in range(8):
            wa[c] = np.concatenate([w[:, c * 128:(c + 1) * 128], w[:, 1024 + c * 128:1024 + (c + 1) * 128],
                                    w[:, 2048 + c * 128:2048 + (c + 1) * 128]], axis=1)
        d["c_w_a"] = wa
        d["c_w_o"] = np.ascontiguousarray(inp["c_w_o"][0])
        d["na_bias"] = make_na_bias(inp["c_rpb"][0])
    if 3 in layers and "mix" in parts:
        w = inp["d_w_qkv"][0]
        perm16 = np.concatenate([np.arange(8, 16), np.arange(0, 8), np.arange(16, 64)])
        wa = np.zeros((8, D, 384), np.float32)
        wb = np.zeros((8, D, 256), np.float32)
        for h in range(8):
            q = w[:, h * 128:(h + 1) * 128]
            k = w[:, 1024 + h * 128:1024 + (h + 1) * 128]
            v = w[:, 2048 + h * 128:2048 + (h + 1) * 128]
            p2 = np.concatenate([perm16, 64 + perm16])
            wa[h] = np.concatenate([q, k, v], axis=1)
            wb[h] = np.concatenate([q[:, p2], k[:, p2]], axis=1)
        d["d_w_a"], d["d_w_b"] = wa, wb
        d["d_w_o"] = np.ascontiguousarray(inp["d_w_o"][0])
        lv = np.concatenate([inp["d_lambda_q1"][0], inp["d_lambda_k1"][0], inp["d_lambda_q2"][0], inp["d_lambda_k2"][0]])
        d["d_lamv"] = np.ascontiguousarray(np.tile(lv[None, :], (128, 1)).astype(np.float32))
    return d


def core_inputs(inp, b, x_override=None):
    x = inp["x"][b] if x_override is None else x_override
    return {
        "xT": np.ascontiguousarray(x.T, dtype=np.float32),
        "pT": np.ascontiguousarray(np.transpose(inp["p"][:, b], (0, 2, 1)), dtype=np.float32),
    }


def kernel(**inp):
    bld = Builder(layers=[0, 1, 2, 3], do_final=True)
    nc = bld.build()
    shared = shared_inputs(inp)
    in_maps = [dict(shared, **core_inputs(inp, b)) for b in range(NCORES)]
    res = run_bass_kernel_spmd(nc, in_maps, core_ids=list(range(NCORES)))
    out = np.stack([np.ascontiguousarray(r["yT"].T) for r in res.results], axis=0)
    return out.astype(np.float32)
```

```python
import math
import numpy as np
import ml_dtypes
import concourse.bass as bass
import concourse.mybir as mybir
from concourse.bass_utils import run_bass_kernel_spmd

F32 = mybir.dt.float32
BF16 = mybir.dt.bfloat16
AF = mybir.ActivationFunctionType
ALU = mybir.AluOpType

S = 2048
D = 1024
NCORES = 8
TC = 512
NTC = S // TC
EPS = 1e-6
NEG = -30000.0


class Sched:
    ENGS = ("pe", "act", "dve", "pool", "sp")

    def __init__(self):
        self.ops = {e: [] for e in self.ENGS}
        self.count = {e: 0 for e in self.ENGS}
        self.known = {e: {} for e in self.ENGS}
        self.lastw = {}
        self.readers = {}
        self.dma_count = {}
        self.phase = "init"

    def _deps(self, eng, reads, writes, is_dma):
        deps = set()
        for k in reads:
            t = self.lastw.get(k)
            if t is not None:
                deps.add(t)
        for k in writes:
            t = self.lastw.get(k)
            if t is not None:
                deps.add(t)
            for r in self.readers.get(k, ()):
                deps.add(r)
        need = {}
        for (sk, val, e) in deps:
            if e == eng and not is_dma and eng == "pe":
                continue
            if self.known[eng].get(sk, 0) >= val:
                continue
            if need.get(sk, 0) < val:
                need[sk] = val
        for sk, val in need.items():
            self.known[eng][sk] = val
        return list(need.items())

    def _commit(self, tok, reads, writes):
        for k in writes:
            self.lastw[k] = tok
            self.readers[k] = []
        for k in reads:
            if k in writes:
                continue
            self.readers.setdefault(k, []).append(tok)

    def op(self, eng, fn, reads=(), writes=(), strict=False):
        waits = self._deps(eng, reads, writes, strict)
        self.count[eng] += 1
        tok = (eng, self.count[eng], eng)
        self._commit(tok, reads, writes)
        self.ops[eng].append((waits, fn, (eng, 1), self.phase))

    def dma(self, eng, sem, fn, reads=(), writes=()):
        waits = self._deps(eng, reads, writes, True)
        self.dma_count[sem] = self.dma_count.get(sem, 0) + 16
        tok = (sem, self.dma_count[sem], None)
        self._commit(tok, reads, writes)
        self.ops[eng].append((waits, fn, (sem, 16), self.phase))
        return tok

    def retag(self, keys, sem):
        tok = (sem, self.dma_count[sem], None)
        for k in keys:
            self.lastw[k] = tok

    def barrier(self):
        cur = dict(self.count)
        for e in self.ENGS:
            waits = []
            for f in self.ENGS:
                if f == e or cur[f] == 0:
                    continue
                if self.known[e].get(f, 0) < cur[f]:
                    self.known[e][f] = cur[f]
                    waits.append((f, cur[f]))
            if waits:
                self.ops[e].append((waits, None, None, self.phase))

    def final_wait(self, eng, sems):
        waits = [(s, self.dma_count[s]) for s in sems]
        self.ops[eng].append((waits, None, None, self.phase))


def keys(name, *idx):
    out = [(name,)]
    for ix in idx:
        if isinstance(ix, int):
            ix = (ix,)
        out = [o + (i,) for o in out for i in ix]
    return out


LAMBDA_INIT = [0.8 - 0.6 * math.exp(-0.3 * i) for i in range(4)]

G_MIX, G_MLP, G_PLE, G_FINAL = 0, 4, 8, 12
NG = 13
NGC = NG * 8 + 8


class Builder:
    def __init__(self, layers, do_final, parts=("mix", "mlp", "ple")):
        self.layers = list(layers)
        self.do_final = do_final
        self.parts = parts
        self.nc = bass.Bass("TRN2", target_bir_lowering=False)
        self.s = Sched()
        self.dram = {}
        self.sb = {}
        self.ps = []
        self.psi = 0
        self.slot_i = 0
        self.fs_i = 0
        self.bank_rot = {}
        self.ring_i = 0
        self.debug = False
        self.scopes = False
        self.dbg_names = []

    def din(self, name, shape, dt=F32):
        t = self.nc.dram_tensor(name, list(shape), dt, kind="ExternalInput").ap()
        self.dram[name] = t
        return t

    def declare_io(self):
        self.din("xT", [D, S])
        self.din("pT", [4, 256, S])
        self.din("gains", [128, NGC])
        self.din("ident", [128, 128])
        self.din("w_up", [4, D, 4 * D])
        self.din("w_down", [4, 4 * D, D])
        self.din("w_ple_gate", [4, D, D])
        self.din("w_ple_proj", [4, 256, D])
        self.din("ropeL", [2, 128, S])
        self.din("ropeP", [2, 128, S])
        if 1 in self.layers and "mix" in self.parts:
            self.din("b_w_in", [D, 416])
            self.din("b_w_kr", [D, 192])
            self.din("b_w_uq", [256, 1536])
            self.din("b_w_uq_p", [256, 1536])
            self.din("b_w_ukv", [128, 2048])
            self.din("b_w_o", [D, D])
        if 3 in self.layers and "mix" in self.parts:
            self.din("d_w_a", [8, D, 384])
            self.din("d_w_b", [8, D, 256])
            self.din("d_w_o", [D, D])
            self.din("d_lamv", [128, 256])
        if 2 in self.layers and "mix" in self.parts:
            self.din("c_w_a", [8, D, 384])
            self.din("c_w_o", [D, D])
            self.din("na_bias", [16, 128, 21, 128])
        if 0 in self.layers and "mix" in self.parts:
            self.din("a_w_h", [15, D, 320])
            self.din("a_w_o", [960, D])
            self.din("dil_masks", [128, 3, 128])
        self.out = self.nc.dram_tensor("yT", [D, S], F32, kind="ExternalOutput").ap()

    def bank(self, group=None):
        if group is None:
            b = self.psi
            self.psi = (self.psi + 1) % 8
            return b
        i = self.bank_rot.get(group, 0)
        self.bank_rot[group] = (i + 1) % len(group)
        return group[i]

    def fscr(self):
        i = self.fs_i
        self.fs_i = (self.fs_i + 1) % self.NFS
        return i

    def load_slot(self, src_fn, kc, ncols, parts=128, si=None):
        if si is None:
            si = self.slot_i
            self.slot_i = (self.slot_i + 1) % self.NSLOT
        slot = self.sb["wslot"][si]
        view = slot[0:parts, 0:kc * ncols].rearrange("p (k n) -> p k n", k=kc)
        sem = f"w{si}"
        for k in range(kc):
            src = src_fn(k)
            self.s.dma("pool", sem,
                       (lambda e, o=view[:, k, :], i=src: e.dma_start(out=o, in_=i)),
                       writes=[("wslot", si)])
        self.s.retag([("wslot", si)], sem)
        return si, view


    def mm(self, out, lhsT, rhs, start, stop, r, w):
        self.s.op("pe", (lambda e: e.matmul(out, lhsT, rhs, start=start, stop=stop)), reads=r, writes=w)

    def act(self, out, in_, func, r, w, **kw):
        self.s.op("act", (lambda e: e.activation(out=out, in_=in_, func=func, **kw)), reads=r, writes=w)

    def tt(self, eng, out, in0, in1, op, r, w, strict=False):
        self.s.op(eng, (lambda e: e.tensor_tensor(out=out, in0=in0, in1=in1, op=op)), reads=r, writes=w, strict=strict)

    def stt(self, eng, out, in0, scalar, in1, op0, op1, r, w):
        self.s.op(eng, (lambda e: e.scalar_tensor_tensor(out=out, in0=in0, scalar=scalar, in1=in1, op0=op0, op1=op1)),
                  reads=r, writes=w)

    def tsc(self, eng, out, in0, s1, op0, r, w, s2=None, op1=None):
        if op1 is None:
            self.s.op(eng, (lambda e: e.tensor_scalar(out=out, in0=in0, scalar1=s1, scalar2=None, op0=op0)), reads=r, writes=w)
        else:
            self.s.op(eng, (lambda e: e.tensor_scalar(out=out, in0=in0, scalar1=s1, scalar2=s2, op0=op0, op1=op1)), reads=r, writes=w)

    def cp(self, eng, out, in_, r, w):
        self.s.op(eng, (lambda e: e.tensor_copy(out=out, in_=in_)), reads=r, writes=w)

    def recip(self, out, in_, r, w):
        self.s.op("dve", (lambda e: e.reciprocal(out=out, in_=in_)), reads=r, writes=w)

    def memset(self, eng, ap, val, w):
        self.s.op(eng, (lambda e: e.memset(ap, val)), writes=w)

    def rview(self, off, n, dt=BF16):
        v = self.sb["R"][:, off:off + n]
        return v.bitcast(F32) if dt == F32 else v

    def load_slot3(self, src3, kc, ncols, nsplit=2):
        si = self.slot_i
        self.slot_i = (self.slot_i + 1) % self.NSLOT
        slot = self.sb["wslot"][si]
        view = slot[:, 0:kc * ncols].rearrange("p (k n) -> p k n", k=kc)
        sem = f"w{si}"
        step = kc // nsplit
        for k0 in range(0, kc, step):
            self.s.dma("pool", sem,
                       (lambda e, o=view[:, k0:k0 + step, :], i=src3[:, k0:k0 + step, :]: e.dma_start(out=o, in_=i)),
                       writes=[("wslot", si)])
        self.s.retag([("wslot", si)], sem)
        return si, view

    def load_rope(self, name):
        sb, s = self.sb, self.s
        for i, key in enumerate(("ropeC", "ropeS")):
            s.dma("sp", "rope", (lambda e, i=i, key=key: e.dma_start(out=sb[key][:, :], in_=self.dram[name][i])),
                  writes=[(key,)])
        s.retag([("ropeC",), ("ropeS",)], "rope")

    def out_proj_pair(self, wo_view, wi, si, otp, otp_key, kparts=128):
        sb = self.sb
        xT = sb["xT"]
        for tc in range(NTC):
            for oc in range(8):
                tsl = slice(tc * TC, (tc + 1) * TC)
                b = self.bank()
                pb = self.ps[b]
                self.mm(pb[:, :], wo_view[0:kparts, wi, oc * 128:(oc + 1) * 128], otp[0:kparts, tsl], True, True,
                        r=[("wslot", si)] + keys(otp_key, tc), w=keys("ps", b))
                self.tt("dve", xT[:, oc, tsl], pb[:, :], xT[:, oc, tsl], ALU.add,
                        r=keys("ps", b) + keys("xT", oc, tc), w=keys("xT", oc, tc))

    LA = 3

    def flush_deferred(self):
        for (_, fn) in self.deferred:
            fn()
        self.deferred = []

    def run_pipeline(self, n, front, back, la=None, flush=True):
        la = self.LA if la is None else la
        self.deferred = []
        for i in range(n + la):
            if i < n:
                front(i)
            if i >= la:
                back(i - la)
                keep = []
                for (due, fn) in self.deferred:
                    if due <= i - la:
                        fn()
                    else:
                        keep.append((due, fn))
                self.deferred = keep
        if flush:
            self.flush_deferred()

    def attn_dense_head(self, KT, QT, kparts, VA, acc_parts_v, emit_norm, nkt=16):
        sb = self.sb
        PT2 = sb["PT2"]
        npair = nkt // 2
        n = NTC * npair
        st = {}
        accb = [self.bank(self.ACC) for _ in range(NTC)]
        RING = (1, 2, 3)

        def front(i):
            qc, kp = divmod(i, npair)
            qsl = slice(qc * TC, (qc + 1) * TC)
            pb = RING[self.ring_i % len(RING)]
            self.ring_i += 1
            for half in range(2):
                kt = 2 * kp + half
                self.mm(self.ps[2 * pb + half][:, :], KT[0:kparts, kt * 128:(kt + 1) * 128], QT[0:kparts, qsl], True, True,
                        r=keys("KT", kt // 4) + keys("QT", qc), w=keys("ps", 2 * pb + half))
            pi = self.pt_i
            self.pt_i = (self.pt_i + 1) % 4
            self.act(PT2[pi][:, :], self.ps2[pb][:, :], AF.Exp, r=keys("ps", 2 * pb) + keys("ps", 2 * pb + 1), w=keys("PT2", pi))
            st[i] = pi

        def back(i):
            qc, kp = divmod(i, npair)
            ba = accb[qc]
            pi = st.pop(i)
            for half in range(2):
                kt = 2 * kp + half
                self.mm(self.ps[ba][:, :], VA[:, kt, :], PT2[pi][:, half * TC:(half + 1) * TC], kt == 0, kt == nkt - 1,
                        r=keys("VA", kt // 8) + keys("PT2", pi), w=keys("ps", ba))
            if kp == npair - 1:
                emit_norm(qc, ba)
        self.run_pipeline(n, front, back, la=2)

    def mixer_mla(self, li):
        s, sb, dram = self.s, self.sb, self.dram
        xT, hT, sq, ones, G = sb["xT"], sb["hT"], sb["sq"], sb["ones"], sb["gains"]
        QT, KT, VA, OTp, cqn, ckvn, KR = sb["QT"], sb["KT"], sb["VA"], sb["OTp"], sb["cqn"], sb["ckvn"], sb["KR"]
        C, Sn = sb["ropeC"], sb["ropeS"]
        scale = float((64 + 32) ** -0.5)
        self.load_rope("ropeL")
        self.memset("pool", VA[:, :, 64:128], 1.0, keys("VA", range(2)))
        self.memset("pool", QT[96:128, :], 0.0, keys("QT", range(NTC)))
        self.memset("pool", KT[96:128, :], 0.0, keys("KT", range(NTC)))
        self.rmsnorm(G_MIX + li)
        w_in, w_kr = dram["b_w_in"], dram["b_w_kr"]
        ai, av = self.load_slot(lambda k: w_in[k * 128:(k + 1) * 128, 0:384], 8, 384, si=0)
        bi, bv = self.load_slot(lambda k: w_kr[k * 128:(k + 1) * 128, :], 8, 192, si=1)
        ui, uv = self.load_slot(lambda k: dram["b_w_uq"][k * 128:(k + 1) * 128, :], 2, 1536, si=2)
        upi, upv = self.load_slot(lambda k: dram["b_w_uq_p"][k * 128:(k + 1) * 128, :], 2, 1536, si=3)
        for tc in range(NTC):
            tsl = slice(tc * TC, (tc + 1) * TC)
            zb = []
            for (view, vi, c0, m) in ((av, ai, 0, 128), (av, ai, 128, 128), (av, ai, 256, 128), (bv, bi, 0, 96), (bv, bi, 96, 96)):
                b = self.bank()
                zb.append(b)
                for k in range(8):
                    self.mm(self.ps[b][0:m, :], view[:, k, c0:c0 + m], hT[:, k, tsl], k == 0, k == 7,
                            r=keys("hT", k, tc) + [("wslot", vi)], w=keys("ps", b))
            for (chunks, nfeat, gcol, dst, dkey) in (((0, 1), 256, NG * 8, cqn, "cqn"), ((2,), 128, NG * 8 + 2, ckvn, "ckvn")):
                for j, zi in enumerate(chunks):
                    self.act(sq[:, j, :], self.ps[zb[zi]][:, :], AF.Square, r=keys("ps", zb[zi]), w=keys("sq", j))
                bn = self.bank()
                for j in range(len(chunks)):
                    self.mm(self.ps[bn][:, :], ones[:, :], sq[:, j, :], j == 0, j == len(chunks) - 1,
                            r=keys("sq", j), w=keys("ps", bn))
                fi = self.fscr()
                fs = sb["fscr"][fi]
                self.act(fs[:, :], self.ps[bn][:, :], AF.Ln, r=keys("ps", bn), w=keys("fscr", fi),
                         scale=1.0 / nfeat, bias=sb["eps"][:, 0:1])
                self.act(fs[:, :], fs[:, :], AF.Exp, r=keys("fscr", fi), w=keys("fscr", fi), scale=-0.5)
                for j, zi in enumerate(chunks):
                    o = dst[:, j, tsl] if len(chunks) > 1 else dst[:, tsl]
                    self.stt("dve", o, self.ps[zb[zi]][:, :], G[:, gcol + j:gcol + j + 1], fs[:, :], ALU.mult, ALU.mult,
                             r=keys("ps", zb[zi]) + keys("fscr", fi), w=keys(dkey, tc))
            f1i, f2i = self.fscr(), self.fscr()
            f1, f2 = sb["fscr"][f1i], sb["fscr"][f2i]
            self.tt("dve", f1[64:96, :], self.ps[zb[3]][64:96, :], C[64:96, tsl], ALU.mult,
                    r=keys("ps", zb[3]) + [("ropeC",)], w=keys("fscr", f1i))
            self.tt("dve", f2[64:96, :], self.ps[zb[4]][64:96, :], Sn[64:96, tsl], ALU.mult,
                    r=keys("ps", zb[4]) + [("ropeS",)], w=keys("fscr", f2i))
            self.tt("pool", KT[64:96, tsl], f1[64:96, :], f2[64:96, :], ALU.add,
                    r=keys("fscr", f1i) + keys("fscr", f2i), w=keys("KT", tc))
        ki, kv = self.load_slot(lambda k: dram["b_w_ukv"][:, :], 1, 2048, si=0)
        woi = 1
        wov = None
        def proj(h):
            for tc in range(NTC):
                tsl = slice(tc * TC, (tc + 1) * TC)
                bq, bqp = self.bank((2, 3, 4, 5, 6, 7)), self.bank((2, 3, 4, 5, 6, 7))
                for (b, view, vi) in ((bq, uv, ui), (bqp, upv, upi)):
                    for k in range(2):
                        self.mm(self.ps[b][0:96, :], view[:, k, h * 96:(h + 1) * 96], cqn[:, k, tsl], k == 0, k == 1,
                                r=keys("cqn", tc) + [("wslot", vi)], w=keys("ps", b))
                self.tsc("dve", QT[0:64, tsl], self.ps[bq][0:64, :], scale, ALU.mult, r=keys("ps", bq), w=keys("QT", tc))
                f1i, f2i = self.fscr(), self.fscr()
                f1, f2 = sb["fscr"][f1i], sb["fscr"][f2i]
                self.stt("dve", f1[64:96, :], self.ps[bq][64:96, :], scale, C[64:96, tsl], ALU.mult, ALU.mult,
                         r=keys("ps", bq) + [("ropeC",)], w=keys("fscr", f1i))
                self.stt("dve", f2[64:96, :], self.ps[bqp][64:96, :], scale, Sn[64:96, tsl], ALU.mult, ALU.mult,
                         r=keys("ps", bqp) + [("ropeS",)], w=keys("fscr", f2i))
                self.tt("pool", QT[64:96, tsl], f1[64:96, :], f2[64:96, :], ALU.add,
                        r=keys("fscr", f1i) + keys("fscr", f2i), w=keys("QT", tc))
                bk = self.bank((2, 3, 4, 5, 6, 7))
                self.mm(self.ps[bk][0:64, :], kv[:, 0, h * 128:h * 128 + 64], ckvn[:, tsl], True, True,
                        r=keys("ckvn", tc) + [("wslot", ki)], w=keys("ps", bk))
                self.cp("dve", KT[0:64, tsl], self.ps[bk][0:64, :], r=keys("ps", bk), w=keys("KT", tc))
            for half in range(2):
                bvb = self.bank((2, 3, 4, 5, 6, 7))
                for j in range(8):
                    tt_ = half * 8 + j
                    self.mm(self.ps[bvb][:, j * 64:(j + 1) * 64], ckvn[:, tt_ * 128:(tt_ + 1) * 128], kv[:, 0, h * 128 + 64:h * 128 + 128],
                            True, True, r=keys("ckvn", tt_ // 4) + [("wslot", ki)], w=keys("ps", bvb))
                self.cp("dve", VA[:, half * 8:(half + 1) * 8, 0:64], self.ps[bvb][:, :].rearrange("p (t d) -> p t d", t=8),
                        r=keys("ps", bvb), w=keys("VA", half))
        proj(0)
        for h in range(16):
            if h % 8 == 0:
                half = h // 8
                _, wov = self.load_slot(lambda k, half=half: dram["b_w_o"][half * 512 + k * 128:half * 512 + (k + 1) * 128, :], 4, 1024, si=woi)
            par = 0
            ot = OTp[par]
            okey = f"OTp{par}"

            def norm(qc, ba, h=h, ot=ot, okey=okey):
                qsl = slice(qc * TC, (qc + 1) * TC)
                fi = self.fscr()
                fs = sb["fscr"][fi]
                self.recip(fs[64:128, :], self.ps[ba][64:128, :], r=keys("ps", ba), w=keys("fscr", fi))
                r0 = (h % 2) * 64
                self.tt("dve", ot[r0:r0 + 64, qsl], self.ps[ba][0:64, :], fs[64:128, :], ALU.mult,
                        r=keys("ps", ba) + keys("fscr", fi), w=keys(okey, qc))
            self.attn_dense_head(KT, QT, 128, VA, 64, norm)
            if h + 1 < 16:
                proj(h + 1)
                if h + 1 == 15:
                    self.mlp_prefetch(li, 2)
            if h % 2 == 1:
                self.out_proj_pair(wov, (h // 2) % 4, woi, ot, okey)


    def dbg(self, name, ap, rkeys):
        if not getattr(self, "debug", False):
            return
        parts, n = ap.shape
        t = self.nc.dram_tensor("dbg_" + name, [parts, n], F32, kind="ExternalOutput").ap()
        self.s.dma("pool", "dbg", (lambda e: e.dma_start(out=t, in_=ap)), reads=rkeys)
        self.dbg_names.append(name)

    def fill_slot(self, si, pieces):
        slot = self.sb["wslot"][si]
        sem = f"w{si}"
        for pc in pieces:
            off, src = pc[0], pc[1]
            p0 = pc[2] if len(pc) > 2 else 0
            parts, n = src.shape
            self.s.dma("pool", sem, (lambda e, o=slot[p0:p0 + parts, off:off + n], i=src: e.dma_start(out=o, in_=i)),
                       writes=[("wslot", si)])
        self.s.retag([("wslot", si)], sem)
        return slot

    def rope_evac(self, dst, psq, psqp, bq, bqp, tsl, scale, dkeys, rows=slice(0, 128)):
        sb = self.sb
        C, Sn = sb["ropeC"], sb["ropeS"]
        f1i, f2i = self.fscr(), self.fscr()
        f1, f2 = sb["fscr"][f1i], sb["fscr"][f2i]
        self.stt("dve", f1[rows, :], psq[rows, :], scale, C[rows, tsl], ALU.mult, ALU.mult,
                 r=keys("ps", bq) + [("ropeC",)], w=keys("fscr", f1i))
        self.stt("dve", f2[rows, :], psqp[rows, :], scale, Sn[rows, tsl], ALU.mult, ALU.mult,
                 r=keys("ps", bqp) + [("ropeS",)], w=keys("fscr", f2i))
        self.tt("pool", dst, f1[rows, :], f2[rows, :], ALU.add,
                r=keys("fscr", f1i) + keys("fscr", f2i), w=dkeys)

    def rope_evac2(self, dstA, dstB, psq, psqp, bq, bqp, tsl, scale, dkeys):
        sb = self.sb
        C, Sn = sb["ropeC"], sb["ropeS"]
        f1i, f2i = self.fscr(), self.fscr()
        f1, f2 = sb["fscr"][f1i], sb["fscr"][f2i]
        self.stt("dve", f1[:, :], psq[:, :], scale, C[:, tsl], ALU.mult, ALU.mult,
                 r=keys("ps", bq) + [("ropeC",)], w=keys("fscr", f1i))
        self.stt("dve", f2[:, :], psqp[:, :], scale, Sn[:, tsl], ALU.mult, ALU.mult,
                 r=keys("ps", bqp) + [("ropeS",)], w=keys("fscr", f2i))
        self.tt("pool", dstA, f1[0:64, :], f2[0:64, :], ALU.add, r=keys("fscr", f1i) + keys("fscr", f2i), w=dkeys)
        self.tt("pool", dstB, f1[64:128, :], f2[64:128, :], ALU.add, r=keys("fscr", f1i) + keys("fscr", f2i), w=dkeys)

    def mixer_diff(self, li):
        s, sb, dram = self.s, self.sb, self.dram
        xT, hT, sq, ones, G = sb["xT"], sb["hT"], sb["sq"], sb["ones"], sb["gains"]
        QT, KT, VA, OTp, PT = sb["QT"], sb["KT"], sb["VA"], sb["OTp"], sb["PT"]
        lam_init = LAMBDA_INIT[li]
        scale = 0.125
        self.load_rope("ropeP")
        QA, QB = QT, self.rview(12288, 2048)
        O1 = [self.rview(o_, 1024, F32) for o_ in (14336, 15360, 20480, 21504)]
        self.memset("pool", QA[64:128, :], 0.0, keys("QT", range(NTC)))
        self.memset("pool", QB[0:64, :], 0.0, keys("QT", range(NTC)))
        lamv, lams = sb["lamv"], sb["lams"]
        s.dma("sp", "lamv", (lambda e: e.dma_start(out=lamv[:, :], in_=dram["d_lamv"])), writes=[("lamv",)])
        fi = self.fscr()
        fs = sb["fscr"][fi]
        for j in range(2):
            self.tt("dve", fs[:, j * 64:(j + 1) * 64], lamv[:, (2 * j) * 64:(2 * j + 1) * 64], lamv[:, (2 * j + 1) * 64:(2 * j + 2) * 64],
                    ALU.mult, r=[("lamv",)], w=keys("fscr", fi), strict=True)
            s.op("dve", (lambda e, j=j: e.reduce_sum(out=lams[:, j:j + 1], in_=fs[:, j * 64:(j + 1) * 64], axis=mybir.AxisListType.X)),
                 reads=keys("fscr", fi), writes=[("lams",)], strict=True)
        self.act(lams[:, 2:4], lams[:, 0:2], AF.Exp, r=[("lams",)], w=[("lams",)])
        s.op("dve", (lambda e: e.memset(lams[:, 6:7], -lam_init)), writes=[("lams",)], strict=True)
        self.tt("dve", lams[:, 4:5], lams[:, 3:4], lams[:, 2:3], ALU.subtract, r=[("lams",)], w=[("lams",)], strict=True)
        self.tt("dve", lams[:, 4:5], lams[:, 4:5], lams[:, 6:7], ALU.add, r=[("lams",)], w=[("lams",)], strict=True)
        s.op("dve", (lambda e: e.tensor_scalar(out=lams[:, 5:6], in0=G[:, NG * 8 + 3:NG * 8 + 4], scalar1=1.0 - lam_init, scalar2=None, op0=ALU.mult)),
             reads=[("gains",), ("lams",)], writes=[("lams",)], strict=True)
        self.rmsnorm(G_MIX + li)
        wa, wb, wo = dram["d_w_a"], dram["d_w_b"], dram["d_w_o"]

        def fetch(h):
            sa, sbi = (0, 1) if h % 2 == 0 else (2, 3)
            A = self.fill_slot(sa, [(k * 384, wa[h, k * 128:(k + 1) * 128, :]) for k in range(8)])
            B = self.fill_slot(sbi, [(k * 256, wb[h, k * 128:(k + 1) * 128, :]) for k in range(8)]
                               + [(2048, wo[h * 128:(h + 1) * 128, :])])
            return (sa, A[:, 0:3072].rearrange("p (k n) -> p k n", k=8), sbi, B[:, 0:2048].rearrange("p (k n) -> p k n", k=8),
                    B[:, 2048:3072].rearrange("p (a n) -> p a n", a=1))
        def proj(sa, A, sbi, B):
            for (dst, dk, c0, c0p) in ((QT, "QT", 0, 0), (KT, "KT", 128, 128)):
                for tc in range(NTC):
                    tsl = slice(tc * TC, (tc + 1) * TC)
                    bq, bqp = self.bank((2, 3, 4, 5)), self.bank((2, 3, 4, 5))
                    for k in range(8):
                        self.mm(self.ps[bq][:, :], A[:, k, c0:c0 + 128], hT[:, k, tsl], k == 0, k == 7,
                                r=keys("hT", k, tc) + [("wslot", sa)], w=keys("ps", bq))
                    for k in range(8):
                        self.mm(self.ps[bqp][:, :], B[:, k, c0p:c0p + 128], hT[:, k, tsl], k == 0, k == 7,
                                r=keys("hT", k, tc) + [("wslot", sbi)], w=keys("ps", bqp))
                    if dk == "KT":
                        self.rope_evac(dst[:, tsl], self.ps[bq], self.ps[bqp], bq, bqp, tsl, 1.0, keys(dk, tc))
                    else:
                        self.rope_evac2(QA[0:64, tsl], QB[64:128, tsl], self.ps[bq], self.ps[bqp], bq, bqp, tsl, scale, keys(dk, tc))
            for t4 in range(4):
                bvb = self.bank((2, 3, 4, 5))
                for j in range(4):
                    tt_ = t4 * 4 + j
                    for k in range(8):
                        self.mm(self.ps[bvb][:, j * 128:(j + 1) * 128], hT[:, k, tt_ * 128:(tt_ + 1) * 128], A[:, k, 256:384], k == 0, k == 7,
                                r=keys("hT", k, tt_ // 4) + [("wslot", sa)], w=keys("ps", bvb))
                self.cp("dve", VA[:, t4 * 4:(t4 + 1) * 4, :], self.ps[bvb][:, :].rearrange("p (t d) -> p t d", t=4),
                        r=keys("ps", bvb), w=keys("VA", t4 // 2))
        cur = fetch(0)
        proj(cur[0], cur[1], cur[2], cur[3])
        for h in range(8):
            sa, A, sbi, B, WO = cur
            nxt = fetch(h + 1) if h + 1 < 8 else None
            if h == 7:
                self.mlp_prefetch(li, 0)
            par = 0
            ot, okey = OTp[par], f"OTp{par}"
            if h == 0:
                self.dbg("QT", QT[:, :], keys("QT", range(4)))
                self.dbg("KT", KT[:, :], keys("KT", range(4)))
                self.dbg("V0", VA[:, 0, :], keys("VA", range(2)))
                self.dbg("lams", lams[:, :], [("lams",)])
            n = NTC * 2 * 8
            st = {}
            o1s = {}
            SETS = ((0, 1), (6, 7))
            RING = (1, 2)
            PT2 = sb["PT2"]

            def front(i, h=h):
                g_, kp = divmod(i, 8)
                qc, mp = divmod(g_, 2)
                qsl = slice(qc * TC, (qc + 1) * TC)
                pb = RING[self.ring_i % 2]
                self.ring_i += 1
                for half in range(2):
                    kt = 2 * kp + half
                    self.mm(self.ps[2 * pb + half][:, :], KT[:, kt * 128:(kt + 1) * 128], (QA if mp == 0 else QB)[:, qsl], True, True,
                            r=keys("KT", kt // 4) + keys("QT", qc), w=keys("ps", 2 * pb + half))
                pi = self.pt_i
                self.pt_i = (self.pt_i + 1) % 4
                self.act(PT2[pi][:, :], self.ps2[pb][:, :], AF.Exp, r=keys("ps", 2 * pb) + keys("ps", 2 * pb + 1), w=keys("PT2", pi))
                st[i] = pi

            def back(i, h=h, ot=ot, okey=okey):
                g_, kp = divmod(i, 8)
                qc, mp = divmod(g_, 2)
                qsl = slice(qc * TC, (qc + 1) * TC)
                ba, bl = SETS[g_ % 2]
                pi = st.pop(i)
                for half in range(2):
                    kt = 2 * kp + half
                    pt = PT2[pi][:, half * TC:(half + 1) * TC]
                    self.mm(self.ps[ba][:, :], VA[:, kt, :], pt, kt == 0, kt == 15,
                            r=keys("VA", kt // 8) + keys("PT2", pi), w=keys("ps", ba))
                    self.mm(self.ps[bl][:, :], ones[:, :], pt, kt == 0, kt == 15,
                            r=keys("PT2", pi), w=keys("ps", bl))
                if kp != 7:
                    return
                ri = self.fscr()
                rr = sb["fscr"][ri]
                self.recip(rr[:, :], self.ps[bl][:, :], r=keys("ps", bl), w=keys("fscr", ri))
                o1 = O1[qc]
                if mp == 0:
                    self.tt("dve", o1[:, :], self.ps[ba][:, :], rr[:, :], ALU.mult,
                            r=keys("ps", ba) + keys("fscr", ri), w=keys("O1", qc))
                    return
                self.tt("dve", rr[:, :], self.ps[ba][:, :], rr[:, :], ALU.mult,
                        r=keys("ps", ba) + keys("fscr", ri), w=keys("fscr", ri))
                self.stt("dve", o1[:, :], rr[:, :], lams[:, 4:5], o1[:, :], ALU.mult, ALU.add,
                         r=keys("fscr", ri) + keys("O1", qc) + [("lams",)], w=keys("O1", qc))
                self.act(sq[:, qc, :], o1[:, :], AF.Square, r=keys("O1", qc), w=keys("sq", qc))

                def tail(qc=qc, o1=o1, qsl=qsl):
                    pbn = RING[self.ring_i % 2]
                    self.ring_i += 1
                    bn = 2 * pbn
                    self.mm(self.ps[bn][:, :], ones[:, :], sq[:, qc, :], True, True, r=keys("sq", qc), w=keys("ps", bn))
                    r2 = self.fscr()
                    r2t = sb["fscr"][r2]
                    self.act(r2t[:, :], self.ps[bn][:, :], AF.Ln, r=keys("ps", bn), w=keys("fscr", r2), scale=1.0 / 128, bias=sb["eps"][:, 0:1])
                    self.act(r2t[:, :], r2t[:, :], AF.Exp, r=keys("fscr", r2), w=keys("fscr", r2), scale=-0.5)
                    self.stt("dve", ot[:, qsl], o1[:, :], lams[:, 5:6], r2t[:, :], ALU.mult, ALU.mult,
                             r=keys("O1", qc) + keys("fscr", r2) + [("lams",)], w=keys(okey, qc))
                self.deferred.append((i + 9, tail))
            self.run_pipeline(n, front, back, la=1, flush=False)
            if h == 0:
                self.dbg("OT", ot[:, :], keys(okey, range(4)))
            if nxt is not None:
                proj(nxt[0], nxt[1], nxt[2], nxt[3])
            self.flush_deferred()
            self.out_proj_pair(WO, 0, sbi, ot, okey)
            cur = nxt


    @staticmethod
    def na_tiles(t):
        if 2 <= t <= 13:
            return [(t + d, d + 2) for d in range(-2, 3)]
        base = {0: 5, 1: 9, 14: 13, 15: 17}[t]
        k0 = 0 if t < 2 else 12
        return [(k0 + j, base + j) for j in range(4)]

    def mixer_na(self, li):
        s, sb, dram = self.s, self.sb, self.dram
        xT, hT, ident = sb["xT"], sb["hT"], sb["ident"]
        QT, KT, OTp, PT = sb["QT"], sb["KT"], sb["OTp"], sb["PT"]
        VA2 = self.rview(12288, 4096).rearrange("p (t h d) -> p t h d", t=16, h=2)
        BIs = [self.rview(o_, 5376).rearrange("p (h b q) -> p h b q", h=2, b=21) for o_ in (24576, 16384)]
        self.memset("pool", VA2[:, :, :, 64:128], 1.0, keys("VA", range(2)))
        QA, QB = QT, self.rview(29952, 2048)
        self.memset("pool", QA[64:128, :], 0.0, keys("QT", range(NTC)))
        self.memset("pool", QB[0:64, :], 0.0, keys("QT", range(NTC)))
        self.rmsnorm(G_MIX + li)
        s.barrier()
        wa, wo, nab = dram["c_w_a"], dram["c_w_o"], dram["na_bias"]

        def fetch(c):
            si = c % 4
            A = self.fill_slot(si, [(k * 384, wa[c, k * 128:(k + 1) * 128, :]) for k in range(8)]
                               + [(3072, wo[c * 128:(c + 1) * 128, :])])
            return si, A[:, 0:3072].rearrange("p (k n) -> p k n", k=8), A[:, 3072:4096].rearrange("p (a n) -> p a n", a=1)
        def load_bias(c):
            for hh in range(2):
                s.dma("pool", f"nab{c % 2}", (lambda e, hh=hh, c=c: e.dma_start(out=BIs[c % 2][:, hh, :, :], in_=nab[2 * c + hh])),
                      writes=[("BI", c % 2)])
            s.retag([("BI", c % 2)], f"nab{c % 2}")

        def proj(si, A):
            for (dst, dk, c0, sc) in ((QT, "QT", 0, 0.125), (KT, "KT", 128, 1.0)):
                for tc in range(NTC):
                    tsl = slice(tc * TC, (tc + 1) * TC)
                    bq = self.bank((2, 3, 4, 5))
                    for k in range(8):
                        self.mm(self.ps[bq][:, :], A[:, k, c0:c0 + 128], hT[:, k, tsl], k == 0, k == 7,
                                r=keys("hT", k, tc) + [("wslot", si)], w=keys("ps", bq))
                    if dk == "KT":
                        self.tsc("dve", dst[:, tsl], self.ps[bq][:, :], sc, ALU.mult, r=keys("ps", bq), w=keys(dk, tc))
                    else:
                        self.tsc("dve", QA[0:64, tsl], self.ps[bq][0:64, :], sc, ALU.mult, r=keys("ps", bq), w=keys(dk, tc))
                        self.tsc("dve", QB[64:128, tsl], self.ps[bq][64:128, :], sc, ALU.mult, r=keys("ps", bq), w=keys(dk, tc))
            for t4 in range(4):
                bvb = self.bank((2, 3, 4, 5))
                for j in range(4):
                    tt_ = t4 * 4 + j
                    for k in range(8):
                        self.mm(self.ps[bvb][:, j * 128:(j + 1) * 128], hT[:, k, tt_ * 128:(tt_ + 1) * 128], A[:, k, 256:384], k == 0, k == 7,
                                r=keys("hT", k, tt_ // 4) + [("wslot", si)], w=keys("ps", bvb))
                self.cp("dve", VA2[:, t4 * 4:(t4 + 1) * 4, :, 0:64], self.ps[bvb][:, :].rearrange("p (t h d) -> p t h d", t=4, h=2),
                        r=keys("ps", bvb), w=keys("VA", t4 // 2))
        wl = {0: fetch(0), 1: fetch(1)}
        load_bias(0)
        proj(wl[0][0], wl[0][1])
        for c in range(8):
            si, A, WO = wl[c]
            if c + 2 < 8:
                wl[c + 2] = fetch(c + 2)
            if c + 1 < 8:
                load_bias(c + 1)
            if c == 7:
                self.mlp_prefetch(li, 0)
            BI = BIs[c % 2]
            par = c % 2
            ot, okey = OTp[par], f"OTp{par}"
            units = [(t, j, kt, blk, len(self.na_tiles(t))) for t in range(16) for j, (kt, blk) in enumerate(self.na_tiles(t))]
            st = {}

            def front(i):
                t, j, kt, blk, nt = units[i]
                q128 = slice(t * 128, (t + 1) * 128)
                bs = self.bank(self.SB)
                for hh in range(2):
                    self.mm(self.ps[bs][:, hh * 128:(hh + 1) * 128], KT[:, kt * 128:(kt + 1) * 128], (QA, QB)[hh][:, q128], True, False,
                            r=keys("KT", kt // 4) + keys("QT", t // 4), w=keys("ps", bs))
                    self.mm(self.ps[bs][:, hh * 128:(hh + 1) * 128], ident[:, :], BI[:, hh, blk, :], False, True,
                            r=[("BI", c % 2), ("ident",)], w=keys("ps", bs))
                pi = self.pt_i
                self.pt_i = (self.pt_i + 1) % 4
                self.act(PT[pi][:, 0:256], self.ps[bs][:, 0:256], AF.Exp, r=keys("ps", bs), w=keys("PT", pi))
                st[i] = pi

            def back(i, ot=ot, okey=okey):
                t, j, kt, blk, nt = units[i]
                q128 = slice(t * 128, (t + 1) * 128)
                bas = (0, 1) if t % 2 == 0 else (6, 7)
                pi = st.pop(i)
                for hh in range(2):
                    self.mm(self.ps[bas[hh]][:, 0:128], VA2[:, kt, hh, :], PT[pi][:, hh * 128:(hh + 1) * 128],
                            j == 0, j == nt - 1, r=keys("VA", kt // 8) + keys("PT", pi), w=keys("ps", bas[hh]))
                if j != nt - 1:
                    return
                fi = self.fscr()
                fs = sb["fscr"][fi]
                for hh in range(2):
                    ba = bas[hh]
                    self.recip(fs[64:128, hh * 128:(hh + 1) * 128], self.ps[ba][64:128, 0:128], r=keys("ps", ba), w=keys("fscr", fi))
                    self.tt("dve", ot[hh * 64:(hh + 1) * 64, q128], self.ps[ba][0:64, 0:128],
                            fs[64:128, hh * 128:(hh + 1) * 128], ALU.mult,
                            r=keys("ps", ba) + keys("fscr", fi), w=keys(okey, t // 4))
            self.run_pipeline(len(units), front, back)
            if c + 1 < 8:
                proj(wl[c + 1][0], wl[c + 1][1])
            self.out_proj_pair(WO, 0, si, ot, okey)


    def mixer_dil(self, li):
        s, sb, dram = self.s, self.sb, self.dram
        xT, hT, ident, PT = sb["xT"], sb["hT"], sb["ident"], sb["PT"]
        C, Sn = sb["ropeC"], sb["ropeS"]
        QTd, KTd = self.rview(0, 2048), self.rview(2048, 2048)
        VAd = self.rview(4096, 4096).rearrange("p (t d) -> p t d", t=32)
        OTa = [self.rview(8192 + p_ * 512, 512) for p_ in range(2)]
        OTb = [self.rview(9216 + p_ * 512, 512) for p_ in range(2)]
        U = [self.rview(12288 + g * 4096, 4096, F32) for g in range(3)]
        MK, scl = sb["dmask"], sb["qkscl"]
        self.load_rope("ropeP")
        s.dma("pool", "dmask", (lambda e: e.dma_start(out=MK[:, :, :], in_=dram["dil_masks"])), writes=[("dmask",)])
        self.memset("dve", scl[0:64, :], 0.125, [("qkscl",)])
        self.memset("dve", scl[64:128, :], 1.0, [("qkscl",)])
        self.rmsnorm(G_MIX + li)
        s.barrier()
        self.memset("pool", VAd[:, :, 64:128], 1.0, [("VAd",)])
        for p_ in range(2):
            self.memset("pool", OTb[p_][64:128, :], 0.0, [("OTb", p_)])
        self.memset("pool", QTd[64:128, :], 0.0, [("QTd",)])
        self.memset("pool", KTd[64:128, :], 0.0, [("KTd",)])
        wh, wo = dram["a_w_h"], dram["a_w_o"]
        DIL = (1, 4, 16)

        def fetch(idx):
            j, g = idx // 3, idx % 3
            h = g * 5 + j
            si = idx % 3
            A = self.fill_slot(si, [(k * 320, wh[h, k * 128:(k + 1) * 128, :]) for k in range(8)])
            return si, A[:, 0:2560].rearrange("p (k n) -> p k n", k=8)
        pend_f = [fetch(0), fetch(1)]
        for j in range(5):
            hrow = lambda g: wo[(g * 5 + j) * 64:(g * 5 + j + 1) * 64, :]
            WOs = self.fill_slot(3, [(0, hrow(0), 0), (0, hrow(1), 64), (1024, hrow(2), 0), (1024, hrow(2), 64)])
            WOv = WOs[:, 0:2048].rearrange("p (a n) -> p a n", a=2)
            for g in range(3):
                idx = j * 3 + g
                dil = DIL[g]
                L = S // dil
                si, A = pend_f.pop(0)
                if idx + 2 < 15:
                    pend_f.append(fetch(idx + 2))
                if idx == 14:
                    self.mlp_prefetch(li, 0)
                for tc in range(NTC):
                    tsl = slice(tc * TC, (tc + 1) * TC)
                    bq, bqp = self.bank(), self.bank()
                    for (b, c0) in ((bq, 0), (bqp, 128)):
                        for k in range(8):
                            self.mm(self.ps[b][:, :], A[:, k, c0:c0 + 128], hT[:, k, tsl], k == 0, k == 7,
                                    r=keys("hT", k, tc) + [("wslot", si)], w=keys("ps", b))
                    f1i, f2i = self.fscr(), self.fscr()
                    f1, f2 = sb["fscr"][f1i], sb["fscr"][f2i]
                    self.stt("dve", f1[:, :], self.ps[bq][:, :], scl[:, 0:1], C[:, tsl], ALU.mult, ALU.mult,
                             r=keys("ps", bq) + [("ropeC",), ("qkscl",)], w=keys("fscr", f1i))
                    self.stt("dve", f2[:, :], self.ps[bqp][:, :], scl[:, 0:1], Sn[:, tsl], ALU.mult, ALU.mult,
                             r=keys("ps", bqp) + [("ropeS",), ("qkscl",)], w=keys("fscr", f2i))
                    n = TC // dil
                    l0 = tc * n
                    for (dst, dk, r0) in ((QTd, "QTd", 0), (KTd, "KTd", 64)):
                        d = dst[0:64, :].rearrange("p (r l) -> p r l", r=dil)[:, :, l0:l0 + n]
                        a0 = f1[r0:r0 + 64, :].rearrange("p (i r) -> p r i", r=dil)
                        a1 = f2[r0:r0 + 64, :].rearrange("p (i r) -> p r i", r=dil)
                        self.tt("pool" if r0 == 0 else "dve", d, a0, a1, ALU.add, r=keys("fscr", f1i) + keys("fscr", f2i), w=[(dk,)])
                nj = L // 128
                vt = {}
                tl = []
                for r in range(dil):
                    for jt in range(nj + 1):
                        p0, m = (0, 64) if jt == 0 else ((jt * 128 - 64, 64) if jt == nj else (jt * 128 - 64, 128))
                        vt[(r, jt)] = (len(tl), m)
                        tl.append((r, p0, m))
                hsub = [hT[:, k, :].rearrange("p (l r) -> p r l", r=dil) for k in range(8)]
                for t0 in range(0, len(tl), 8):
                    grp = tl[t0:t0 + 8]
                    bvb = self.bank()
                    for jj, (r, p0, m) in enumerate(grp):
                        for k in range(8):
                            self.mm(self.ps[bvb][0:m, jj * 64:(jj + 1) * 64], hsub[k][:, r, p0:p0 + m], A[:, k, 256:320], k == 0, k == 7,
                                    r=keys("hT", k, range(NTC)) + [("wslot", si)], w=keys("ps", bvb))
                    ng = len(grp)
                    self.cp("dve", VAd[:, t0:t0 + ng, 0:64], self.ps[bvb][:, 0:ng * 64].rearrange("p (t d) -> p t d", t=ng),
                            r=keys("ps", bvb), w=[("VAd",)])
                Ug = U[g][:, :].rearrange("p (l r) -> p r l", r=dil)
                units = [(r, jq) for r in range(dil) for jq in range(nj)]
                st = {}

                def front(i, L=L, nj=nj, vt=vt):
                    r, jq = units[i]
                    q0 = r * L + jq * 128
                    bs = self.bank(self.SB)
                    wins = []
                    for w_, jt in enumerate((jq, jq + 1)):
                        ti_, m = vt[(r, jt)]
                        if jt == 0:
                            k0_, mk = r * L, MK[0:64, 2, :]
                        elif jt == nj:
                            k0_, mk = r * L + L - 64, MK[0:64, 1, :]
                        else:
                            k0_, mk = r * L + jt * 128 - 64, MK[:, w_, :]
                        cs = slice(w_ * 128, (w_ + 1) * 128)
                        self.mm(self.ps[bs][0:m, cs], KTd[:, k0_:k0_ + m], QTd[:, q0:q0 + 128], True, False,
                                r=[("KTd",), ("QTd",)], w=keys("ps", bs))
                        self.mm(self.ps[bs][0:m, cs], ident[0:m, 0:m], mk, False, True,
                                r=[("dmask",), ("ident",)], w=keys("ps", bs))
                        wins.append((ti_, m, cs))
                    pi = self.pt_i
                    self.pt_i = (self.pt_i + 1) % 4
                    for (ti_, m, cs) in wins:
                        self.act(PT[pi][0:m, cs], self.ps[bs][0:m, cs], AF.Exp, r=keys("ps", bs), w=keys("PT", pi))
                    st[i] = (pi, wins)

                def back(i, Ug=Ug, g=g):
                    r, jq = units[i]
                    pi, wins = st.pop(i)
                    ba = self.bank(self.ACC)
                    for w_, (ti_, m, cs) in enumerate(wins):
                        self.mm(self.ps[ba][:, 0:128], VAd[0:m, ti_, :], PT[pi][0:m, cs], w_ == 0, w_ == 1,
                                r=[("VAd",)] + keys("PT", pi), w=keys("ps", ba))
                    self.act(Ug[:, r, jq * 128:(jq + 1) * 128], self.ps[ba][:, 0:128], AF.Copy, r=keys("ps", ba), w=[("U", g)])
                self.run_pipeline(len(units), front, back)
            fsum = U[0]
            self.tt("pool", U[0][64:128, :], U[0][64:128, :], U[1][64:128, :], ALU.add, r=[("U", 0), ("U", 1)], w=[("U", 0)])
            self.tt("pool", U[0][64:128, :], U[0][64:128, :], U[2][64:128, :], ALU.add, r=[("U", 0), ("U", 2)], w=[("U", 0)])
            for tc in range(NTC):
                tsl = slice(tc * TC, (tc + 1) * TC)
                par = tc % 2
                fi = self.fscr()
                fs = sb["fscr"][fi]
                self.act(fs[0:64, :], U[0][64:128, tsl], AF.Ln, r=[("U", 0)], w=keys("fscr", fi))
                self.act(fs[0:64, :], fs[0:64, :], AF.Exp, r=keys("fscr", fi), w=keys("fscr", fi), scale=-1.0)
                self.tt("dve", OTa[par][0:64, :], U[0][0:64, tsl], fs[0:64, :], ALU.mult,
                        r=[("U", 0)] + keys("fscr", fi), w=[("OTa", par)])
                self.tt("dve", OTa[par][64:128, :], U[1][0:64, tsl], fs[0:64, :], ALU.mult,
                        r=[("U", 1)] + keys("fscr", fi), w=[("OTa", par)])
                self.tt("dve", OTb[par][0:64, :], U[2][0:64, tsl], fs[0:64, :], ALU.mult,
                        r=[("U", 2)] + keys("fscr", fi), w=[("OTb", par)])
                for oc in range(8):
                    b = self.bank()
                    pb = self.ps[b]
                    self.mm(pb[:, :], WOv[:, 0, oc * 128:(oc + 1) * 128], OTa[par][:, :], True, False,
                            r=[("wslot", 3), ("OTa", par)], w=keys("ps", b))
                    self.mm(pb[:, :], WOv[:, 1, oc * 128:(oc + 1) * 128], OTb[par][:, :], False, True,
                            r=[("wslot", 3), ("OTb", par)], w=keys("ps", b))
                    self.tt("dve", xT[:, oc, tsl], pb[:, :], xT[:, oc, tsl], ALU.add,
                            r=keys("ps", b) + keys("xT", oc, tc), w=keys("xT", oc, tc))

    def mixer(self, li):
        self.s.barrier()
        getattr(self, ["mixer_dil", "mixer_mla", "mixer_na", "mixer_diff"][li])(li)
        self.s.barrier()

    def rmsnorm(self, gcol):
        s, sb = self.s, self.sb
        xT, hT, sq, ones, G = sb["xT"], sb["hT"], sb["sq"], sb["ones"], sb["gains"]
        for tc in range(NTC):
            tsl = slice(tc * TC, (tc + 1) * TC)
            for c in range(8):
                s.op("act", (lambda e, c=c, tsl=tsl: e.activation(out=sq[:, c, :], in_=xT[:, c, tsl], func=AF.Square)),
                     reads=keys("xT", c, tc), writes=keys("sq", c))
            b = self.bank()
            pb = self.ps[b]
            for c in range(8):
                s.op("pe", (lambda e, c=c, pb=pb: e.matmul(pb[:, :], ones[:, :], sq[:, c, :], start=(c == 0), stop=(c == 7))),
                     reads=keys("sq", c), writes=keys("ps", b))
            fi = self.fscr()
            fs = sb["fscr"][fi]
            s.op("act", (lambda e, pb=pb, fs=fs: e.activation(out=fs[:, :], in_=pb[:, :], func=AF.Ln, scale=1.0 / D, bias=sb["eps"][:, 0:1])),
                 reads=keys("ps", b), writes=keys("fscr", fi))
            s.op("act", (lambda e, fs=fs: e.activation(out=fs[:, :], in_=fs[:, :], func=AF.Exp, scale=-0.5)),
                 reads=keys("fscr", fi), writes=keys("fscr", fi))
            for c in range(8):
                s.op("dve", (lambda e, c=c, tsl=tsl, fs=fs: e.scalar_tensor_tensor(
                    out=hT[:, c, tsl], in0=xT[:, c, tsl], scalar=G[:, gcol * 8 + c:gcol * 8 + c + 1], in1=fs[:, :],
                    op0=ALU.mult, op1=ALU.mult)),
                     reads=keys("xT", c, tc) + keys("fscr", fi), writes=keys("hT", c, tc))

    def mlp_prefetch(self, li, first_slot):
        if "mlp" not in self.parts:
            return
        self.slot_i = first_slot
        w_up = self.dram["w_up"][li]
        w_down = self.dram["w_down"][li]
        u = self.load_slot3(w_up[:, 0:512].rearrange("(k p) n -> p k n", p=128), 8, 512)
        d = self.load_slot3(w_down[0:512, :].rearrange("(k p) n -> p k n", p=128), 4, 1024)
        self.mlp_pre = (u, d)

    def mlp(self, li):
        s, sb = self.s, self.sb
        xT, hT, hid = sb["xT"], sb["hT"], sb["hid"]
        w_up = self.dram["w_up"][li]
        w_down = self.dram["w_down"][li]
        self.rmsnorm(G_MLP + li)
        NG_ = 8
        fills = {}

        def fetch(g):
            u = self.load_slot3(w_up[:, g * 512:(g + 1) * 512].rearrange("(k p) n -> p k n", p=128), 8, 512)
            d = self.load_slot3(w_down[g * 512:(g + 1) * 512, :].rearrange("(k p) n -> p k n", p=128), 4, 1024)
            fills[g] = (u, d)

        if getattr(self, "mlp_pre", None) is not None:
            fills[0] = self.mlp_pre
            self.mlp_pre = None
        else:
            fetch(0)
        self.ple_pre = None
        for g in range(NG_):
            if g + 1 < NG_:
                fetch(g + 1)
            elif "ple" in self.parts:
                self.ple_pre = self.ple_prefetch(li)
            (ui, uv), (di, dv) = fills[g]
            hb = g % 2
            for tc in range(NTC):
                for mi in range(4):
                    tsl = slice(tc * TC, (tc + 1) * TC)
                    b = self.bank()
                    pb = self.ps[b]
                    for k in range(8):
                        s.op("pe", (lambda e, k=k, pb=pb, uv=uv, mi=mi, tsl=tsl: e.matmul(
                            pb[:, :], uv[:, k, mi * 128:(mi + 1) * 128], hT[:, k, tsl], start=(k == 0), stop=(k == 7))),
                             reads=keys("hT", k, tc) + [("wslot", ui)], writes=keys("ps", b))
                    fi = self.fscr()
                    fs = sb["fscr"][fi]
                    s.op("act", (lambda e, pb=pb, fs=fs: e.activation(out=fs[:, :], in_=pb[:, :], func=AF.Relu)),
                         reads=keys("ps", b), writes=keys("fscr", fi))
                    s.op("pool", (lambda e, fs=fs, hb=hb, mi=mi, tsl=tsl: e.tensor_tensor(
                        out=hid[hb][:, mi, tsl], in0=fs[:, :], in1=fs[:, :], op=ALU.mult)),
                         reads=keys("fscr", fi), writes=keys("hid", hb, mi, tc))
            if g == NG_ - 1 and self.ple_pre is not None:
                wg_ = self.dram["w_ple_gate"][li]
                g1 = self.load_slot(lambda k: wg_[k * 128:(k + 1) * 128, 512:1024], 8, 512)
                self.ple_pre = self.ple_pre + (g1,)
            for tc in range(NTC):
                for oc in range(8):
                    tsl = slice(tc * TC, (tc + 1) * TC)
                    b = self.bank()
                    pb = self.ps[b]
                    for k in range(4):
                        s.op("pe", (lambda e, k=k, pb=pb, dv=dv, oc=oc, hb=hb, tsl=tsl: e.matmul(
                            pb[:, :], dv[:, k, oc * 128:(oc + 1) * 128], hid[hb][:, k, tsl], start=(k == 0), stop=(k == 3))),
                             reads=keys("hid", hb, k, tc) + [("wslot", di)], writes=keys("ps", b))
                    s.op("dve", (lambda e, pb=pb, oc=oc, tsl=tsl: e.tensor_tensor(
                        out=xT[:, oc, tsl], in0=pb[:, :], in1=xT[:, oc, tsl], op=ALU.add)),
                         reads=keys("ps", b) + keys("xT", oc, tc), writes=keys("xT", oc, tc))

    def ple_prefetch(self, li):
        s, sb = self.s, self.sb
        pT = sb["pT"]
        wg = self.dram["w_ple_gate"][li]
        wp = self.dram["w_ple_proj"][li]
        for k in range(2):
            s.dma("pool", "pT", (lambda e, k=k: e.dma_start(out=pT[:, k, :], in_=self.dram["pT"][li, k * 128:(k + 1) * 128, :])),
                  writes=[("pT",)])
        s.retag([("pT",)], "pT")
        p_ = self.load_slot(lambda k: wp[k * 128:(k + 1) * 128, :], 2, 1024)
        g0 = self.load_slot(lambda k: wg[k * 128:(k + 1) * 128, 0:512], 8, 512)
        return p_, g0

    def ple(self, li):
        s, sb = self.s, self.sb
        xT, hT, pT = sb["xT"], sb["hT"], sb["pT"]
        wg = self.dram["w_ple_gate"][li]
        pre = getattr(self, "ple_pre", None)
        if pre is None:
            pre = self.ple_prefetch(li)
        self.ple_pre = None
        (pi, pv), g0 = pre[0], pre[1]
        self.rmsnorm(G_PLE + li)
        g1 = pre[2] if len(pre) > 2 else self.load_slot(lambda k: wg[k * 128:(k + 1) * 128, 512:1024], 8, 512)
        gates = (g0, g1)
        for tc in range(NTC):
            tsl = slice(tc * TC, (tc + 1) * TC)
            for oc in range(8):
                gi, gv = gates[oc // 4]
                m = oc % 4
                b = self.bank()
                pb = self.ps[b]
                for k in range(8):
                    self.mm(pb[:, :], gv[:, k, m * 128:(m + 1) * 128], hT[:, k, tsl], k == 0, k == 7,
                            r=keys("hT", k, tc) + [("wslot", gi)], w=keys("ps", b))
                b2 = self.bank()
                pb2 = self.ps[b2]
                for k in range(2):
                    self.mm(pb2[:, :], pv[:, k, oc * 128:(oc + 1) * 128], pT[:, k, tsl], k == 0, k == 1,
                            r=[("pT",), ("wslot", pi)], w=keys("ps", b2))
                fi = self.fscr()
                fs = sb["fscr"][fi]
                self.act(fs[:, :], pb[:, :], AF.Sigmoid, r=keys("ps", b), w=keys("fscr", fi))
                self.tt("dve", fs[:, :], pb2[:, :], fs[:, :], ALU.mult, r=keys("ps", b2) + keys("fscr", fi), w=keys("fscr", fi))
                self.tt("pool", xT[:, oc, tsl], fs[:, :], xT[:, oc, tsl], ALU.add,
                        r=keys("fscr", fi) + keys("xT", oc, tc), w=keys("xT", oc, tc))

    def final_norm(self):
        s, sb = self.s, self.sb
        xT, sq, ones, G = sb["xT"], sb["sq"], sb["ones"], sb["gains"]
        for tc in range(NTC):
            tsl = slice(tc * TC, (tc + 1) * TC)
            for c in range(8):
                s.op("act", (lambda e, c=c, tsl=tsl: e.activation(out=sq[:, c, :], in_=xT[:, c, tsl], func=AF.Square)),
                     reads=keys("xT", c, tc), writes=keys("sq", c))
            b = self.bank()
            pb = self.ps[b]
            for c in range(8):
                s.op("pe", (lambda e, c=c, pb=pb: e.matmul(pb[:, :], ones[:, :], sq[:, c, :], start=(c == 0), stop=(c == 7))),
                     reads=keys("sq", c), writes=keys("ps", b))
            fi = self.fscr()
            fs = sb["fscr"][fi]
            s.op("act", (lambda e, pb=pb, fs=fs: e.activation(out=fs[:, :], in_=pb[:, :], func=AF.Ln, scale=1.0 / D, bias=sb["eps"][:, 0:1])),
                 reads=keys("ps", b), writes=keys("fscr", fi))
            s.op("act", (lambda e, fs=fs: e.activation(out=fs[:, :], in_=fs[:, :], func=AF.Exp, scale=-0.5)),
                 reads=keys("fscr", fi), writes=keys("fscr", fi))
            for c in range(8):
                s.op("dve", (lambda e, c=c, tsl=tsl, fs=fs: e.scalar_tensor_tensor(
                    out=xT[:, c, tsl], in0=xT[:, c, tsl], scalar=G[:, G_FINAL * 8 + c:G_FINAL * 8 + c + 1], in1=fs[:, :],
                    op0=ALU.mult, op1=ALU.mult)),
                     reads=keys("xT", c, tc) + keys("fscr", fi), writes=keys("xT", c, tc))

    def build(self):
        nc, s = self.nc, self.s
        self.declare_io()
        self.NSLOT = 4
        self.NFS = 6
        import contextlib
        with contextlib.ExitStack() as st:
            def sbt(name, shape, dt):
                return st.enter_context(nc.sbuf_tensor(name, shape, dt))
            sb = self.sb
            sb["xT"] = sbt("xT_sb", [128, 8, S], F32)
            sb["hT"] = sbt("hT_sb", [128, 8, S], BF16)
            sb["wslot"] = [sbt(f"wslot{i}", [128, 4096], BF16) for i in range(self.NSLOT)]
            sb["fscr"] = [sbt(f"fscr{i}", [128, TC], F32) for i in range(self.NFS)]
            sb["gains"] = sbt("gains_sb", [128, NGC], F32)
            sb["ones"] = sbt("ones_sb", [128, 128], BF16)
            sb["ident"] = sbt("ident_sb", [128, 128], BF16)
            sb["eps"] = sbt("eps_sb", [128, 1], F32)
            sb["lamv"] = sbt("lamv_sb", [128, 256], F32)
            sb["dmask"] = sbt("dmask_sb", [128, 3, 128], BF16)
            sb["qkscl"] = sbt("qkscl_sb", [128, 1], F32)
            sb["lams"] = sbt("lams_sb", [128, 8], F32)
            sb["R"] = sbt("R_sb", [128, 32768], BF16)
            rv = self.rview
            sb["hid"] = [rv(i * 8192, 8192).rearrange("p (m t) -> p m t", m=4) for i in range(2)]
            sb["sq"] = rv(16384, 4096).rearrange("p (c t) -> p c t", c=8)
            sb["pT"] = rv(20480, 4096).rearrange("p (k t) -> p k t", k=2)
            sb["QT"] = rv(0, 2048)
            sb["KT"] = rv(2048, 2048)
            sb["VA"] = rv(4096, 2048).rearrange("p (t d) -> p t d", t=16)
            sb["OTp"] = [rv(6144 + i * 2048, 2048) for i in range(2)]
            sb["PT"] = [rv(10240 + i * 512, 512) for i in range(4)]
            sb["PT2"] = [rv(8192 + i * 1024, 1024) for i in range(4)]
            sb["cqn"] = rv(12288, 4096).rearrange("p (k t) -> p k t", k=2)
            sb["ckvn"] = rv(20480, 2048)
            sb["KR"] = rv(22528, 2048)
            sb["ropeC"] = rv(24576, 4096, F32)
            sb["ropeS"] = rv(28672, 4096, F32)
            self.ACC, self.SB, self.MISC = (0, 1), (2, 3, 4, 5), (6, 7)
            self.pt_i = 0
            self.ps2 = [st.enter_context(nc.psum_tensor(f"ps{i}", [128, 2 * TC], F32)) for i in range(4)]
            self.ps = [self.ps2[i // 2][:, (i % 2) * TC:(i % 2 + 1) * TC] for i in range(8)]

            xT = sb["xT"]
            xsrc = self.dram["xT"].rearrange("(c p) t -> p c t", p=128)
            for c in range(8):
                s.dma("sp", "xin", (lambda e, c=c: e.dma_start(out=xT[:, c, :], in_=xsrc[:, c, :])),
                      writes=keys("xT", c, range(NTC)))
            s.retag(keys("xT", range(8), range(NTC)), "xin")
            s.dma("sp", "cst", (lambda e: e.dma_start(out=sb["gains"][:, :], in_=self.dram["gains"])), writes=[("gains",)])
            s.dma("pool", "cst2", (lambda e: e.dma_start(out=sb["ident"][:, :], in_=self.dram["ident"])), writes=[("ident",)])
            s.op("dve", (lambda e: e.memset(sb["ones"][:, :], 1.0)), writes=[("ones",)])
            s.op("dve", (lambda e: e.memset(sb["eps"][:, :], EPS)), writes=[("eps",)])
            s.barrier()
            for e in ("pe", "act", "dve", "pool"):
                for cs_ in ("cst", "cst2"):
                    s.ops[e].append(([(cs_, s.dma_count[cs_])], None, None, "init"))
                    s.known[e][cs_] = s.dma_count[cs_]

            for li in self.layers:
                if "mix" in self.parts:
                    s.phase = f"mix{li}"
                    self.mixer(li)
                if "mlp" in self.parts:
                    s.phase = f"mlp{li}"
                    self.mlp(li)
                if "ple" in self.parts:
                    s.phase = f"ple{li}"
                    self.ple(li)
            s.phase = "final"
            if self.do_final:
                self.final_norm()
            ydst = self.out.rearrange("(c p) t -> p c t", p=128)
            for c in range(8):
                s.dma("sp", "yout", (lambda e, c=c: e.dma_start(out=ydst[:, c, :], in_=xT[:, c, :])),
                      reads=keys("xT", c, range(NTC)))
            s.final_wait("sp", ["yout"] + (["dbg"] if self.dbg_names else []))

            self.emit(st)
        return nc

    def emit(self, st):
        nc, s = self.nc, self.s
        semnames = list(Sched.ENGS) + sorted(s.dma_count.keys())
        sems = {n: st.enter_context(nc.semaphore(f"sem_{n}")) for n in semnames}
        block = st.enter_context(nc.Block())

        def run(eng_name):
            def body(e):
                for waits, fn, inc, phase in s.ops[eng_name]:
                    for sk, val in waits:
                        e.wait_ge(sems[sk], val)
                    if fn is None:
                        continue
                    if self.scopes:
                        with nc.named_scope(phase):
                            ins = fn(e)
                    else:
                        ins = fn(e)
                    ins.then_inc(sems[inc[0]], inc[1])
            return body

        block.tensor(run("pe"))
        block.scalar(run("act"))
        block.vector(run("dve"))
        block.gpsimd(run("pool"))
        block.sync(run("sp"))


def rope_np(rot_dim):
    inv = np.power(np.float32(500000.0), -(np.arange(0, rot_dim, 2, dtype=np.float32) / np.float32(rot_dim))).astype(np.float32)
    ang = (np.arange(S, dtype=np.float32)[:, None] * inv[None, :]).astype(np.float32)
    return np.cos(ang).astype(np.float32), np.sin(ang).astype(np.float32)


def make_consts():
    d = {"ident": np.eye(128, dtype=np.float32)}
    c, sn = rope_np(32)
    C = np.ones((128, S), np.float32)
    Sg = np.zeros((128, S), np.float32)
    C[64:80] = c.T; C[80:96] = c.T
    Sg[64:80] = -sn.T; Sg[80:96] = sn.T
    d["ropeL"] = np.stack([C, Sg])
    c, sn = rope_np(16)
    C = np.ones((128, S), np.float32)
    Sg = np.zeros((128, S), np.float32)
    for o in (0, 64):
        C[o:o + 8] = c.T; C[o + 8:o + 16] = c.T
        Sg[o:o + 8] = -sn.T; Sg[o + 8:o + 16] = sn.T
    d["ropeP"] = np.stack([C, Sg])
    return d


def make_na_bias(rpb):
    out = np.full((16, 128, 21, 128), NEG, np.float32)
    pairs = [(5, 5 + d, d + 2) for d in range(-2, 3)]
    for t in (0, 1, 14, 15):
        pairs += [(t, kt, blk) for (kt, blk) in Builder.na_tiles(t)]
    qq = np.arange(128)
    kk = np.arange(128)
    for (t, kt, blk) in pairs:
        r = 2 * t + qq // 64
        c = qq % 64
        kr = 2 * kt + kk // 64
        kc = kk % 64
        rs = np.clip(r - 4, 0, 24)
        w0 = np.clip(c - 8, 0, 48)
        valid = ((kr[:, None] >= rs[None, :]) & (kr[:, None] < rs[None, :] + 8)
                 & (kc[:, None] >= w0[None, :]) & (kc[:, None] < w0[None, :] + 16))
        ro = np.clip(kr[:, None] - r[None, :] + 7, 0, 14)
        co = np.clip(kc[:, None] - c[None, :] + 15, 0, 30)
        g = rpb[:, ro, co]
        out[:, :, blk, :] = np.where(valid[None], g, np.float32(NEG))
    return out


def chunked(v):
    return np.ascontiguousarray(v.reshape(8, 128).T)


def make_gains(inp):
    g = np.zeros((128, NGC), np.float32)
    mixn = [inp["a_norm"][0], inp["b_norm"][0], inp["c_norm"][0], inp["d_norm"][0]]
    for i in range(4):
        g[:, (G_MIX + i) * 8:(G_MIX + i + 1) * 8] = chunked(mixn[i])
        g[:, (G_MLP + i) * 8:(G_MLP + i + 1) * 8] = chunked(inp["mlp_norm"][i])
        g[:, (G_PLE + i) * 8:(G_PLE + i + 1) * 8] = chunked(inp["ple_norm"][i])
    g[:, G_FINAL * 8:(G_FINAL + 1) * 8] = chunked(inp["final_norm"])
    g[:, NG * 8:NG * 8 + 2] = inp["b_q_norm"][0].reshape(2, 128).T
    g[:, NG * 8 + 2] = inp["b_kv_norm"][0]
    g[:, NG * 8 + 3] = inp["d_subln"][0]
    return g


def shared_inputs(inp, layers=(0, 1, 2, 3), parts=("mix", "mlp", "ple")):
    d = make_consts()
    d["gains"] = make_gains(inp)
    for k in ("w_up", "w_down", "w_ple_gate", "w_ple_proj"):
        d[k] = np.ascontiguousarray(inp[k], dtype=np.float32)
    if 1 in layers and "mix" in parts:
        w_in = inp["b_w_in"][0]
        perm32 = np.concatenate([np.arange(16, 32), np.arange(0, 16)])
        kr = w_in[:, 384:416]
        d["b_w_in"] = np.ascontiguousarray(w_in)
        d["b_w_kr"] = np.ascontiguousarray(np.concatenate([w_in[:, 0:64], kr, w_in[:, 0:64], kr[:, perm32]], axis=1))
        wq = inp["b_w_uq"][0]
        idx = np.arange(1536).reshape(16, 96).copy()
        idx[:, 64:96] = idx[:, 64:96][:, perm32]
        d["b_w_uq"] = np.ascontiguousarray(wq)
        d["b_w_uq_p"] = np.ascontiguousarray(wq[:, idx.reshape(-1)])
        d["b_w_ukv"] = np.ascontiguousarray(inp["b_w_ukv"][0])
        d["b_w_o"] = np.ascontiguousarray(inp["b_w_o"][0])
    if 0 in layers and "mix" in parts:
        w = inp["a_w_qkv"][0]
        perm16 = np.concatenate([np.arange(8, 16), np.arange(0, 8), np.arange(16, 64)])
        wh = np.zeros((15, D, 320), np.float32)
        for h in range(15):
            q = w[:, h * 64:(h + 1) * 64]
            k = w[:, 960 + h * 64:960 + (h + 1) * 64]
            v = w[:, 1920 + h * 64:1920 + (h + 1) * 64]
            wh[h] = np.concatenate([q, k, q[:, perm16], k[:, perm16], v], axis=1)
        d["a_w_h"] = wh
        d["a_w_o"] = np.ascontiguousarray(inp["a_w_o"][0])
        kk = np.arange(128)[:, None]
        qq = np.arange(128)[None, :]
        mA = np.where(kk >= qq, 0.0, NEG)
        mB = np.where(kk <= qq, 0.0, NEG)
        mAe = np.full((128, 128), NEG)
        mAe[0:64] = mA[64:128]
        d["dil_masks"] = np.ascontiguousarray(np.stack([mA, mB, mAe], axis=1).astype(np.float32))
    if 2 in layers and "mix" in parts:
        w = inp["c_w_qkv"][0]
        wa = np.zeros((8, D, 384), np.float32)
        for c in range(8):
            wa[c] = np.concatenate([w[:, c * 128:(c + 1) * 128], w[:, 1024 + c * 128:1024 + (c + 1) * 128],
                                    w[:, 2048 + c * 128:2048 + (c + 1) * 128]], axis=1)
        d["c_w_a"] = wa
        d["c_w_o"] = np.ascontiguousarray(inp["c_w_o"][0])
        d["na_bias"] = make_na_bias(inp["c_rpb"][0])
    if 3 in layers and "mix" in parts:
        w = inp["d_w_qkv"][0]
        perm16 = np.concatenate([np.arange(8, 16), np.arange(0, 8), np.arange(16, 64)])
        wa = np.zeros((8, D, 384), np.float32)
        wb = np.zeros((8, D, 256), np.float32)
        for h in range(8):
            q = w[:, h * 128:(h + 1) * 128]
            k = w[:, 1024 + h * 128:1024 + (h + 1) * 128]
            v = w[:, 2048 + h * 128:2048 + (h + 1) * 128]
            p2 = np.concatenate([perm16, 64 + perm16])
            wa[h] = np.concatenate([q, k, v], axis=1)
            wb[h] = np.concatenate([q[:, p2], k[:, p2]], axis=1)
        d["d_w_a"], d["d_w_b"] = wa, wb
        d["d_w_o"] = np.ascontiguousarray(inp["d_w_o"][0])
        lv = np.concatenate([inp["d_lambda_q1"][0], inp["d_lambda_k1"][0], inp["d_lambda_q2"][0], inp["d_lambda_k2"][0]])
        d["d_lamv"] = np.ascontiguousarray(np.tile(lv[None, :], (128, 1)).astype(np.float32))
    return d


def core_inputs(inp, b, x_override=None):
    x = inp["x"][b] if x_override is None else x_override
    return {
        "xT": np.ascontiguousarray(x.T, dtype=np.float32),
        "pT": np.ascontiguousarray(np.transpose(inp["p"][:, b], (0, 2, 1)), dtype=np.float32),
    }


def kernel(**inp):
    bld = Builder(layers=[0, 1, 2, 3], do_final=True)
    nc = bld.build()
    shared = shared_inputs(inp)
    in_maps = [dict(shared, **core_inputs(inp, b)) for b in range(NCORES)]
    res = run_bass_kernel_spmd(nc, in_maps, core_ids=list(range(NCORES)))
    out = np.stack([np.ascontiguousarray(r["yT"].T) for r in res.results], axis=0)
    return out.astype(np.float32)
```

```python
import math
import numpy as np
import ml_dtypes
import concourse.bass as bass
import concourse.mybir as mybir
from concourse.bass_utils import run_bass_kernel_spmd

F32 = mybir.dt.float32
BF16 = mybir.dt.bfloat16
AF = mybir.ActivationFunctionType
ALU = mybir.AluOpType

S = 2048
D = 1024
NCORES = 8
TC = 512
NTC = S // TC
EPS = 1e-6
NEG = -30000.0


class Sched:
    ENGS = ("pe", "act", "dve", "pool", "sp")

    def __init__(self):
        self.ops = {e: [] for e in self.ENGS}
        self.count = {e: 0 for e in self.ENGS}
        self.known = {e: {} for e in self.ENGS}
        self.lastw = {}
        self.readers = {}
        self.dma_count = {}
        self.phase = "init"

    def _deps(self, eng, reads, writes, is_dma):
        deps = set()
        for k in reads:
            t = self.lastw.get(k)
            if t is not None:
                deps.add(t)
        for k in writes:
            t = self.lastw.get(k)
            if t is not None:
                deps.add(t)
            for r in self.readers.get(k, ()):
                deps.add(r)
        need = {}
        for (sk, val, e) in deps:
            if e == eng and not is_dma:
                continue
            if self.known[eng].get(sk, 0) >= val:
                continue
            if need.get(sk, 0) < val:
                need[sk] = val
        for sk, val in need.items():
            self.known[eng][sk] = val
        return list(need.items())

    def _commit(self, tok, reads, writes):
        for k in writes:
            self.lastw[k] = tok
            self.readers[k] = []
        for k in reads:
            if k in writes:
                continue
            self.readers.setdefault(k, []).append(tok)

    def op(self, eng, fn, reads=(), writes=(), strict=False):
        waits = self._deps(eng, reads, writes, strict)
        self.count[eng] += 1
        tok = (eng, self.count[eng], eng)
        self._commit(tok, reads, writes)
        self.ops[eng].append((waits, fn, (eng, 1), self.phase))

    def dma(self, eng, sem, fn, reads=(), writes=()):
        waits = self._deps(eng, reads, writes, True)
        self.dma_count[sem] = self.dma_count.get(sem, 0) + 16
        tok = (sem, self.dma_count[sem], None)
        self._commit(tok, reads, writes)
        self.ops[eng].append((waits, fn, (sem, 16), self.phase))
        return tok

    def retag(self, keys, sem):
        tok = (sem, self.dma_count[sem], None)
        for k in keys:
            self.lastw[k] = tok

    def barrier(self):
        cur = dict(self.count)
        for e in self.ENGS:
            waits = []
            for f in self.ENGS:
                if f == e or cur[f] == 0:
                    continue
                if self.known[e].get(f, 0) < cur[f]:
                    self.known[e][f] = cur[f]
                    waits.append((f, cur[f]))
            if waits:
                self.ops[e].append((waits, None, None, self.phase))

    def final_wait(self, eng, sems):
        waits = [(s, self.dma_count[s]) for s in sems]
        self.ops[eng].append((waits, None, None, self.phase))


def keys(name, *idx):
    out = [(name,)]
    for ix in idx:
        if isinstance(ix, int):
            ix = (ix,)
        out = [o + (i,) for o in out for i in ix]
    return out


LAMBDA_INIT = [0.8 - 0.6 * math.exp(-0.3 * i) for i in range(4)]

G_MIX, G_MLP, G_PLE, G_FINAL = 0, 4, 8, 12
NG = 13
NGC = NG * 8 + 8


class Builder:
    def __init__(self, layers, do_final, parts=("mix", "mlp", "ple")):
        self.layers = list(layers)
        self.do_final = do_final
        self.parts = parts
        self.nc = bass.Bass("TRN2", target_bir_lowering=False)
        self.s = Sched()
        self.dram = {}
        self.sb = {}
        self.ps = []
        self.psi = 0
        self.slot_i = 0
        self.fs_i = 0
        self.bank_rot = {}
        self.ring_i = 0
        self.debug = False
        self.scopes = False
        self.dbg_names = []

    def din(self, name, shape, dt=F32):
        t = self.nc.dram_tensor(name, list(shape), dt, kind="ExternalInput").ap()
        self.dram[name] = t
        return t

    def declare_io(self):
        self.din("xT", [D, S])
        self.din("pT", [4, 256, S])
        self.din("gains", [128, NGC])
        self.din("ident", [128, 128])
        self.din("w_up", [4, D, 4 * D])
        self.din("w_down", [4, 4 * D, D])
        self.din("w_ple_gate", [4, D, D])
        self.din("w_ple_proj", [4, 256, D])
        self.din("ropeL", [2, 128, S])
        self.din("ropeP", [2, 128, S])
        if 1 in self.layers and "mix" in self.parts:
            self.din("b_w_in", [D, 416])
            self.din("b_w_kr", [D, 192])
            self.din("b_w_uq", [256, 1536])
            self.din("b_w_uq_p", [256, 1536])
            self.din("b_w_ukv", [128, 2048])
            self.din("b_w_o", [D, D])
        if 3 in self.layers and "mix" in self.parts:
            self.din("d_w_a", [8, D, 384])
            self.din("d_w_b", [8, D, 256])
            self.din("d_w_o", [D, D])
            self.din("d_lamv", [128, 256])
        if 2 in self.layers and "mix" in self.parts:
            self.din("c_w_a", [8, D, 384])
            self.din("c_w_o", [D, D])
            self.din("na_bias", [16, 128, 21, 128])
        if 0 in self.layers and "mix" in self.parts:
            self.din("a_w_h", [15, D, 320])
            self.din("a_w_o", [960, D])
            self.din("dil_masks", [128, 3, 128])
        self.out = self.nc.dram_tensor("yT", [D, S], F32, kind="ExternalOutput").ap()

    def bank(self, group=None):
        if group is None:
            b = self.psi
            self.psi = (self.psi + 1) % 8
            return b
        i = self.bank_rot.get(group, 0)
        self.bank_rot[group] = (i + 1) % len(group)
        return group[i]

    def fscr(self):
        i = self.fs_i
        self.fs_i = (self.fs_i + 1) % self.NFS
        return i

    def load_slot(self, src_fn, kc, ncols, parts=128, si=None):
        if si is None:
            si = self.slot_i
            self.slot_i = (self.slot_i + 1) % self.NSLOT
        slot = self.sb["wslot"][si]
        view = slot[0:parts, 0:kc * ncols].rearrange("p (k n) -> p k n", k=kc)
        sem = f"w{si}"
        for k in range(kc):
            src = src_fn(k)
            self.s.dma("pool", sem,
                       (lambda e, o=view[:, k, :], i=src: e.dma_start(out=o, in_=i)),
                       writes=[("wslot", si)])
        self.s.retag([("wslot", si)], sem)
        return si, view


    def mm(self, out, lhsT, rhs, start, stop, r, w):
        self.s.op("pe", (lambda e: e.matmul(out, lhsT, rhs, start=start, stop=stop)), reads=r, writes=w)

    def act(self, out, in_, func, r, w, **kw):
        self.s.op("act", (lambda e: e.activation(out=out, in_=in_, func=func, **kw)), reads=r, writes=w)

    def tt(self, eng, out, in0, in1, op, r, w, strict=False):
        self.s.op(eng, (lambda e: e.tensor_tensor(out=out, in0=in0, in1=in1, op=op)), reads=r, writes=w, strict=strict)

    def stt(self, eng, out, in0, scalar, in1, op0, op1, r, w):
        self.s.op(eng, (lambda e: e.scalar_tensor_tensor(out=out, in0=in0, scalar=scalar, in1=in1, op0=op0, op1=op1)),
                  reads=r, writes=w)

    def tsc(self, eng, out, in0, s1, op0, r, w, s2=None, op1=None):
        if op1 is None:
            self.s.op(eng, (lambda e: e.tensor_scalar(out=out, in0=in0, scalar1=s1, scalar2=None, op0=op0)), reads=r, writes=w)
        else:
            self.s.op(eng, (lambda e: e.tensor_scalar(out=out, in0=in0, scalar1=s1, scalar2=s2, op0=op0, op1=op1)), reads=r, writes=w)

    def cp(self, eng, out, in_, r, w):
        self.s.op(eng, (lambda e: e.tensor_copy(out=out, in_=in_)), reads=r, writes=w)

    def recip(self, out, in_, r, w):
        self.s.op("dve", (lambda e: e.reciprocal(out=out, in_=in_)), reads=r, writes=w)

    def memset(self, eng, ap, val, w):
        self.s.op(eng, (lambda e: e.memset(ap, val)), writes=w)

    def rview(self, off, n, dt=BF16):
        v = self.sb["R"][:, off:off + n]
        return v.bitcast(F32) if dt == F32 else v

    def load_slot3(self, src3, kc, ncols, nsplit=2):
        si = self.slot_i
        self.slot_i = (self.slot_i + 1) % self.NSLOT
        slot = self.sb["wslot"][si]
        view = slot[:, 0:kc * ncols].rearrange("p (k n) -> p k n", k=kc)
        sem = f"w{si}"
        step = kc // nsplit
        for k0 in range(0, kc, step):
            self.s.dma("pool", sem,
                       (lambda e, o=view[:, k0:k0 + step, :], i=src3[:, k0:k0 + step, :]: e.dma_start(out=o, in_=i)),
                       writes=[("wslot", si)])
        self.s.retag([("wslot", si)], sem)
        return si, view

    def load_rope(self, name):
        sb, s = self.sb, self.s
        for i, key in enumerate(("ropeC", "ropeS")):
            s.dma("sp", "rope", (lambda e, i=i, key=key: e.dma_start(out=sb[key][:, :], in_=self.dram[name][i])),
                  writes=[(key,)])
        s.retag([("ropeC",), ("ropeS",)], "rope")

    def out_proj_pair(self, wo_view, wi, si, otp, otp_key, kparts=128):
        sb = self.sb
        xT = sb["xT"]
        for tc in range(NTC):
            for oc in range(8):
                tsl = slice(tc * TC, (tc + 1) * TC)
                b = self.bank()
                pb = self.ps[b]
                self.mm(pb[:, :], wo_view[0:kparts, wi, oc * 128:(oc + 1) * 128], otp[0:kparts, tsl], True, True,
                        r=[("wslot", si)] + keys(otp_key, tc), w=keys("ps", b))
                self.tt("dve", xT[:, oc, tsl], pb[:, :], xT[:, oc, tsl], ALU.add,
                        r=keys("ps", b) + keys("xT", oc, tc), w=keys("xT", oc, tc))

    LA = 3

    def flush_deferred(self):
        for (_, fn) in self.deferred:
            fn()
        self.deferred = []

    def run_pipeline(self, n, front, back, la=None, flush=True):
        la = self.LA if la is None else la
        self.deferred = []
        for i in range(n + la):
            if i < n:
                front(i)
            if i >= la:
                back(i - la)
                keep = []
                for (due, fn) in self.deferred:
                    if due <= i - la:
                        fn()
                    else:
                        keep.append((due, fn))
                self.deferred = keep
        if flush:
            self.flush_deferred()

    def attn_dense_head(self, KT, QT, kparts, VA, acc_parts_v, emit_norm, nkt=16):
        sb = self.sb
        PT2 = sb["PT2"]
        npair = nkt // 2
        n = NTC * npair
        st = {}
        accb = [self.bank(self.ACC) for _ in range(NTC)]
        RING = (1, 2, 3)

        def front(i):
            qc, kp = divmod(i, npair)
            qsl = slice(qc * TC, (qc + 1) * TC)
            pb = RING[self.ring_i % len(RING)]
            self.ring_i += 1
            for half in range(2):
                kt = 2 * kp + half
                self.mm(self.ps[2 * pb + half][:, :], KT[0:kparts, kt * 128:(kt + 1) * 128], QT[0:kparts, qsl], True, True,
                        r=keys("KT", kt // 4) + keys("QT", qc), w=keys("ps", 2 * pb + half))
            pi = self.pt_i
            self.pt_i = (self.pt_i + 1) % 4
            self.act(PT2[pi][:, :], self.ps2[pb][:, :], AF.Exp, r=keys("ps", 2 * pb) + keys("ps", 2 * pb + 1), w=keys("PT2", pi))
            st[i] = pi

        def back(i):
            qc, kp = divmod(i, npair)
            ba = accb[qc]
            pi = st.pop(i)
            for half in range(2):
                kt = 2 * kp + half
                self.mm(self.ps[ba][:, :], VA[:, kt, :], PT2[pi][:, half * TC:(half + 1) * TC], kt == 0, kt == nkt - 1,
                        r=keys("VA", kt // 8) + keys("PT2", pi), w=keys("ps", ba))
            if kp == npair - 1:
                emit_norm(qc, ba)
        self.run_pipeline(n, front, back, la=2)

    def mixer_mla(self, li):
        s, sb, dram = self.s, self.sb, self.dram
        xT, hT, sq, ones, G = sb["xT"], sb["hT"], sb["sq"], sb["ones"], sb["gains"]
        QT, KT, VA, OTp, cqn, ckvn, KR = sb["QT"], sb["KT"], sb["VA"], sb["OTp"], sb["cqn"], sb["ckvn"], sb["KR"]
        C, Sn = sb["ropeC"], sb["ropeS"]
        scale = float((64 + 32) ** -0.5)
        self.load_rope("ropeL")
        self.memset("pool", VA[:, :, 64:128], 1.0, keys("VA", range(2)))
        self.memset("pool", QT[96:128, :], 0.0, keys("QT", range(NTC)))
        self.memset("pool", KT[96:128, :], 0.0, keys("KT", range(NTC)))
        self.rmsnorm(G_MIX + li)
        w_in, w_kr = dram["b_w_in"], dram["b_w_kr"]
        ai, av = self.load_slot(lambda k: w_in[k * 128:(k + 1) * 128, 0:384], 8, 384, si=0)
        bi, bv = self.load_slot(lambda k: w_kr[k * 128:(k + 1) * 128, :], 8, 192, si=1)
        ui, uv = self.load_slot(lambda k: dram["b_w_uq"][k * 128:(k + 1) * 128, :], 2, 1536, si=2)
        upi, upv = self.load_slot(lambda k: dram["b_w_uq_p"][k * 128:(k + 1) * 128, :], 2, 1536, si=3)
        for tc in range(NTC):
            tsl = slice(tc * TC, (tc + 1) * TC)
            zb = []
            for (view, vi, c0, m) in ((av, ai, 0, 128), (av, ai, 128, 128), (av, ai, 256, 128), (bv, bi, 0, 96), (bv, bi, 96, 96)):
                b = self.bank()
                zb.append(b)
                for k in range(8):
                    self.mm(self.ps[b][0:m, :], view[:, k, c0:c0 + m], hT[:, k, tsl], k == 0, k == 7,
                            r=keys("hT", k, tc) + [("wslot", vi)], w=keys("ps", b))
            for (chunks, nfeat, gcol, dst, dkey) in (((0, 1), 256, NG * 8, cqn, "cqn"), ((2,), 128, NG * 8 + 2, ckvn, "ckvn")):
                for j, zi in enumerate(chunks):
                    self.act(sq[:, j, :], self.ps[zb[zi]][:, :], AF.Square, r=keys("ps", zb[zi]), w=keys("sq", j))
                bn = self.bank()
                for j in range(len(chunks)):
                    self.mm(self.ps[bn][:, :], ones[:, :], sq[:, j, :], j == 0, j == len(chunks) - 1,
                            r=keys("sq", j), w=keys("ps", bn))
                fi = self.fscr()
                fs = sb["fscr"][fi]
                self.act(fs[:, :], self.ps[bn][:, :], AF.Ln, r=keys("ps", bn), w=keys("fscr", fi),
                         scale=1.0 / nfeat, bias=sb["eps"][:, 0:1])
                self.act(fs[:, :], fs[:, :], AF.Exp, r=keys("fscr", fi), w=keys("fscr", fi), scale=-0.5)
                for j, zi in enumerate(chunks):
                    o = dst[:, j, tsl] if len(chunks) > 1 else dst[:, tsl]
                    self.stt("dve", o, self.ps[zb[zi]][:, :], G[:, gcol + j:gcol + j + 1], fs[:, :], ALU.mult, ALU.mult,
                             r=keys("ps", zb[zi]) + keys("fscr", fi), w=keys(dkey, tc))
            f1i, f2i = self.fscr(), self.fscr()
            f1, f2 = sb["fscr"][f1i], sb["fscr"][f2i]
            self.tt("dve", f1[64:96, :], self.ps[zb[3]][64:96, :], C[64:96, tsl], ALU.mult,
                    r=keys("ps", zb[3]) + [("ropeC",)], w=keys("fscr", f1i))
            self.tt("dve", f2[64:96, :], self.ps[zb[4]][64:96, :], Sn[64:96, tsl], ALU.mult,
                    r=keys("ps", zb[4]) + [("ropeS",)], w=keys("fscr", f2i))
            self.tt("pool", KT[64:96, tsl], f1[64:96, :], f2[64:96, :], ALU.add,
                    r=keys("fscr", f1i) + keys("fscr", f2i), w=keys("KT", tc))
        ki, kv = self.load_slot(lambda k: dram["b_w_ukv"][:, :], 1, 2048, si=0)
        woi = 1
        wov = None
        def proj(h):
            for tc in range(NTC):
                tsl = slice(tc * TC, (tc + 1) * TC)
                bq, bqp = self.bank((2, 3, 4, 5, 6, 7)), self.bank((2, 3, 4, 5, 6, 7))
                for (b, view, vi) in ((bq, uv, ui), (bqp, upv, upi)):
                    for k in range(2):
                        self.mm(self.ps[b][0:96, :], view[:, k, h * 96:(h + 1) * 96], cqn[:, k, tsl], k == 0, k == 1,
                                r=keys("cqn", tc) + [("wslot", vi)], w=keys("ps", b))
                self.tsc("dve", QT[0:64, tsl], self.ps[bq][0:64, :], scale, ALU.mult, r=keys("ps", bq), w=keys("QT", tc))
                f1i, f2i = self.fscr(), self.fscr()
                f1, f2 = sb["fscr"][f1i], sb["fscr"][f2i]
                self.stt("dve", f1[64:96, :], self.ps[bq][64:96, :], scale, C[64:96, tsl], ALU.mult, ALU.mult,
                         r=keys("ps", bq) + [("ropeC",)], w=keys("fscr", f1i))
                self.stt("dve", f2[64:96, :], self.ps[bqp][64:96, :], scale, Sn[64:96, tsl], ALU.mult, ALU.mult,
                         r=keys("ps", bqp) + [("ropeS",)], w=keys("fscr", f2i))
                self.tt("pool", QT[64:96, tsl], f1[64:96, :], f2[64:96, :], ALU.add,
                        r=keys("fscr", f1i) + keys("fscr", f2i), w=keys("QT", tc))
                bk = self.bank((2, 3, 4, 5, 6, 7))
                self.mm(self.ps[bk][0:64, :], kv[:, 0, h * 128:h * 128 + 64], ckvn[:, tsl], True, True,
                        r=keys("ckvn", tc) + [("wslot", ki)], w=keys("ps", bk))
                self.cp("dve", KT[0:64, tsl], self.ps[bk][0:64, :], r=keys("ps", bk), w=keys("KT", tc))
            for half in range(2):
                bvb = self.bank((2, 3, 4, 5, 6, 7))
                for j in range(8):
                    tt_ = half * 8 + j
                    self.mm(self.ps[bvb][:, j * 64:(j + 1) * 64], ckvn[:, tt_ * 128:(tt_ + 1) * 128], kv[:, 0, h * 128 + 64:h * 128 + 128],
                            True, True, r=keys("ckvn", tt_ // 4) + [("wslot", ki)], w=keys("ps", bvb))
                self.cp("dve", VA[:, half * 8:(half + 1) * 8, 0:64], self.ps[bvb][:, :].rearrange("p (t d) -> p t d", t=8),
                        r=keys("ps", bvb), w=keys("VA", half))
        proj(0)
        for h in range(16):
            if h % 8 == 0:
                half = h // 8
                _, wov = self.load_slot(lambda k, half=half: dram["b_w_o"][half * 512 + k * 128:half * 512 + (k + 1) * 128, :], 4, 1024, si=woi)
            par = 0
            ot = OTp[par]
            okey = f"OTp{par}"

            def norm(qc, ba, h=h, ot=ot, okey=okey):
                qsl = slice(qc * TC, (qc + 1) * TC)
                fi = self.fscr()
                fs = sb["fscr"][fi]
                self.recip(fs[64:128, :], self.ps[ba][64:128, :], r=keys("ps", ba), w=keys("fscr", fi))
                r0 = (h % 2) * 64
                self.tt("dve", ot[r0:r0 + 64, qsl], self.ps[ba][0:64, :], fs[64:128, :], ALU.mult,
                        r=keys("ps", ba) + keys("fscr", fi), w=keys(okey, qc))
            self.attn_dense_head(KT, QT, 128, VA, 64, norm)
            if h + 1 < 16:
                proj(h + 1)
                if h + 1 == 15:
                    self.mlp_prefetch(li, 2)
            if h % 2 == 1:
                self.out_proj_pair(wov, (h // 2) % 4, woi, ot, okey)


    def dbg(self, name, ap, rkeys):
        if not getattr(self, "debug", False):
            return
        parts, n = ap.shape
        t = self.nc.dram_tensor("dbg_" + name, [parts, n], F32, kind="ExternalOutput").ap()
        self.s.dma("pool", "dbg", (lambda e: e.dma_start(out=t, in_=ap)), reads=rkeys)
        self.dbg_names.append(name)

    def fill_slot(self, si, pieces):
        slot = self.sb["wslot"][si]
        sem = f"w{si}"
        for pc in pieces:
            off, src = pc[0], pc[1]
            p0 = pc[2] if len(pc) > 2 else 0
            parts, n = src.shape
            self.s.dma("pool", sem, (lambda e, o=slot[p0:p0 + parts, off:off + n], i=src: e.dma_start(out=o, in_=i)),
                       writes=[("wslot", si)])
        self.s.retag([("wslot", si)], sem)
        return slot

    def rope_evac(self, dst, psq, psqp, bq, bqp, tsl, scale, dkeys, rows=slice(0, 128)):
        sb = self.sb
        C, Sn = sb["ropeC"], sb["ropeS"]
        f1i, f2i = self.fscr(), self.fscr()
        f1, f2 = sb["fscr"][f1i], sb["fscr"][f2i]
        self.stt("dve", f1[rows, :], psq[rows, :], scale, C[rows, tsl], ALU.mult, ALU.mult,
                 r=keys("ps", bq) + [("ropeC",)], w=keys("fscr", f1i))
        self.stt("dve", f2[rows, :], psqp[rows, :], scale, Sn[rows, tsl], ALU.mult, ALU.mult,
                 r=keys("ps", bqp) + [("ropeS",)], w=keys("fscr", f2i))
        self.tt("pool", dst, f1[rows, :], f2[rows, :], ALU.add,
                r=keys("fscr", f1i) + keys("fscr", f2i), w=dkeys)

    def rope_evac2(self, dstA, dstB, psq, psqp, bq, bqp, tsl, scale, dkeys):
        sb = self.sb
        C, Sn = sb["ropeC"], sb["ropeS"]
        f1i, f2i = self.fscr(), self.fscr()
        f1, f2 = sb["fscr"][f1i], sb["fscr"][f2i]
        self.stt("dve", f1[:, :], psq[:, :], scale, C[:, tsl], ALU.mult, ALU.mult,
                 r=keys("ps", bq) + [("ropeC",)], w=keys("fscr", f1i))
        self.stt("dve", f2[:, :], psqp[:, :], scale, Sn[:, tsl], ALU.mult, ALU.mult,
                 r=keys("ps", bqp) + [("ropeS",)], w=keys("fscr", f2i))
        self.tt("pool", dstA, f1[0:64, :], f2[0:64, :], ALU.add, r=keys("fscr", f1i) + keys("fscr", f2i), w=dkeys)
        self.tt("pool", dstB, f1[64:128, :], f2[64:128, :], ALU.add, r=keys("fscr", f1i) + keys("fscr", f2i), w=dkeys)

    def mixer_diff(self, li):
        s, sb, dram = self.s, self.sb, self.dram
        xT, hT, sq, ones, G = sb["xT"], sb["hT"], sb["sq"], sb["ones"], sb["gains"]
        QT, KT, VA, OTp, PT = sb["QT"], sb["KT"], sb["VA"], sb["OTp"], sb["PT"]
        lam_init = LAMBDA_INIT[li]
        scale = 0.125
        self.load_rope("ropeP")
        QA, QB = QT, self.rview(12288, 2048)
        O1 = [self.rview(o_, 1024, F32) for o_ in (14336, 15360, 20480, 21504)]
        self.memset("pool", QA[64:128, :], 0.0, keys("QT", range(NTC)))
        self.memset("pool", QB[0:64, :], 0.0, keys("QT", range(NTC)))
        lamv, lams = sb["lamv"], sb["lams"]
        s.dma("sp", "lamv", (lambda e: e.dma_start(out=lamv[:, :], in_=dram["d_lamv"])), writes=[("lamv",)])
        fi = self.fscr()
        fs = sb["fscr"][fi]
        for j in range(2):
            self.tt("dve", fs[:, j * 64:(j + 1) * 64], lamv[:, (2 * j) * 64:(2 * j + 1) * 64], lamv[:, (2 * j + 1) * 64:(2 * j + 2) * 64],
                    ALU.mult, r=[("lamv",)], w=keys("fscr", fi), strict=True)
            s.op("dve", (lambda e, j=j: e.reduce_sum(out=lams[:, j:j + 1], in_=fs[:, j * 64:(j + 1) * 64], axis=mybir.AxisListType.X)),
                 reads=keys("fscr", fi), writes=[("lams",)], strict=True)
        self.act(lams[:, 2:4], lams[:, 0:2], AF.Exp, r=[("lams",)], w=[("lams",)])
        s.op("dve", (lambda e: e.memset(lams[:, 6:7], -lam_init)), writes=[("lams",)], strict=True)
        self.tt("dve", lams[:, 4:5], lams[:, 3:4], lams[:, 2:3], ALU.subtract, r=[("lams",)], w=[("lams",)], strict=True)
        self.tt("dve", lams[:, 4:5], lams[:, 4:5], lams[:, 6:7], ALU.add, r=[("lams",)], w=[("lams",)], strict=True)
        s.op("dve", (lambda e: e.tensor_scalar(out=lams[:, 5:6], in0=G[:, NG * 8 + 3:NG * 8 + 4], scalar1=1.0 - lam_init, scalar2=None, op0=ALU.mult)),
             reads=[("gains",), ("lams",)], writes=[("lams",)], strict=True)
        self.rmsnorm(G_MIX + li)
        wa, wb, wo = dram["d_w_a"], dram["d_w_b"], dram["d_w_o"]

        def fetch(h):
            sa, sbi = (0, 1) if h % 2 == 0 else (2, 3)
            A = self.fill_slot(sa, [(k * 384, wa[h, k * 128:(k + 1) * 128, :]) for k in range(8)])
            B = self.fill_slot(sbi, [(k * 256, wb[h, k * 128:(k + 1) * 128, :]) for k in range(8)]
                               + [(2048, wo[h * 128:(h + 1) * 128, :])])
            return (sa, A[:, 0:3072].rearrange("p (k n) -> p k n", k=8), sbi, B[:, 0:2048].rearrange("p (k n) -> p k n", k=8),
                    B[:, 2048:3072].rearrange("p (a n) -> p a n", a=1))
        def proj(sa, A, sbi, B):
            for (dst, dk, c0, c0p) in ((QT, "QT", 0, 0), (KT, "KT", 128, 128)):
                for tc in range(NTC):
                    tsl = slice(tc * TC, (tc + 1) * TC)
                    bq, bqp = self.bank((2, 3, 4, 5)), self.bank((2, 3, 4, 5))
                    for k in range(8):
                        self.mm(self.ps[bq][:, :], A[:, k, c0:c0 + 128], hT[:, k, tsl], k == 0, k == 7,
                                r=keys("hT", k, tc) + [("wslot", sa)], w=keys("ps", bq))
                    for k in range(8):
                        self.mm(self.ps[bqp][:, :], B[:, k, c0p:c0p + 128], hT[:, k, tsl], k == 0, k == 7,
                                r=keys("hT", k, tc) + [("wslot", sbi)], w=keys("ps", bqp))
                    if dk == "KT":
                        self.rope_evac(dst[:, tsl], self.ps[bq], self.ps[bqp], bq, bqp, tsl, 1.0, keys(dk, tc))
                    else:
                        self.rope_evac2(QA[0:64, tsl], QB[64:128, tsl], self.ps[bq], self.ps[bqp], bq, bqp, tsl, scale, keys(dk, tc))
            for t4 in range(4):
                bvb = self.bank((2, 3, 4, 5))
                for j in range(4):
                    tt_ = t4 * 4 + j
                    for k in range(8):
                        self.mm(self.ps[bvb][:, j * 128:(j + 1) * 128], hT[:, k, tt_ * 128:(tt_ + 1) * 128], A[:, k, 256:384], k == 0, k == 7,
                                r=keys("hT", k, tt_ // 4) + [("wslot", sa)], w=keys("ps", bvb))
                self.cp("dve", VA[:, t4 * 4:(t4 + 1) * 4, :], self.ps[bvb][:, :].rearrange("p (t d) -> p t d", t=4),
                        r=keys("ps", bvb), w=keys("VA", t4 // 2))
        cur = fetch(0)
        proj(cur[0], cur[1], cur[2], cur[3])
        for h in range(8):
            sa, A, sbi, B, WO = cur
            nxt = fetch(h + 1) if h + 1 < 8 else None
            if h == 7:
                self.mlp_prefetch(li, 0)
            par = 0
            ot, okey = OTp[par], f"OTp{par}"
            if h == 0:
                self.dbg("QT", QT[:, :], keys("QT", range(4)))
                self.dbg("KT", KT[:, :], keys("KT", range(4)))
                self.dbg("V0", VA[:, 0, :], keys("VA", range(2)))
                self.dbg("lams", lams[:, :], [("lams",)])
            n = NTC * 2 * 8
            st = {}
            o1s = {}
            SETS = ((0, 1), (6, 7))
            RING = (1, 2)
            PT2 = sb["PT2"]

            def front(i, h=h):
                g_, kp = divmod(i, 8)
                qc, mp = divmod(g_, 2)
                qsl = slice(qc * TC, (qc + 1) * TC)
                pb = RING[self.ring_i % 2]
                self.ring_i += 1
                for half in range(2):
                    kt = 2 * kp + half
                    self.mm(self.ps[2 * pb + half][:, :], KT[:, kt * 128:(kt + 1) * 128], (QA if mp == 0 else QB)[:, qsl], True, True,
                            r=keys("KT", kt // 4) + keys("QT", qc), w=keys("ps", 2 * pb + half))
                pi = self.pt_i
                self.pt_i = (self.pt_i + 1) % 4
                self.act(PT2[pi][:, :], self.ps2[pb][:, :], AF.Exp, r=keys("ps", 2 * pb) + keys("ps", 2 * pb + 1), w=keys("PT2", pi))
                st[i] = pi

            def back(i, h=h, ot=ot, okey=okey):
                g_, kp = divmod(i, 8)
                qc, mp = divmod(g_, 2)
                qsl = slice(qc * TC, (qc + 1) * TC)
                ba, bl = SETS[g_ % 2]
                pi = st.pop(i)
                for half in range(2):
                    kt = 2 * kp + half
                    pt = PT2[pi][:, half * TC:(half + 1) * TC]
                    self.mm(self.ps[ba][:, :], VA[:, kt, :], pt, kt == 0, kt == 15,
                            r=keys("VA", kt // 8) + keys("PT2", pi), w=keys("ps", ba))
                    self.mm(self.ps[bl][:, :], ones[:, :], pt, kt == 0, kt == 15,
                            r=keys("PT2", pi), w=keys("ps", bl))
                if kp != 7:
                    return
                ri = self.fscr()
                rr = sb["fscr"][ri]
                self.recip(rr[:, :], self.ps[bl][:, :], r=keys("ps", bl), w=keys("fscr", ri))
                o1 = O1[qc]
                if mp == 0:
                    self.tt("dve", o1[:, :], self.ps[ba][:, :], rr[:, :], ALU.mult,
                            r=keys("ps", ba) + keys("fscr", ri), w=keys("O1", qc))
                    return
                self.tt("dve", rr[:, :], self.ps[ba][:, :], rr[:, :], ALU.mult,
                        r=keys("ps", ba) + keys("fscr", ri), w=keys("fscr", ri))
                self.stt("dve", o1[:, :], rr[:, :], lams[:, 4:5], o1[:, :], ALU.mult, ALU.add,
                         r=keys("fscr", ri) + keys("O1", qc) + [("lams",)], w=keys("O1", qc))
                self.act(sq[:, qc, :], o1[:, :], AF.Square, r=keys("O1", qc), w=keys("sq", qc))

                def tail(qc=qc, o1=o1, qsl=qsl):
                    pbn = RING[self.ring_i % 2]
                    self.ring_i += 1
                    bn = 2 * pbn
                    self.mm(self.ps[bn][:, :], ones[:, :], sq[:, qc, :], True, True, r=keys("sq", qc), w=keys("ps", bn))
                    r2 = self.fscr()
                    r2t = sb["fscr"][r2]
                    self.act(r2t[:, :], self.ps[bn][:, :], AF.Ln, r=keys("ps", bn), w=keys("fscr", r2), scale=1.0 / 128, bias=sb["eps"][:, 0:1])
                    self.act(r2t[:, :], r2t[:, :], AF.Exp, r=keys("fscr", r2), w=keys("fscr", r2), scale=-0.5)
                    self.stt("dve", ot[:, qsl], o1[:, :], lams[:, 5:6], r2t[:, :], ALU.mult, ALU.mult,
                             r=keys("O1", qc) + keys("fscr", r2) + [("lams",)], w=keys(okey, qc))
                self.deferred.append((i + 9, tail))
            self.run_pipeline(n, front, back, la=1, flush=False)
            if h == 0:
                self.dbg("OT", ot[:, :], keys(okey, range(4)))
            if nxt is not None:
                proj(nxt[0], nxt[1], nxt[2], nxt[3])
            self.flush_deferred()
            self.out_proj_pair(WO, 0, sbi, ot, okey)
            cur = nxt


    @staticmethod
    def na_tiles(t):
        if 2 <= t <= 13:
            return [(t + d, d + 2) for d in range(-2, 3)]
        base = {0: 5, 1: 9, 14: 13, 15: 17}[t]
        k0 = 0 if t < 2 else 12
        return [(k0 + j, base + j) for j in range(4)]

    def mixer_na(self, li):
        s, sb, dram = self.s, self.sb, self.dram
        xT, hT, ident = sb["xT"], sb["hT"], sb["ident"]
        QT, KT, OTp, PT = sb["QT"], sb["KT"], sb["OTp"], sb["PT"]
        VA2 = self.rview(12288, 4096).rearrange("p (t h d) -> p t h d", t=16, h=2)
        BIs = [self.rview(o_, 5376).rearrange("p (h b q) -> p h b q", h=2, b=21) for o_ in (24576, 16384)]
        self.memset("pool", VA2[:, :, :, 64:128], 1.0, keys("VA", range(2)))
        QA, QB = QT, self.rview(29952, 2048)
        self.memset("pool", QA[64:128, :], 0.0, keys("QT", range(NTC)))
        self.memset("pool", QB[0:64, :], 0.0, keys("QT", range(NTC)))
        self.rmsnorm(G_MIX + li)
        s.barrier()
        wa, wo, nab = dram["c_w_a"], dram["c_w_o"], dram["na_bias"]

        def fetch(c):
            si = c % 4
            A = self.fill_slot(si, [(k * 384, wa[c, k * 128:(k + 1) * 128, :]) for k in range(8)]
                               + [(3072, wo[c * 128:(c + 1) * 128, :])])
            return si, A[:, 0:3072].rearrange("p (k n) -> p k n", k=8), A[:, 3072:4096].rearrange("p (a n) -> p a n", a=1)
        def load_bias(c):
            for hh in range(2):
                s.dma("pool", f"nab{c % 2}", (lambda e, hh=hh, c=c: e.dma_start(out=BIs[c % 2][:, hh, :, :], in_=nab[2 * c + hh])),
                      writes=[("BI", c % 2)])
            s.retag([("BI", c % 2)], f"nab{c % 2}")

        def proj(si, A):
            for (dst, dk, c0, sc) in ((QT, "QT", 0, 0.125), (KT, "KT", 128, 1.0)):
                for tc in range(NTC):
                    tsl = slice(tc * TC, (tc + 1) * TC)
                    bq = self.bank((2, 3, 4, 5))
                    for k in range(8):
                        self.mm(self.ps[bq][:, :], A[:, k, c0:c0 + 128], hT[:, k, tsl], k == 0, k == 7,
                                r=keys("hT", k, tc) + [("wslot", si)], w=keys("ps", bq))
                    if dk == "KT":
                        self.tsc("dve", dst[:, tsl], self.ps[bq][:, :], sc, ALU.mult, r=keys("ps", bq), w=keys(dk, tc))
                    else:
                        self.tsc("dve", QA[0:64, tsl], self.ps[bq][0:64, :], sc, ALU.mult, r=keys("ps", bq), w=keys(dk, tc))
                        self.tsc("dve", QB[64:128, tsl], self.ps[bq][64:128, :], sc, ALU.mult, r=keys("ps", bq), w=keys(dk, tc))
            for t4 in range(4):
                bvb = self.bank((2, 3, 4, 5))
                for j in range(4):
                    tt_ = t4 * 4 + j
                    for k in range(8):
                        self.mm(self.ps[bvb][:, j * 128:(j + 1) * 128], hT[:, k, tt_ * 128:(tt_ + 1) * 128], A[:, k, 256:384], k == 0, k == 7,
                                r=keys("hT", k, tt_ // 4) + [("wslot", si)], w=keys("ps", bvb))
                self.cp("dve", VA2[:, t4 * 4:(t4 + 1) * 4, :, 0:64], self.ps[bvb][:, :].rearrange("p (t h d) -> p t h d", t=4, h=2),
                        r=keys("ps", bvb), w=keys("VA", t4 // 2))
        wl = {0: fetch(0), 1: fetch(1)}
        load_bias(0)
        proj(wl[0][0], wl[0][1])
        for c in range(8):
            si, A, WO = wl[c]
            if c + 2 < 8:
                wl[c + 2] = fetch(c + 2)
            if c + 1 < 8:
                load_bias(c + 1)
            if c == 7:
                self.mlp_prefetch(li, 0)
            BI = BIs[c % 2]
            par = c % 2
            ot, okey = OTp[par], f"OTp{par}"
            units = [(t, j, kt, blk, len(self.na_tiles(t))) for t in range(16) for j, (kt, blk) in enumerate(self.na_tiles(t))]
            st = {}

            def front(i):
                t, j, kt, blk, nt = units[i]
                q128 = slice(t * 128, (t + 1) * 128)
                bs = self.bank(self.SB)
                for hh in range(2):
                    self.mm(self.ps[bs][:, hh * 128:(hh + 1) * 128], KT[:, kt * 128:(kt + 1) * 128], (QA, QB)[hh][:, q128], True, False,
                            r=keys("KT", kt // 4) + keys("QT", t // 4), w=keys("ps", bs))
                    self.mm(self.ps[bs][:, hh * 128:(hh + 1) * 128], ident[:, :], BI[:, hh, blk, :], False, True,
                            r=[("BI", c % 2), ("ident",)], w=keys("ps", bs))
                pi = self.pt_i
                self.pt_i = (self.pt_i + 1) % 4
                self.act(PT[pi][:, 0:256], self.ps[bs][:, 0:256], AF.Exp, r=keys("ps", bs), w=keys("PT", pi))
                st[i] = pi

            def back(i, ot=ot, okey=okey):
                t, j, kt, blk, nt = units[i]
                q128 = slice(t * 128, (t + 1) * 128)
                bas = (0, 1) if t % 2 == 0 else (6, 7)
                pi = st.pop(i)
                for hh in range(2):
                    self.mm(self.ps[bas[hh]][:, 0:128], VA2[:, kt, hh, :], PT[pi][:, hh * 128:(hh + 1) * 128],
                            j == 0, j == nt - 1, r=keys("VA", kt // 8) + keys("PT", pi), w=keys("ps", bas[hh]))
                if j != nt - 1:
                    return
                fi = self.fscr()
                fs = sb["fscr"][fi]
                for hh in range(2):
                    ba = bas[hh]
                    self.recip(fs[64:128, hh * 128:(hh + 1) * 128], self.ps[ba][64:128, 0:128], r=keys("ps", ba), w=keys("fscr", fi))
                    self.tt("dve", ot[hh * 64:(hh + 1) * 64, q128], self.ps[ba][0:64, 0:128],
                            fs[64:128, hh * 128:(hh + 1) * 128], ALU.mult,
                            r=keys("ps", ba) + keys("fscr", fi), w=keys(okey, t // 4))
            self.run_pipeline(len(units), front, back)
            if c + 1 < 8:
                proj(wl[c + 1][0], wl[c + 1][1])
            self.out_proj_pair(WO, 0, si, ot, okey)


    def mixer_dil(self, li):
        s, sb, dram = self.s, self.sb, self.dram
        xT, hT, ident, PT = sb["xT"], sb["hT"], sb["ident"], sb["PT"]
        C, Sn = sb["ropeC"], sb["ropeS"]
        QTd, KTd = self.rview(0, 2048), self.rview(2048, 2048)
        VAd = self.rview(4096, 4096).rearrange("p (t d) -> p t d", t=32)
        OTa = [self.rview(8192 + p_ * 512, 512) for p_ in range(2)]
        OTb = [self.rview(9216 + p_ * 512, 512) for p_ in range(2)]
        U = [self.rview(12288 + g * 4096, 4096, F32) for g in range(3)]
        MK, scl = sb["dmask"], sb["qkscl"]
        self.load_rope("ropeP")
        s.dma("pool", "dmask", (lambda e: e.dma_start(out=MK[:, :, :], in_=dram["dil_masks"])), writes=[("dmask",)])
        self.memset("dve", scl[0:64, :], 0.125, [("qkscl",)])
        self.memset("dve", scl[64:128, :], 1.0, [("qkscl",)])
        self.rmsnorm(G_MIX + li)
        s.barrier()
        self.memset("pool", VAd[:, :, 64:128], 1.0, [("VAd",)])
        for p_ in range(2):
            self.memset("pool", OTb[p_][64:128, :], 0.0, [("OTb", p_)])
        self.memset("pool", QTd[64:128, :], 0.0, [("QTd",)])
        self.memset("pool", KTd[64:128, :], 0.0, [("KTd",)])
        wh, wo = dram["a_w_h"], dram["a_w_o"]
        DIL = (1, 4, 16)

        def fetch(idx):
            j, g = idx // 3, idx % 3
            h = g * 5 + j
            si = idx % 3
            A = self.fill_slot(si, [(k * 320, wh[h, k * 128:(k + 1) * 128, :]) for k in range(8)])
            return si, A[:, 0:2560].rearrange("p (k n) -> p k n", k=8)
        pend_f = [fetch(0), fetch(1)]
        for j in range(5):
            hrow = lambda g: wo[(g * 5 + j) * 64:(g * 5 + j + 1) * 64, :]
            WOs = self.fill_slot(3, [(0, hrow(0), 0), (0, hrow(1), 64), (1024, hrow(2), 0), (1024, hrow(2), 64)])
            WOv = WOs[:, 0:2048].rearrange("p (a n) -> p a n", a=2)
            for g in range(3):
                idx = j * 3 + g
                dil = DIL[g]
                L = S // dil
                si, A = pend_f.pop(0)
                if idx + 2 < 15:
                    pend_f.append(fetch(idx + 2))
                if idx == 14:
                    self.mlp_prefetch(li, 0)
                for tc in range(NTC):
                    tsl = slice(tc * TC, (tc + 1) * TC)
                    bq, bqp = self.bank(), self.bank()
                    for (b, c0) in ((bq, 0), (bqp, 128)):
                        for k in range(8):
                            self.mm(self.ps[b][:, :], A[:, k, c0:c0 + 128], hT[:, k, tsl], k == 0, k == 7,
                                    r=keys("hT", k, tc) + [("wslot", si)], w=keys("ps", b))
                    f1i, f2i = self.fscr(), self.fscr()
                    f1, f2 = sb["fscr"][f1i], sb["fscr"][f2i]
                    self.stt("dve", f1[:, :], self.ps[bq][:, :], scl[:, 0:1], C[:, tsl], ALU.mult, ALU.mult,
                             r=keys("ps", bq) + [("ropeC",), ("qkscl",)], w=keys("fscr", f1i))
                    self.stt("dve", f2[:, :], self.ps[bqp][:, :], scl[:, 0:1], Sn[:, tsl], ALU.mult, ALU.mult,
                             r=keys("ps", bqp) + [("ropeS",), ("qkscl",)], w=keys("fscr", f2i))
                    n = TC // dil
                    l0 = tc * n
                    for (dst, dk, r0) in ((QTd, "QTd", 0), (KTd, "KTd", 64)):
                        d = dst[0:64, :].rearrange("p (r l) -> p r l", r=dil)[:, :, l0:l0 + n]
                        a0 = f1[r0:r0 + 64, :].rearrange("p (i r) -> p r i", r=dil)
                        a1 = f2[r0:r0 + 64, :].rearrange("p (i r) -> p r i", r=dil)
                        self.tt("pool" if r0 == 0 else "dve", d, a0, a1, ALU.add, r=keys("fscr", f1i) + keys("fscr", f2i), w=[(dk,)])
                nj = L // 128
                vt = {}
                tl = []
                for r in range(dil):
                    for jt in range(nj + 1):
                        p0, m = (0, 64) if jt == 0 else ((jt * 128 - 64, 64) if jt == nj else (jt * 128 - 64, 128))
                        vt[(r, jt)] = (len(tl), m)
                        tl.append((r, p0, m))
                hsub = [hT[:, k, :].rearrange("p (l r) -> p r l", r=dil) for k in range(8)]
                for t0 in range(0, len(tl), 8):
                    grp = tl[t0:t0 + 8]
                    bvb = self.bank()
                    for jj, (r, p0, m) in enumerate(grp):
                        for k in range(8):
                            self.mm(self.ps[bvb][0:m, jj * 64:(jj + 1) * 64], hsub[k][:, r, p0:p0 + m], A[:, k, 256:320], k == 0, k == 7,
                                    r=keys("hT", k, range(NTC)) + [("wslot", si)], w=keys("ps", bvb))
                    ng = len(grp)
                    self.cp("dve", VAd[:, t0:t0 + ng, 0:64], self.ps[bvb][:, 0:ng * 64].rearrange("p (t d) -> p t d", t=ng),
                            r=keys("ps", bvb), w=[("VAd",)])
                Ug = U[g][:, :].rearrange("p (l r) -> p r l", r=dil)
                units = [(r, jq) for r in range(dil) for jq in range(nj)]
                st = {}

                def front(i, L=L, nj=nj, vt=vt):
                    r, jq = units[i]
                    q0 = r * L + jq * 128
                    bs = self.bank(self.SB)
                    wins = []
                    for w_, jt in enumerate((jq, jq + 1)):
                        ti_, m = vt[(r, jt)]
                        if jt == 0:
                            k0_, mk = r * L, MK[0:64, 2, :]
                        elif jt == nj:
                            k0_, mk = r * L + L - 64, MK[0:64, 1, :]
                        else:
                            k0_, mk = r * L + jt * 128 - 64, MK[:, w_, :]
                        cs = slice(w_ * 128, (w_ + 1) * 128)
                        self.mm(self.ps[bs][0:m, cs], KTd[:, k0_:k0_ + m], QTd[:, q0:q0 + 128], True, False,
                                r=[("KTd",), ("QTd",)], w=keys("ps", bs))
                        self.mm(self.ps[bs][0:m, cs], ident[0:m, 0:m], mk, False, True,
                                r=[("dmask",), ("ident",)], w=keys("ps", bs))
                        wins.append((ti_, m, cs))
                    pi = self.pt_i
                    self.pt_i = (self.pt_i + 1) % 4
                    for (ti_, m, cs) in wins:
                        self.act(PT[pi][0:m, cs], self.ps[bs][0:m, cs], AF.Exp, r=keys("ps", bs), w=keys("PT", pi))
                    st[i] = (pi, wins)

                def back(i, Ug=Ug, g=g):
                    r, jq = units[i]
                    pi, wins = st.pop(i)
                    ba = self.bank(self.ACC)
                    for w_, (ti_, m, cs) in enumerate(wins):
                        self.mm(self.ps[ba][:, 0:128], VAd[0:m, ti_, :], PT[pi][0:m, cs], w_ == 0, w_ == 1,
                                r=[("VAd",)] + keys("PT", pi), w=keys("ps", ba))
                    self.act(Ug[:, r, jq * 128:(jq + 1) * 128], self.ps[ba][:, 0:128], AF.Copy, r=keys("ps", ba), w=[("U", g)])
                self.run_pipeline(len(units), front, back)
            fsum = U[0]
            self.tt("pool", U[0][64:128, :], U[0][64:128, :], U[1][64:128, :], ALU.add, r=[("U", 0), ("U", 1)], w=[("U", 0)])
            self.tt("pool", U[0][64:128, :], U[0][64:128, :], U[2][64:128, :], ALU.add, r=[("U", 0), ("U", 2)], w=[("U", 0)])
            for tc in range(NTC):
                tsl = slice(tc * TC, (tc + 1) * TC)
                par = tc % 2
                fi = self.fscr()
                fs = sb["fscr"][fi]
                self.act(fs[0:64, :], U[0][64:128, tsl], AF.Ln, r=[("U", 0)], w=keys("fscr", fi))
                self.act(fs[0:64, :], fs[0:64, :], AF.Exp, r=keys("fscr", fi), w=keys("fscr", fi), scale=-1.0)
                self.tt("dve", OTa[par][0:64, :], U[0][0:64, tsl], fs[0:64, :], ALU.mult,
                        r=[("U", 0)] + keys("fscr", fi), w=[("OTa", par)])
                self.tt("dve", OTa[par][64:128, :], U[1][0:64, tsl], fs[0:64, :], ALU.mult,
                        r=[("U", 1)] + keys("fscr", fi), w=[("OTa", par)])
                self.tt("dve", OTb[par][0:64, :], U[2][0:64, tsl], fs[0:64, :], ALU.mult,
                        r=[("U", 2)] + keys("fscr", fi), w=[("OTb", par)])
                for oc in range(8):
                    b = self.bank()
                    pb = self.ps[b]
                    self.mm(pb[:, :], WOv[:, 0, oc * 128:(oc + 1) * 128], OTa[par][:, :], True, False,
                            r=[("wslot", 3), ("OTa", par)], w=keys("ps", b))
                    self.mm(pb[:, :], WOv[:, 1, oc * 128:(oc + 1) * 128], OTb[par][:, :], False, True,
                            r=[("wslot", 3), ("OTb", par)], w=keys("ps", b))
                    self.tt("dve", xT[:, oc, tsl], pb[:, :], xT[:, oc, tsl], ALU.add,
                            r=keys("ps", b) + keys("xT", oc, tc), w=keys("xT", oc, tc))

    def mixer(self, li):
        self.s.barrier()
        getattr(self, ["mixer_dil", "mixer_mla", "mixer_na", "mixer_diff"][li])(li)
        self.s.barrier()

    def rmsnorm(self, gcol):
        s, sb = self.s, self.sb
        xT, hT, sq, ones, G = sb["xT"], sb["hT"], sb["sq"], sb["ones"], sb["gains"]
        for tc in range(NTC):
            tsl = slice(tc * TC, (tc + 1) * TC)
            for c in range(8):
                s.op("act", (lambda e, c=c, tsl=tsl: e.activation(out=sq[:, c, :], in_=xT[:, c, tsl], func=AF.Square)),
                     reads=keys("xT", c, tc), writes=keys("sq", c))
            b = self.bank()
            pb = self.ps[b]
            for c in range(8):
                s.op("pe", (lambda e, c=c, pb=pb: e.matmul(pb[:, :], ones[:, :], sq[:, c, :], start=(c == 0), stop=(c == 7))),
                     reads=keys("sq", c), writes=keys("ps", b))
            fi = self.fscr()
            fs = sb["fscr"][fi]
            s.op("act", (lambda e, pb=pb, fs=fs: e.activation(out=fs[:, :], in_=pb[:, :], func=AF.Ln, scale=1.0 / D, bias=sb["eps"][:, 0:1])),
                 reads=keys("ps", b), writes=keys("fscr", fi))
            s.op("act", (lambda e, fs=fs: e.activation(out=fs[:, :], in_=fs[:, :], func=AF.Exp, scale=-0.5)),
                 reads=keys("fscr", fi), writes=keys("fscr", fi))
            for c in range(8):
                s.op("dve", (lambda e, c=c, tsl=tsl, fs=fs: e.scalar_tensor_tensor(
                    out=hT[:, c, tsl], in0=xT[:, c, tsl], scalar=G[:, gcol * 8 + c:gcol * 8 + c + 1], in1=fs[:, :],
                    op0=ALU.mult, op1=ALU.mult)),
                     reads=keys("xT", c, tc) + keys("fscr", fi), writes=keys("hT", c, tc))

    def mlp_prefetch(self, li, first_slot):
        if "mlp" not in self.parts:
            return
        self.slot_i = first_slot
        w_up = self.dram["w_up"][li]
        w_down = self.dram["w_down"][li]
        u = self.load_slot3(w_up[:, 0:512].rearrange("(k p) n -> p k n", p=128), 8, 512)
        d = self.load_slot3(w_down[0:512, :].rearrange("(k p) n -> p k n", p=128), 4, 1024)
        self.mlp_pre = (u, d)

    def mlp(self, li):
        s, sb = self.s, self.sb
        xT, hT, hid = sb["xT"], sb["hT"], sb["hid"]
        w_up = self.dram["w_up"][li]
        w_down = self.dram["w_down"][li]
        self.rmsnorm(G_MLP + li)
        NG_ = 8
        fills = {}

        def fetch(g):
            u = self.load_slot3(w_up[:, g * 512:(g + 1) * 512].rearrange("(k p) n -> p k n", p=128), 8, 512)
            d = self.load_slot3(w_down[g * 512:(g + 1) * 512, :].rearrange("(k p) n -> p k n", p=128), 4, 1024)
            fills[g] = (u, d)

        if getattr(self, "mlp_pre", None) is not None:
            fills[0] = self.mlp_pre
            self.mlp_pre = None
        else:
            fetch(0)
        self.ple_pre = None
        for g in range(NG_):
            if g + 1 < NG_:
                fetch(g + 1)
            elif "ple" in self.parts:
                self.ple_pre = self.ple_prefetch(li)
            (ui, uv), (di, dv) = fills[g]
            hb = g % 2
            for tc in range(NTC):
                for mi in range(4):
                    tsl = slice(tc * TC, (tc + 1) * TC)
                    b = self.bank()
                    pb = self.ps[b]
                    for k in range(8):
                        s.op("pe", (lambda e, k=k, pb=pb, uv=uv, mi=mi, tsl=tsl: e.matmul(
                            pb[:, :], uv[:, k, mi * 128:(mi + 1) * 128], hT[:, k, tsl], start=(k == 0), stop=(k == 7))),
                             reads=keys("hT", k, tc) + [("wslot", ui)], writes=keys("ps", b))
                    fi = self.fscr()
                    fs = sb["fscr"][fi]
                    s.op("act", (lambda e, pb=pb, fs=fs: e.activation(out=fs[:, :], in_=pb[:, :], func=AF.Relu)),
                         reads=keys("ps", b), writes=keys("fscr", fi))
                    s.op("pool", (lambda e, fs=fs, hb=hb, mi=mi, tsl=tsl: e.tensor_tensor(
                        out=hid[hb][:, mi, tsl], in0=fs[:, :], in1=fs[:, :], op=ALU.mult)),
                         reads=keys("fscr", fi), writes=keys("hid", hb, mi, tc))
            if g == NG_ - 1 and self.ple_pre is not None:
                wg_ = self.dram["w_ple_gate"][li]
                g1 = self.load_slot(lambda k: wg_[k * 128:(k + 1) * 128, 512:1024], 8, 512)
                self.ple_pre = self.ple_pre + (g1,)
            for tc in range(NTC):
                for oc in range(8):
                    tsl = slice(tc * TC, (tc + 1) * TC)
                    b = self.bank()
                    pb = self.ps[b]
                    for k in range(4):
                        s.op("pe", (lambda e, k=k, pb=pb, dv=dv, oc=oc, hb=hb, tsl=tsl: e.matmul(
                            pb[:, :], dv[:, k, oc * 128:(oc + 1) * 128], hid[hb][:, k, tsl], start=(k == 0), stop=(k == 3))),
                             reads=keys("hid", hb, k, tc) + [("wslot", di)], writes=keys("ps", b))
                    s.op("dve", (lambda e, pb=pb, oc=oc, tsl=tsl: e.tensor_tensor(
                        out=xT[:, oc, tsl], in0=pb[:, :], in1=xT[:, oc, tsl], op=ALU.add)),
                         reads=keys("ps", b) + keys("xT", oc, tc), writes=keys("xT", oc, tc))

    def ple_prefetch(self, li):
        s, sb = self.s, self.sb
        pT = sb["pT"]
        wg = self.dram["w_ple_gate"][li]
        wp = self.dram["w_ple_proj"][li]
        for k in range(2):
            s.dma("pool", "pT", (lambda e, k=k: e.dma_start(out=pT[:, k, :], in_=self.dram["pT"][li, k * 128:(k + 1) * 128, :])),
                  writes=[("pT",)])
        s.retag([("pT",)], "pT")
        p_ = self.load_slot(lambda k: wp[k * 128:(k + 1) * 128, :], 2, 1024)
        g0 = self.load_slot(lambda k: wg[k * 128:(k + 1) * 128, 0:512], 8, 512)
        return p_, g0

    def ple(self, li):
        s, sb = self.s, self.sb
        xT, hT, pT = sb["xT"], sb["hT"], sb["pT"]
        wg = self.dram["w_ple_gate"][li]
        pre = getattr(self, "ple_pre", None)
        if pre is None:
            pre = self.ple_prefetch(li)
        self.ple_pre = None
        (pi, pv), g0 = pre[0], pre[1]
        self.rmsnorm(G_PLE + li)
        g1 = pre[2] if len(pre) > 2 else self.load_slot(lambda k: wg[k * 128:(k + 1) * 128, 512:1024], 8, 512)
        gates = (g0, g1)
        for tc in range(NTC):
            tsl = slice(tc * TC, (tc + 1) * TC)
            for oc in range(8):
                gi, gv = gates[oc // 4]
                m = oc % 4
                b = self.bank()
                pb = self.ps[b]
                for k in range(8):
                    self.mm(pb[:, :], gv[:, k, m * 128:(m + 1) * 128], hT[:, k, tsl], k == 0, k == 7,
                            r=keys("hT", k, tc) + [("wslot", gi)], w=keys("ps", b))
                b2 = self.bank()
                pb2 = self.ps[b2]
                for k in range(2):
                    self.mm(pb2[:, :], pv[:, k, oc * 128:(oc + 1) * 128], pT[:, k, tsl], k == 0, k == 1,
                            r=[("pT",), ("wslot", pi)], w=keys("ps", b2))
                fi = self.fscr()
                fs = sb["fscr"][fi]
                self.act(fs[:, :], pb[:, :], AF.Sigmoid, r=keys("ps", b), w=keys("fscr", fi))
                self.tt("dve", fs[:, :], pb2[:, :], fs[:, :], ALU.mult, r=keys("ps", b2) + keys("fscr", fi), w=keys("fscr", fi))
                self.tt("pool", xT[:, oc, tsl], fs[:, :], xT[:, oc, tsl], ALU.add,
                        r=keys("fscr", fi) + keys("xT", oc, tc), w=keys("xT", oc, tc))

    def final_norm(self):
        s, sb = self.s, self.sb
        xT, sq, ones, G = sb["xT"], sb["sq"], sb["ones"], sb["gains"]
        for tc in range(NTC):
            tsl = slice(tc * TC, (tc + 1) * TC)
            for c in range(8):
                s.op("act", (lambda e, c=c, tsl=tsl: e.activation(out=sq[:, c, :], in_=xT[:, c, tsl], func=AF.Square)),
                     reads=keys("xT", c, tc), writes=keys("sq", c))
            b = self.bank()
            pb = self.ps[b]
            for c in range(8):
                s.op("pe", (lambda e, c=c, pb=pb: e.matmul(pb[:, :], ones[:, :], sq[:, c, :], start=(c == 0), stop=(c == 7))),
                     reads=keys("sq", c), writes=keys("ps", b))
            fi = self.fscr()
            fs = sb["fscr"][fi]
            s.op("act", (lambda e, pb=pb, fs=fs: e.activation(out=fs[:, :], in_=pb[:, :], func=AF.Ln, scale=1.0 / D, bias=sb["eps"][:, 0:1])),
                 reads=keys("ps", b), writes=keys("fscr", fi))
            s.op("act", (lambda e, fs=fs: e.activation(out=fs[:, :], in_=fs[:, :], func=AF.Exp, scale=-0.5)),
                 reads=keys("fscr", fi), writes=keys("fscr", fi))
            for c in range(8):
                s.op("dve", (lambda e, c=c, tsl=tsl, fs=fs: e.scalar_tensor_tensor(
                    out=xT[:, c, tsl], in0=xT[:, c, tsl], scalar=G[:, G_FINAL * 8 + c:G_FINAL * 8 + c + 1], in1=fs[:, :],
                    op0=ALU.mult, op1=ALU.mult)),
                     reads=keys("xT", c, tc) + keys("fscr", fi), writes=keys("xT", c, tc))

    def build(self):
        nc, s = self.nc, self.s
        self.declare_io()
        self.NSLOT = 4
        self.NFS = 6
        import contextlib
        with contextlib.ExitStack() as st:
            def sbt(name, shape, dt):
                return st.enter_context(nc.sbuf_tensor(name, shape, dt))
            sb = self.sb
            sb["xT"] = sbt("xT_sb", [128, 8, S], F32)
            sb["hT"] = sbt("hT_sb", [128, 8, S], BF16)
            sb["wslot"] = [sbt(f"wslot{i}", [128, 4096], BF16) for i in range(self.NSLOT)]
            sb["fscr"] = [sbt(f"fscr{i}", [128, TC], F32) for i in range(self.NFS)]
            sb["gains"] = sbt("gains_sb", [128, NGC], F32)
            sb["ones"] = sbt("ones_sb", [128, 128], BF16)
            sb["ident"] = sbt("ident_sb", [128, 128], BF16)
            sb["eps"] = sbt("eps_sb", [128, 1], F32)
            sb["lamv"] = sbt("lamv_sb", [128, 256], F32)
            sb["dmask"] = sbt("dmask_sb", [128, 3, 128], BF16)
            sb["qkscl"] = sbt("qkscl_sb", [128, 1], F32)
            sb["lams"] = sbt("lams_sb", [128, 8], F32)
            sb["R"] = sbt("R_sb", [128, 32768], BF16)
            rv = self.rview
            sb["hid"] = [rv(i * 8192, 8192).rearrange("p (m t) -> p m t", m=4) for i in range(2)]
            sb["sq"] = rv(16384, 4096).rearrange("p (c t) -> p c t", c=8)
            sb["pT"] = rv(20480, 4096).rearrange("p (k t) -> p k t", k=2)
            sb["QT"] = rv(0, 2048)
            sb["KT"] = rv(2048, 2048)
            sb["VA"] = rv(4096, 2048).rearrange("p (t d) -> p t d", t=16)
            sb["OTp"] = [rv(6144 + i * 2048, 2048) for i in range(2)]
            sb["PT"] = [rv(10240 + i * 512, 512) for i in range(4)]
            sb["PT2"] = [rv(8192 + i * 1024, 1024) for i in range(4)]
            sb["cqn"] = rv(12288, 4096).rearrange("p (k t) -> p k t", k=2)
            sb["ckvn"] = rv(20480, 2048)
            sb["KR"] = rv(22528, 2048)
            sb["ropeC"] = rv(24576, 4096, F32)
            sb["ropeS"] = rv(28672, 4096, F32)
            self.ACC, self.SB, self.MISC = (0, 1), (2, 3, 4, 5), (6, 7)
            self.pt_i = 0
            self.ps2 = [st.enter_context(nc.psum_tensor(f"ps{i}", [128, 2 * TC], F32)) for i in range(4)]
            self.ps = [self.ps2[i // 2][:, (i % 2) * TC:(i % 2 + 1) * TC] for i in range(8)]

            xT = sb["xT"]
            xsrc = self.dram["xT"].rearrange("(c p) t -> p c t", p=128)
            for c in range(8):
                s.dma("sp", "xin", (lambda e, c=c: e.dma_start(out=xT[:, c, :], in_=xsrc[:, c, :])),
                      writes=keys("xT", c, range(NTC)))
            s.retag(keys("xT", range(8), range(NTC)), "xin")
            s.dma("sp", "cst", (lambda e: e.dma_start(out=sb["gains"][:, :], in_=self.dram["gains"])), writes=[("gains",)])
            s.dma("pool", "cst2", (lambda e: e.dma_start(out=sb["ident"][:, :], in_=self.dram["ident"])), writes=[("ident",)])
            s.op("dve", (lambda e: e.memset(sb["ones"][:, :], 1.0)), writes=[("ones",)])
            s.op("dve", (lambda e: e.memset(sb["eps"][:, :], EPS)), writes=[("eps",)])
            s.barrier()
            for e in ("pe", "act", "dve", "pool"):
                for cs_ in ("cst", "cst2"):
                    s.ops[e].append(([(cs_, s.dma_count[cs_])], None, None, "init"))
                    s.known[e][cs_] = s.dma_count[cs_]

            for li in self.layers:
                if "mix" in self.parts:
                    s.phase = f"mix{li}"
                    self.mixer(li)
                if "mlp" in self.parts:
                    s.phase = f"mlp{li}"
                    self.mlp(li)
                if "ple" in self.parts:
                    s.phase = f"ple{li}"
                    self.ple(li)
            s.phase = "final"
            if self.do_final:
                self.final_norm()
            ydst = self.out.rearrange("(c p) t -> p c t", p=128)
            for c in range(8):
                s.dma("sp", "yout", (lambda e, c=c: e.dma_start(out=ydst[:, c, :], in_=xT[:, c, :])),
                      reads=keys("xT", c, range(NTC)))
            s.final_wait("sp", ["yout"] + (["dbg"] if self.dbg_names else []))

            self.emit(st)
        return nc

    def emit(self, st):
        nc, s = self.nc, self.s
        semnames = list(Sched.ENGS) + sorted(s.dma_count.keys())
        sems = {n: st.enter_context(nc.semaphore(f"sem_{n}")) for n in semnames}
        block = st.enter_context(nc.Block())

        def run(eng_name):
            def body(e):
                for waits, fn, inc, phase in s.ops[eng_name]:
                    for sk, val in waits:
                        e.wait_ge(sems[sk], val)
                    if fn is None:
                        continue
                    if self.scopes:
                        with nc.named_scope(phase):
                            ins = fn(e)
                    else:
                        ins = fn(e)
                    ins.then_inc(sems[inc[0]], inc[1])
            return body

        block.tensor(run("pe"))
        block.scalar(run("act"))
        block.vector(run("dve"))
        block.gpsimd(run("pool"))
        block.sync(run("sp"))


def rope_np(rot_dim):
    inv = np.power(np.float32(500000.0), -(np.arange(0, rot_dim, 2, dtype=np.float32) / np.float32(rot_dim))).astype(np.float32)
    ang = (np.arange(S, dtype=np.float32)[:, None] * inv[None, :]).astype(np.float32)
    return np.cos(ang).astype(np.float32), np.sin(ang).astype(np.float32)


def make_consts():
    d = {"ident": np.eye(128, dtype=np.float32)}
    c, sn = rope_np(32)
    C = np.ones((128, S), np.float32)
    Sg = np.zeros((128, S), np.float32)
    C[64:80] = c.T; C[80:96] = c.T
    Sg[64:80] = -sn.T; Sg[80:96] = sn.T
    d["ropeL"] = np.stack([C, Sg])
    c, sn = rope_np(16)
    C = np.ones((128, S), np.float32)
    Sg = np.zeros((128, S), np.float32)
    for o in (0, 64):
        C[o:o + 8] = c.T; C[o + 8:o + 16] = c.T
        Sg[o:o + 8] = -sn.T; Sg[o + 8:o + 16] = sn.T
    d["ropeP"] = np.stack([C, Sg])
    return d


def make_na_bias(rpb):
    out = np.full((16, 128, 21, 128), NEG, np.float32)
    pairs = [(5, 5 + d, d + 2) for d in range(-2, 3)]
    for t in (0, 1, 14, 15):
        pairs += [(t, kt, blk) for (kt, blk) in Builder.na_tiles(t)]
    qq = np.arange(128)
    kk = np.arange(128)
    for (t, kt, blk) in pairs:
        r = 2 * t + qq // 64
        c = qq % 64
        kr = 2 * kt + kk // 64
        kc = kk % 64
        rs = np.clip(r - 4, 0, 24)
        w0 = np.clip(c - 8, 0, 48)
        valid = ((kr[:, None] >= rs[None, :]) & (kr[:, None] < rs[None, :] + 8)
                 & (kc[:, None] >= w0[None, :]) & (kc[:, None] < w0[None, :] + 16))
        ro = np.clip(kr[:, None] - r[None, :] + 7, 0, 14)
        co = np.clip(kc[:, None] - c[None, :] + 15, 0, 30)
        g = rpb[:, ro, co]
        out[:, :, blk, :] = np.where(valid[None], g, np.float32(NEG))
    return out


def chunked(v):
    return np.ascontiguousarray(v.reshape(8, 128).T)


def make_gains(inp):
    g = np.zeros((128, NGC), np.float32)
    mixn = [inp["a_norm"][0], inp["b_norm"][0], inp["c_norm"][0], inp["d_norm"][0]]
    for i in range(4):
        g[:, (G_MIX + i) * 8:(G_MIX + i + 1) * 8] = chunked(mixn[i])
        g[:, (G_MLP + i) * 8:(G_MLP + i + 1) * 8] = chunked(inp["mlp_norm"][i])
        g[:, (G_PLE + i) * 8:(G_PLE + i + 1) * 8] = chunked(inp["ple_norm"][i])
    g[:, G_FINAL * 8:(G_FINAL + 1) * 8] = chunked(inp["final_norm"])
    g[:, NG * 8:NG * 8 + 2] = inp["b_q_norm"][0].reshape(2, 128).T
    g[:, NG * 8 + 2] = inp["b_kv_norm"][0]
    g[:, NG * 8 + 3] = inp["d_subln"][0]
    return g


def shared_inputs(inp, layers=(0, 1, 2, 3), parts=("mix", "mlp", "ple")):
    d = make_consts()
    d["gains"] = make_gains(inp)
    for k in ("w_up", "w_down", "w_ple_gate", "w_ple_proj"):
        d[k] = np.ascontiguousarray(inp[k], dtype=np.float32)
    if 1 in layers and "mix" in parts:
        w_in = inp["b_w_in"][0]
        perm32 = np.concatenate([np.arange(16, 32), np.arange(0, 16)])
        kr = w_in[:, 384:416]
        d["b_w_in"] = np.ascontiguousarray(w_in)
        d["b_w_kr"] = np.ascontiguousarray(np.concatenate([w_in[:, 0:64], kr, w_in[:, 0:64], kr[:, perm32]], axis=1))
        wq = inp["b_w_uq"][0]
        idx = np.arange(1536).reshape(16, 96).copy()
        idx[:, 64:96] = idx[:, 64:96][:, perm32]
        d["b_w_uq"] = np.ascontiguousarray(wq)
        d["b_w_uq_p"] = np.ascontiguousarray(wq[:, idx.reshape(-1)])
        d["b_w_ukv"] = np.ascontiguousarray(inp["b_w_ukv"][0])
        d["b_w_o"] = np.ascontiguousarray(inp["b_w_o"][0])
    if 0 in layers and "mix" in parts:
        w = inp["a_w_qkv"][0]
        perm16 = np.concatenate([np.arange(8, 16), np.arange(0, 8), np.arange(16, 64)])
        wh = np.zeros((15, D, 320), np.float32)
        for h in range(15):
            q = w[:, h * 64:(h + 1) * 64]
            k = w[:, 960 + h * 64:960 + (h + 1) * 64]
            v = w[:, 1920 + h * 64:1920 + (h + 1) * 64]
            wh[h] = np.concatenate([q, k, q[:, perm16], k[:, perm16], v], axis=1)
        d["a_w_h"] = wh
        d["a_w_o"] = np.ascontiguousarray(inp["a_w_o"][0])
        kk = np.arange(128)[:, None]
        qq = np.arange(128)[None, :]
        mA = np.where(kk >= qq, 0.0, NEG)
        mB = np.where(kk <= qq, 0.0, NEG)
        mAe = np.full((128, 128), NEG)
        mAe[0:64] = mA[64:128]
        d["dil_masks"] = np.ascontiguousarray(np.stack([mA, mB, mAe], axis=1).astype(np.float32))
    if 2 in layers and "mix" in parts:
        w = inp["c_w_qkv"][0]
        wa = np.zeros((8, D, 384), np.float32)
        for c in range(8):
            wa[c] = np.concatenate([w[:, c * 128:(c + 1) * 128], w[:, 1024 + c * 128:1024 + (c + 1) * 128],
                                    w[:, 2048 + c * 128:2048 + (c + 1) * 128]], axis=1)
        d["c_w_a"] = wa
        d["c_w_o"] = np.ascontiguousarray(inp["c_w_o"][0])
        d["na_bias"] = make_na_bias(inp["c_rpb"][0])
    if 3 in layers and "mix" in parts:
        w = inp["d_w_qkv"][0]
        perm16 = np.concatenate([np.arange(8, 16), np.arange(0, 8), np.arange(16, 64)])
        wa = np.zeros((8, D, 384), np.float32)
        wb = np.zeros((8, D, 256), np.float32)
        for h in range(8):
            q = w[:, h * 128:(h + 1) * 128]
            k = w[:, 1024 + h * 128:1024 + (h + 1) * 128]
            v = w[:, 2048 + h * 128:2048 + (h + 1) * 128]
            p2 = np.concatenate([perm16, 64 + perm16])
            wa[h] = np.concatenate([q, k, v], axis=1)
            wb[h] = np.concatenate([q[:, p2], k[:, p2]], axis=1)
        d["d_w_a"], d["d_w_b"] = wa, wb
        d["d_w_o"] = np.ascontiguousarray(inp["d_w_o"][0])
        lv = np.concatenate([inp["d_lambda_q1"][0], inp["d_lambda_k1"][0], inp["d_lambda_q2"][0], inp["d_lambda_k2"][0]])
        d["d_lamv"] = np.ascontiguousarray(np.tile(lv[None, :], (128, 1)).astype(np.float32))
    return d


def core_inputs(inp, b, x_override=None):
    x = inp["x"][b] if x_override is None else x_override
    return {
        "xT": np.ascontiguousarray(x.T, dtype=np.float32),
        "pT": np.ascontiguousarray(np.transpose(inp["p"][:, b], (0, 2, 1)), dtype=np.float32),
    }


def kernel(**inp):
    bld = Builder(layers=[0, 1, 2, 3], do_final=True)
    nc = bld.build()
    shared = shared_inputs(inp)
    in_maps = [dict(shared, **core_inputs(inp, b)) for b in range(NCORES)]
    res = run_bass_kernel_spmd(nc, in_maps, core_ids=list(range(NCORES)))
    out = np.stack([np.ascontiguousarray(r["yT"].T) for r in res.results], axis=0)
    return out.astype(np.float32)
```
